# Optimizing a Trainium2 kernel written in Bass

```python
import math
import jax, jax.numpy as jnp
from jax import lax
import numpy as np

D_MODEL = 1024
BATCH = 16
SEQ = 4096
DEPTH = 2

GRID_W = 64
RMS_EPS = 1e-6
NEG_INF = -1e30
N_BRANCH = 4
BRANCH_W = 256

DIL_CFG = ((128, 1), (512, 4), (2048, 16))
N_DIL = 3
A_HEADS = 4
A_HEAD_DIM = 64
B_HEADS = 4
B_Q_RANK = 256
B_KV_RANK = 128
B_NOPE_DIM = 64
B_ROPE_DIM = 32
B_V_DIM = 64
ROPE_THETA = 10000.0
Q_BLOCK = 128
C_HEADS = 4
C_HEAD_DIM = 64
NA_ROWS = 8
NA_COLS = 16
NA_KEY_COLS = 2 * NA_COLS
D_Q_HEADS = 4
D_KV_HEADS = 2
D_HEAD_DIM = 64
D_RADIUS = 128
T5_BUCKETS = 32
T5_MAX_DIST = 1024
T5_HEADS = N_DIL * A_HEADS + D_Q_HEADS
D_FF = 2816
CONV_W = 3

A_COLS = 3 * N_DIL * A_HEADS * A_HEAD_DIM
C_COLS = 3 * C_HEADS * C_HEAD_DIM
IN_SPLITS = (A_COLS, B_Q_RANK, B_KV_RANK, B_ROPE_DIM, C_COLS,
             D_Q_HEADS * D_HEAD_DIM, D_KV_HEADS * D_HEAD_DIM, D_KV_HEADS * D_HEAD_DIM)
IN_COLS = sum(IN_SPLITS)

kernel_name = "hybrid_gated_multi_mixer_encoder"


def rms_norm(x, g):
    xf = x.astype(jnp.float32)
    y = xf * lax.rsqrt(jnp.mean(xf * xf, axis=-1, keepdims=True) + RMS_EPS)
    return (y * g.astype(jnp.float32)).astype(x.dtype)


def split_columns(t):
    parts, start = [], 0
    for width in IN_SPLITS:
        parts.append(t[..., start:start + width])
        start += width
    return parts


def t5_bucket(rel):
    half = T5_BUCKETS // 2
    exact = half // 2
    n = np.abs(rel)
    large = exact + (np.log(np.maximum(n, 1) / exact) / math.log(T5_MAX_DIST / exact)
                     * (half - exact)).astype(np.int32)
    large = np.minimum(large, half - 1)
    return (np.where(rel > 0, half, 0) + np.where(n < exact, n, large)).astype(np.int32)


def relative_bias(t5_table, rel, h0, h1):
    return t5_table[t5_bucket(rel)][:, h0:h1].T


def band_attention(q, k, v, bias, radius, sink=None):
    N, L, Hq, dh = q.shape
    Hkv = k.shape[2]
    rep = Hq // Hkv
    blk = radius
    nblk = -(-L // blk)
    Lp = nblk * blk
    qb = jnp.pad(q, ((0, 0), (0, Lp - L), (0, 0), (0, 0))).reshape(N, nblk, blk, Hkv, rep, dh)

    def bands(t):
        tp = jnp.pad(t, ((0, 0), (blk, Lp - L + blk), (0, 0), (0, 0))).reshape(N, nblk + 2, blk, Hkv, dh)
        return jnp.concatenate([tp[:, :-2], tp[:, 1:-1], tp[:, 2:]], axis=2)

    kb, vb = bands(k), bands(v)
    s = jnp.einsum("nbqgrd,nbkgd->nbgrqk", qb, kb).astype(jnp.float32) * (dh ** -0.5)
    rel = np.arange(3 * blk)[None, :] - blk - np.arange(blk)[:, None]
    kpos = np.arange(nblk)[:, None] * blk - blk + np.arange(3 * blk)[None, :]
    mask = (np.abs(rel) <= radius)[None] & ((kpos >= 0) & (kpos < L))[:, None, :]
    b = bias[:, np.clip(rel + radius, 0, 2 * radius)].astype(jnp.float32).reshape(Hkv, rep, blk, 3 * blk)
    s = jnp.where(mask[None, :, None, None], s + b[None, None], NEG_INF)
    m = jnp.max(s, axis=-1)
    if sink is not None:
        sk = sink.astype(jnp.float32).reshape(Hkv, rep)[None, None, :, :, None]
        m = jnp.maximum(m, sk)
    p = jnp.exp(s - m[..., None])
    l = jnp.sum(p, axis=-1)
    if sink is not None:
        l = l + jnp.exp(sk - m)
    o = jnp.einsum("nbgrqk,nbkgd->nbqgrd", p.astype(v.dtype), vb).astype(jnp.float32)
    m_t = m.transpose(0, 1, 4, 2, 3)
    l_t = l.transpose(0, 1, 4, 2, 3)
    o = (o / l_t[..., None]).reshape(N, Lp, Hq, dh)[:, :L].astype(q.dtype)
    return o, m_t.reshape(N, Lp, Hq)[:, :L], l_t.reshape(N, Lp, Hq)[:, :L]


def to_strided(t, dil):
    B, S = t.shape[:2]
    return t.reshape(B, S // dil, dil, *t.shape[2:]).swapaxes(1, 2).reshape(B * dil, S // dil, *t.shape[2:])


def from_strided(t, dil):
    Bd, Sd = t.shape[:2]
    return t.reshape(Bd // dil, dil, Sd, *t.shape[2:]).swapaxes(1, 2).reshape(Bd // dil, Sd * dil, *t.shape[2:])


def dilated_attention(q, k, v, t5_table):
    outs, ms, ls = [], [], []
    for g, (window, dil) in enumerate(DIL_CFG):
        radius = window // dil // 2
        bias = relative_bias(t5_table, dil * np.arange(-radius, radius + 1), g * A_HEADS, (g + 1) * A_HEADS)
        o, m, l = band_attention(to_strided(q[:, :, g], dil), to_strided(k[:, :, g], dil),
                                 to_strided(v[:, :, g], dil), bias, radius)
        outs.append(from_strided(o, dil))
        ms.append(from_strided(m, dil))
        ls.append(from_strided(l, dil))
    m_all = jnp.stack(ms)
    wgt = jnp.stack(ls) * jnp.exp(m_all - jnp.max(m_all, axis=0, keepdims=True))
    o_all = jnp.stack(outs).astype(jnp.float32)
    out = jnp.sum(wgt[..., None] * o_all, axis=0) / jnp.sum(wgt, axis=0)[..., None]
    return out.astype(q.dtype)


def rope(x):
    S, half = x.shape[1], x.shape[-1] // 2
    inv = ROPE_THETA ** (-jnp.arange(half, dtype=jnp.float32) / half)
    ang = jnp.arange(S, dtype=jnp.float32)[:, None] * inv[None, :]
    shape = (1, S) + (1,) * (x.ndim - 3) + (half,)
    cos, sin = jnp.cos(ang).reshape(shape), jnp.sin(ang).reshape(shape)
    xf = x.astype(jnp.float32)
    x1, x2 = xf[..., :half], xf[..., half:]
    return jnp.concatenate([x1 * cos - x2 * sin, x1 * sin + x2 * cos], axis=-1).astype(x.dtype)


def dense_attention(q, k, v):
    B, S, H, dq = q.shape
    nb = S // Q_BLOCK
    qb = q.reshape(B, nb, Q_BLOCK, H, dq).transpose(1, 0, 2, 3, 4)

    def one_block(qi):
        s = jnp.einsum("bqhd,bkhd->bhqk", qi, k).astype(jnp.float32) * (dq ** -0.5)
        p = jax.nn.softmax(s, axis=-1)
        return jnp.einsum("bhqk,bkhd->bqhd", p.astype(v.dtype), v)

    o = lax.map(one_block, qb)
    return o.transpose(1, 0, 2, 3, 4).reshape(B, S, H, v.shape[-1])


def mla_attention(c_q, c_kv, k_rope, q_norm_g, w_uq, kv_norm_g, w_ukv):
    B, S = c_q.shape[:2]
    q = (rms_norm(c_q, q_norm_g) @ w_uq).reshape(B, S, B_HEADS, B_NOPE_DIM + B_ROPE_DIM)
    kv = (rms_norm(c_kv, kv_norm_g) @ w_ukv).reshape(B, S, B_HEADS, B_NOPE_DIM + B_V_DIM)
    q = jnp.concatenate([q[..., :B_NOPE_DIM], rope(q[..., B_NOPE_DIM:])], axis=-1)
    kr = jnp.broadcast_to(rope(k_rope)[:, :, None, :], (B, S, B_HEADS, B_ROPE_DIM))
    k = jnp.concatenate([kv[..., :B_NOPE_DIM], kr], axis=-1)
    return dense_attention(q, k, kv[..., B_NOPE_DIM:])


def neighbourhood_attention(q, k, v, rpb):
    B, S, H, dh = q.shape
    rows = S // GRID_W
    kh = min(NA_ROWS, rows)
    ncb = GRID_W // NA_COLS
    r = np.arange(rows)
    row_idx = np.clip(r - kh // 2, 0, rows - kh)[:, None] + np.arange(kh)[None, :]
    cb_start = np.clip(np.arange(ncb) * NA_COLS - NA_COLS // 2, 0, GRID_W - NA_KEY_COLS)
    col_idx = cb_start[:, None] + np.arange(NA_KEY_COLS)[None, :]
    qcol = np.arange(ncb)[:, None] * NA_COLS + np.arange(NA_COLS)[None, :]
    qcol_start = np.clip(qcol - NA_COLS // 2, 0, GRID_W - NA_COLS)
    kc = col_idx[:, None, :]
    col_ok = (kc >= qcol_start[..., None]) & (kc < qcol_start[..., None] + NA_COLS)
    dc_idx = np.clip(kc - qcol[..., None] + NA_COLS - 1, 0, 2 * NA_COLS - 2)
    dr_idx = row_idx - r[:, None] + NA_ROWS - 1

    gather = (slice(None), row_idx[:, :, None, None], col_idx[None, None])
    kg = k.reshape(B, rows, GRID_W, H, dh)[gather]
    vg = v.reshape(B, rows, GRID_W, H, dh)[gather]
    qg = q.reshape(B, rows, ncb, NA_COLS, H, dh)
    s = jnp.einsum("brcqhd,brkcwhd->brchqkw", qg, kg).astype(jnp.float32) * (dh ** -0.5)
    bias = rpb[:, dr_idx][..., dc_idx].astype(jnp.float32)
    s = s + bias.transpose(1, 3, 0, 4, 2, 5)[None]
    s = jnp.where(col_ok[None, None, :, None, :, None, :], s, NEG_INF)
    p = jax.nn.softmax(s.reshape(s.shape[:-2] + (kh * NA_KEY_COLS,)), axis=-1).reshape(s.shape)
    o = jnp.einsum("brchqkw,brkcwhd->brcqhd", p.astype(v.dtype), vg)
    return o.reshape(B, S, H, dh)


def dwconv_centered(u, w, b):
    F = u.shape[-1]
    y = lax.conv_general_dilated(u, w[:, None, :].astype(u.dtype), window_strides=(1,),
                                 padding=((CONV_W // 2, CONV_W // 2),),
                                 dimension_numbers=("NWC", "WIO", "NWC"), feature_group_count=F)
    return y + b.astype(u.dtype)


def setup_inputs(seed: int = 0) -> dict:
    key = jax.random.key(seed)
    ks = jax.random.split(key, 20)
    L, D, f32 = DEPTH, D_MODEL, jnp.float32

    def nrm(k, shape, scale):
        return jax.random.normal(k, shape, f32) * scale

    def gain(k, shape):
        return 1.0 + 0.02 * jax.random.normal(k, shape, f32)

    return {
        "x": nrm(ks[0], (BATCH, SEQ, D), 1.0),
        "t5_table": nrm(ks[1], (T5_BUCKETS, T5_HEADS), 0.3),
        "norm_mix_g": gain(ks[2], (L, D)),
        "w_in": nrm(ks[3], (L, D, IN_COLS), D ** -0.5),
        "q_norm_g": gain(ks[4], (L, B_Q_RANK)),
        "w_uq": nrm(ks[5], (L, B_Q_RANK, B_HEADS * (B_NOPE_DIM + B_ROPE_DIM)), B_Q_RANK ** -0.5),
        "kv_norm_g": gain(ks[6], (L, B_KV_RANK)),
        "w_ukv": nrm(ks[7], (L, B_KV_RANK, B_HEADS * (B_NOPE_DIM + B_V_DIM)), B_KV_RANK ** -0.5),
        "na_bias": nrm(ks[8], (L, C_HEADS, 2 * NA_ROWS - 1, 2 * NA_COLS - 1), 0.3),
        "sink_logit": nrm(ks[9], (L, D_Q_HEADS), 0.5),
        "w_gate": nrm(ks[10], (L, N_BRANCH, D, D), D ** -0.5),
        "w_branch": nrm(ks[11], (L, N_BRANCH, BRANCH_W, D), BRANCH_W ** -0.5),
        "w_out": nrm(ks[12], (L, D, D), D ** -0.5),
        "norm_ffn_g": gain(ks[13], (L, D)),
        "w_ffn_gate": nrm(ks[14], (L, D, D_FF), D ** -0.5),
        "w_ffn_up": nrm(ks[15], (L, D, D_FF), D ** -0.5),
        "conv_w": nrm(ks[16], (L, CONV_W, D_FF), CONV_W ** -0.5),
        "conv_b": nrm(ks[17], (L, D_FF), 0.02),
        "w_ffn_down": nrm(ks[18], (L, D_FF, D), D_FF ** -0.5),
        "final_g": gain(ks[19], (D,)),
    }


def reference(x, t5_table, norm_mix_g, w_in, q_norm_g, w_uq, kv_norm_g, w_ukv, na_bias, sink_logit,
              w_gate, w_branch, w_out, norm_ffn_g, w_ffn_gate, w_ffn_up, conv_w, conv_b, w_ffn_down,
              final_g):
    B, S, _ = x.shape
    d_bias = relative_bias(t5_table, np.arange(-D_RADIUS, D_RADIUS + 1), N_DIL * A_HEADS, T5_HEADS)
    for layer in range(DEPTH):
        h = rms_norm(x, norm_mix_g[layer])
        a_qkv, b_cq, b_ckv, b_kr, c_qkv, d_q, d_k, d_v = split_columns(h @ w_in[layer])

        a = a_qkv.reshape(B, S, 3, N_DIL, A_HEADS, A_HEAD_DIM)
        y_a = dilated_attention(a[:, :, 0], a[:, :, 1], a[:, :, 2], t5_table)

        y_b = mla_attention(b_cq, b_ckv, b_kr, q_norm_g[layer], w_uq[layer], kv_norm_g[layer], w_ukv[layer])

        c = c_qkv.reshape(B, S, 3, C_HEADS, C_HEAD_DIM)
        y_c = neighbourhood_attention(c[:, :, 0], c[:, :, 1], c[:, :, 2], na_bias[layer])

        y_d, _, _ = band_attention(d_q.reshape(B, S, D_Q_HEADS, D_HEAD_DIM),
                                   d_k.reshape(B, S, D_KV_HEADS, D_HEAD_DIM),
                                   d_v.reshape(B, S, D_KV_HEADS, D_HEAD_DIM),
                                   d_bias, D_RADIUS, sink_logit[layer])

        branches = (y_a, y_b, y_c, y_d)
        merged = jax.nn.sigmoid(h @ w_gate[layer, 0]) * (branches[0].reshape(B, S, BRANCH_W) @ w_branch[layer, 0])
        for i in range(1, N_BRANCH):
            merged = merged + jax.nn.sigmoid(h @ w_gate[layer, i]) * (
                branches[i].reshape(B, S, BRANCH_W) @ w_branch[layer, i])
        x = x + merged @ w_out[layer]

        h = rms_norm(x, norm_ffn_g[layer])
        u = jax.nn.gelu(dwconv_centered(h @ w_ffn_gate[layer], conv_w[layer], conv_b[layer])) * (h @ w_ffn_up[layer])
        x = x + u @ w_ffn_down[layer]
    return rms_norm(x, final_g)
```

```python
import math
from contextlib import ExitStack

import numpy as np
import concourse.bass as bass
import concourse.mybir as mybir
from concourse.bass_utils import run_bass_kernel_spmd

F32 = mybir.dt.float32
BF16 = mybir.dt.bfloat16
U8 = mybir.dt.uint8
AF = mybir.ActivationFunctionType
ALU = mybir.AluOpType

S = 4096
D = 1024
NCH = 8
DEPTH = 2
INC = 4000
DFF = 2816
NFF = 22
EPS = 1e-6
SEM_EPOCH = 24000
SB_BYTES = 189 * 1024

RT_D = 511
RT_A = 383
RT = RT_D + 3 * RT_A
DIL = (1, 4, 16)


class _Op:
    __slots__ = ("idx", "eng", "fn", "lane", "deps", "inc", "count", "dma_key")

    def __init__(self, idx, eng, fn, lane, dma_key):
        self.idx = idx
        self.eng = eng
        self.fn = fn
        self.lane = lane
        self.deps = {}
        self.inc = False
        self.count = None
        self.dma_key = dma_key


class _Rec:
    def __init__(self):
        self.call = None

    def __getattr__(self, name):
        def f(*a, **k):
            self.call = (name, a, k)
            return None
        return f


class Prog:
    ENGS = ("pe", "act", "dve", "pool", "sp")

    def __init__(self, nc):
        self.nc = nc
        self.ops = []
        self.by_eng = {e: [] for e in self.ENGS}
        self.res = {}
        self.dma_count = {}
        self.pool_map = {}
        self.last = {}

    def _state(self, key):
        st = self.res.get(key)
        if st is None:
            st = [{}, {}]
            self.res[key] = st
        return st

    def op(self, eng, fn, reads=(), writes=(), sem=None, partial=False):
        dma = sem is not None
        if dma:
            key = self.pool_map.get(sem)
            if key is None:
                key = len(self.pool_map)
                self.pool_map[sem] = key
            lane = ("dma", key)
        else:
            key = None
            lane = eng
        if fn is not None:
            rec = _Rec()
            fn(rec)
            fn = rec.call
        o = _Op(len(self.ops), eng, fn, lane, key)
        if dma:
            self.dma_count[key] = self.dma_count.get(key, 0) + 1
            o.count = self.dma_count[key]

        def add_dep(d, raw=False):
            if d.lane == o.lane and not dma:
                if not raw or eng == "pe":
                    return
            cur = o.deps.get(d.lane)
            if cur is None or d.idx > cur.idx:
                o.deps[d.lane] = d

        for r in reads:
            w, rd = self._state(r)
            for d in w.values():
                add_dep(d, raw=True)
        for wkey in writes:
            w, rd = self._state(wkey)
            if not (partial and not rd):
                for d in w.values():
                    add_dep(d)
            for d in rd.values():
                add_dep(d)
        for r in reads:
            w, rd = self._state(r)
            rd[o.lane] = o
        for wkey in writes:
            st = self._state(wkey)
            if partial and not st[1]:
                st[0][o.lane] = o
            else:
                st[0] = {o.lane: o}
                st[1] = {}
        for d in o.deps.values():
            if d.dma_key is None:
                d.inc = True
        self.ops.append(o)
        self.by_eng[eng].append(o)
        self.last[lane] = o
        return o

    def barrier(self, engs=None):
        lasts = list(self.last.values())
        for e in (engs or self.ENGS):
            o = _Op(len(self.ops), e, None, e, None)
            for d in lasts:
                if d.lane == e:
                    continue
                o.deps[d.lane] = d
                if d.dma_key is None:
                    d.inc = True
            self.ops.append(o)
            self.by_eng[e].append(o)
        self.res = {}
        self.pool_map = {}
        self.last = {}

    def emit(self, stack):
        nc = self.nc
        sems = {}

        def get_sem(name):
            s = sems.get(name)
            if s is None:
                s = stack.enter_context(nc.semaphore("s%d" % len(sems)))
                sems[name] = s
            return s

        for e in self.ENGS:
            c = 0
            for o in self.by_eng[e]:
                if o.dma_key is None and o.inc:
                    c += 1
                    o.count = c

        def sem_for(o):
            if o.dma_key is None:
                ep, v = divmod(o.count - 1, SEM_EPOCH)
                return get_sem(("c", o.lane, ep)), v + 1
            per = SEM_EPOCH // 16
            ep, v = divmod(o.count - 1, per)
            return get_sem(("d", o.dma_key, ep)), (v + 1) * 16

        for o in self.ops:
            if o.dma_key is not None or o.inc:
                sem_for(o)
        block = stack.enter_context(nc.Block())
        deco = {"pe": block.tensor, "act": block.scalar, "dve": block.vector,
                "pool": block.gpsimd, "sp": block.sync}
        for e in self.ENGS:
            ops = self.by_eng[e]
            if not ops:
                continue

            def body(engh, ops=ops):
                waited = {}
                for o in ops:
                    for d in o.deps.values():
                        s, v = sem_for(d)
                        k = id(s)
                        if waited.get(k, 0) >= v:
                            continue
                        waited[k] = v
                        engh.wait_ge(s, v)
                    if o.fn is None:
                        continue
                    name, a, k = o.fn
                    ins = getattr(engh, name)(*a, **k)
                    if o.dma_key is not None:
                        s, v = sem_for(o)
                        ins.then_inc(s, 16)
                    elif o.inc:
                        s, v = sem_for(o)
                        ins.then_inc(s, 1)

            deco[e](body)
        self.n_sems = len(sems)


def _t5_bucket(rel):
    half = 16
    exact = 8
    n = np.abs(rel)
    large = exact + (np.log(np.maximum(n, 1) / exact) / math.log(1024 / exact)
                     * (half - exact)).astype(np.int32)
    large = np.minimum(large, half - 1)
    return (np.where(rel > 0, half, 0) + np.where(n < exact, n, large)).astype(np.int32)


def _c_rs(qr):
    return min(max(qr - 4, 0), 56)


def _c_sig(i, j):
    sig = []
    for krl in range(2):
        for qrl in range(2):
            kr = 2 * j + krl
            qr = 2 * i + qrl
            rs = _c_rs(qr)
            if rs <= kr < rs + 8:
                sig.append(kr - qr + 7)
            else:
                sig.append(None)
    return tuple(sig)


def _c_tiles():
    sigs = []
    table = []
    for i in range(32):
        row = []
        for j in range(32):
            sg = _c_sig(i, j)
            if all(v is None for v in sg):
                continue
            if sg not in sigs:
                sigs.append(sg)
            row.append((j, sigs.index(sg)))
        table.append(row)
    return sigs, table


C_SIGS, C_TABLE = _c_tiles()
NCT = len(C_SIGS)


def _consts():
    ohv = np.zeros((32, RT), np.float32)
    msk = np.zeros((16, RT), np.float32)
    u = np.arange(RT_D)
    rel = u - 255
    val = np.abs(rel) <= 128
    b = _t5_bucket(rel)
    ohv[b[val], u[val]] = 1.0
    msk[:, u[val]] = 1.0
    for g, dil in enumerate(DIL):
        off = RT_D + g * RT_A
        u = np.arange(RT_A)
        j = u - 191
        val = np.abs(j) <= 64
        b = _t5_bucket(j * dil)
        ohv[b[val], off + u[val]] = 1.0
        msk[:, off + u[val]] = 1.0
    jf = np.zeros((128, 128), np.float32)
    jf[np.arange(128), 127 - np.arange(128)] = 1.0
    ohc = np.zeros((31, 64, 64), np.float32)
    mc = np.zeros((64, 64), np.float32)
    for qc in range(64):
        cs = min(max(qc - 8, 0), 48)
        for kc in range(cs, cs + 16):
            ohc[kc - qc + 15, kc, qc] = 1.0
            mc[kc, qc] = 1.0
    maskc2 = np.concatenate([mc, mc], 0)
    inv = 10000.0 ** (-np.arange(16, dtype=np.float32) / 16)
    ang = np.arange(S, dtype=np.float32)[None, :] * inv[:, None]
    cos2 = np.concatenate([np.cos(ang), np.cos(ang)], 0).astype(np.float32)
    sin2 = np.concatenate([np.sin(ang), np.sin(ang)], 0).astype(np.float32)
    return dict(c_ohv=ohv, c_msk=msk, c_jf=jf, c_ohc=ohc.reshape(31, 4096),
                c_maskc=maskc2, c_cos=cos2, c_sin=sin2)


def build_nc(NS=2, L=DEPTH, dbg=False, phases="ABCD", mixers="abcd"):
    nc = bass.Bass("TRN2", target_bir_lowering=False)
    NT = NS * S

    def din(name, shape, dt=F32):
        return nc.dram_tensor(name, list(shape), dt, kind="ExternalInput").ap()

    def dscr(name, shape, dt=BF16):
        kind = "ExternalOutput" if dbg else "Internal"
        return nc.dram_tensor(name, list(shape), dt, kind=kind).ap()

    xT_in = din("xT", [D, NT])
    t5_in = din("t5", [32, 16])
    gmix_in = din("gmix", [128, L * 8])
    gffn_in = din("gffn", [128, L * 8])
    gfin_in = din("gfin", [128, 8])
    gq_in = din("gq", [128, L * 2])
    gkv_in = din("gkv", [128, L])
    cw_in = din("cw", [128, L * NFF * 3])
    cb_in = din("cb", [128, L * NFF])
    snk_in = din("snk", [128, L * 4])
    nab_in = din("nab", [L, 31, 60])
    w_in_in = din("w_in", [L, D, INC])
    w_uq_in = din("w_uq", [L, 256, 384])
    w_ukv_in = din("w_ukv", [L, 128, 512])
    w_gate_in = din("w_gate", [L, 4, D, D])
    w_br_in = din("w_branch", [L, 4, 256, D])
    w_out_in = din("w_out", [L, D, D])
    w_fg_in = din("w_ffn_gate", [L, D, DFF])
    w_fu_in = din("w_ffn_up", [L, D, DFF])
    w_fd_in = din("w_ffn_down", [L, DFF, D])
    c_ohv = din("c_ohv", [32, RT])
    c_msk = din("c_msk", [16, RT])
    c_jf = din("c_jf", [128, 128])
    c_ohc = din("c_ohc", [31, 4096])
    c_maskc = din("c_maskc", [128, 64])
    c_cos = din("c_cos", [32, S])
    c_sin = din("c_sin", [32, S])

    outT = nc.dram_tensor("outT", [D, NT], F32, kind="ExternalOutput").ap()

    evec_d = dscr("evec_d", [16, RT], F32)
    mcol_d = dscr("mcol_d", [60, 64, 64], F32)
    hT_d = dscr("hT_d", [D, NT])
    QA_d = dscr("QA_d", [3, 2, 128, NT])
    KA_d = dscr("KA_d", [3, 2, 128, NT])
    VA_d = dscr("VA_d", [NT, 768])
    QB_d = dscr("QB_d", [4, 96, NT])
    KB_d = dscr("KB_d", [4, 96, NT])
    VB_d = dscr("VB_d", [NT, 256])
    QC_d = dscr("QC_d", [2, 128, NT])
    KC_d = dscr("KC_d", [2, 128, NT])
    VC_d = dscr("VC_d", [NT, 256])
    QD_d = dscr("QD_d", [2, 128, NT])
    KD_d = dscr("KD_d", [128, NT])
    VD_d = dscr("VD_d", [NT, 128])
    yT_d = dscr("yT_d", [4, 256, NT])
    xa_d = dscr("xa_d", [D, NT], F32)
    xb_d = dscr("xb_d", [D, NT], F32)

    st = ExitStack()
    P = Prog(nc)
    big = st.enter_context(nc.sbuf_tensor("big", [128, SB_BYTES], U8))
    psb = [st.enter_context(nc.psum_tensor("ps%d" % i, [128, 512], F32))[:, :] for i in range(8)]

    class Arena:
        def __init__(self, base, limit):
            self.off = base
            self.limit = limit

        def tile(self, shape, dt):
            n = 1
            for v in shape:
                n *= v
            nb = n * (4 if dt == F32 else 2)
            nb_al = (nb + 63) // 64 * 64
            assert self.off + nb_al <= self.limit, ("SBUF arena overflow", self.off, nb_al, self.limit)
            ap = big[:, self.off:self.off + nb].bitcast(dt)
            self.off += nb_al
            if len(shape) == 2:
                ap = ap.rearrange("p (a b) -> p a b", a=shape[0])
            elif len(shape) == 3:
                ap = ap.rearrange("p (a b c) -> p a b c", a=shape[0], b=shape[1])
            return ap

    pa = Arena(0, 16 * 1024)
    gmix = pa.tile([L * 8], F32)
    gffn = pa.tile([L * 8], F32)
    gfin = pa.tile([8], F32)
    gq = pa.tile([L * 2], F32)
    gkv = pa.tile([L], F32)
    cw = pa.tile([L * NFF * 3], F32)
    cb = pa.tile([L * NFF], F32)
    snk = pa.tile([L * 4], F32)
    epsc = pa.tile([1], F32)
    ones_bf = pa.tile([128], BF16)
    jf_bf = pa.tile([128], BF16)
    EB_A = pa.tile([24, 128], BF16)
    EB_D = pa.tile([12, 128], BF16)
    PBASE = pa.off

    pscnt = [0, 0]

    def psum():
        i = pscnt[0] % 6
        pscnt[0] += 1
        return psb[i], "ps%d" % i

    def psum_acc():
        i = 6 + pscnt[1] % 2
        pscnt[1] += 1
        return psb[i], "ps%d" % i

    def dma(eng, out, in_, reads, writes, sem, partial=False):
        P.op(eng, lambda e: e.dma_start(out=out, in_=in_), reads=reads, writes=writes,
             sem=sem, partial=partial)

    def setup():
        ar = Arena(PBASE, SB_BYTES)
        for i, (t, src) in enumerate([(gmix, gmix_in), (gffn, gffn_in), (gfin, gfin_in), (gq, gq_in),
                                      (gkv, gkv_in), (cw, cw_in), (cb, cb_in), (snk, snk_in)]):
            dma("sp", t, src, [], ["sv%d" % i], "sv%d" % i)
        dma("pool", jf_bf, c_jf, [], ["jf"], "jf")
        P.op("dve", lambda e: e.memset(ones_bf, 1.0), writes=["ones"])
        P.op("dve", lambda e: e.memset(epsc, EPS), writes=["epsc"])
        P.op("act", lambda e: e.activation(out=snk, in_=snk, func=AF.Exp), reads=["sv7"], writes=["sv7"])
        t5f = ar.tile([16], F32)
        t5hi = ar.tile([16], BF16)
        t5hf = ar.tile([16], F32)
        t5lo = ar.tile([16], BF16)
        ohv = ar.tile([RT], BF16)
        mskt = ar.tile([RT], F32)
        evec = ar.tile([RT], F32)
        hk = ar.tile([36, 128], BF16)
        dma("sp", t5f[0:32], t5_in, [], ["t5f"], "t5f")
        dma("pool", ohv[0:32], c_ohv, [], ["ohv"], "ohv")
        dma("sp", mskt[0:16], c_msk, [], ["mskt"], "mskt")
        P.op("dve", lambda e: e.tensor_copy(t5hi[0:32], t5f[0:32]), reads=["t5f"], writes=["t5hi"])
        P.op("dve", lambda e: e.tensor_copy(t5hf[0:32], t5hi[0:32]), reads=["t5hi"], writes=["t5hf"])
        P.op("dve", lambda e: e.tensor_tensor(t5lo[0:32], t5f[0:32], t5hf[0:32], ALU.subtract),
             reads=["t5f", "t5hf"], writes=["t5lo"])
        ncol = 415
        for k in range(4):
            ps, pk = psum()
            c0 = k * ncol
            P.op("pe", lambda e, ps=ps, c0=c0: e.matmul(ps[0:16, 0:ncol], t5hi[0:32, :], ohv[0:32, c0:c0 + ncol],
                                                        start=True, stop=False),
                 reads=["t5hi", "ohv"], writes=[pk])
            P.op("pe", lambda e, ps=ps, c0=c0: e.matmul(ps[0:16, 0:ncol], t5lo[0:32, :], ohv[0:32, c0:c0 + ncol],
                                                        start=False, stop=True),
                 reads=["t5lo", "ohv"], writes=[pk])
            P.op("act", lambda e, ps=ps, c0=c0: e.activation(out=evec[0:16, c0:c0 + ncol], in_=ps[0:16, 0:ncol],
                                                             func=AF.Exp),
                 reads=[pk], writes=["evec"], partial=True)
        P.op("dve", lambda e: e.tensor_tensor(evec[0:16], evec[0:16], mskt[0:16], ALU.mult),
             reads=["evec", "mskt"], writes=["evec"])
        dma("sp", evec_d, evec[0:16], ["evec"], [], "evst")
        P.barrier()
        tiles = []
        for h in range(4):
            for d in (-1, 0, 1):
                tiles.append((12 + h, 0 + d * 128 + 128))
        for g in range(3):
            for h in range(4):
                for ab in range(2):
                    tiles.append((4 * g + h, RT_D + g * RT_A + (0 if ab == 0 else 128)))
        for t, (row, u0) in enumerate(tiles):
            src = bass.AP(evec_d.tensor, row * RT + u0, [[1, 128], [1, 128]])
            dma("pool", hk[:, t, :], src, [], ["hk%d" % t], "hk%d" % t)
        for b in range(9):
            ps, pk = psum()
            for q in range(4):
                t = b * 4 + q
                P.op("pe", lambda e, ps=ps, t=t, q=q: e.matmul(ps[:, q * 128:(q + 1) * 128], hk[:, t, :], jf_bf,
                                                               start=True, stop=True),
                     reads=["hk%d" % t, "jf"], writes=[pk])
            if b < 3:
                dst = EB_D[:, b * 4:(b + 1) * 4, :]
            else:
                dst = EB_A[:, (b - 3) * 4:(b - 2) * 4, :]
            P.op("dve", lambda e, ps=ps, dst=dst: e.tensor_copy(dst, ps.rearrange("p (a b) -> p a b", a=4)),
                 reads=[pk], writes=["EB"], partial=True)
        P.barrier()

    def rms_norm(xt, xkey, nch, n, gcol, sq, sqkey, rt, rtkey, rstd, rstdkey, hT, hkey, feat):
        P.op("act", lambda e: e.activation(out=sq, in_=xt, func=AF.Square), reads=[xkey], writes=[sqkey])
        ps, pk = psum()
        for c in range(nch):
            P.op("pe", lambda e, c=c: e.matmul(ps[:, 0:n], ones_bf, sq[:, c, :], start=(c == 0), stop=(c == nch - 1)),
                 reads=[sqkey, "ones"], writes=[pk])
        P.op("act", lambda e: e.activation(out=rt, in_=ps[:, 0:n], func=AF.Sqrt, scale=1.0 / feat, bias=epsc),
             reads=[pk, "epsc"], writes=[rtkey])
        P.op("dve", lambda e: e.reciprocal(rstd, rt), reads=[rtkey], writes=[rstdkey])
        for c in range(nch):
            P.op("dve", lambda e, c=c: e.scalar_tensor_tensor(hT[:, c, :], xt[:, c, :], gcol(c), rstd,
                                                              ALU.mult, ALU.mult),
                 reads=[xkey, rstdkey], writes=[hkey], partial=True)

    def phase_a(l, x_src):
        ar = Arena(PBASE, SB_BYTES)
        w_in = ar.tile([8, INC], BF16)
        wkrot = ar.tile([8, 96], BF16)
        wuq = ar.tile([2, 384], BF16)
        wuqr = ar.tile([2, 4, 96], BF16)
        wukv = ar.tile([512], BF16)
        xt = [ar.tile([8, 512], F32) for _ in range(2)]
        sq = ar.tile([8, 512], BF16)
        rt = ar.tile([512], F32)
        rstd = ar.tile([512], F32)
        hT = [ar.tile([8, 512], BF16) for _ in range(2)]
        cs = [ar.tile([2, 512], F32) for _ in range(1)]
        NSTG = 4
        stg = [ar.tile([512], BF16) for _ in range(NSTG)]
        vst = [ar.tile([4, 1152], BF16) for _ in range(1)]
        cq = ar.tile([2, 512], F32)
        cqsq = ar.tile([2, 512], BF16)
        cqn = ar.tile([2, 512], BF16)
        ckv = ar.tile([1, 512], F32)
        ckvsq = ar.tile([1, 512], BF16)
        ckvn = ar.tile([1, 512], BF16)
        rt2 = ar.tile([512], F32)
        rs2 = ar.tile([512], F32)
        kr = ar.tile([512], F32)
        t1 = ar.tile([512], F32)
        t2 = ar.tile([512], F32)
        qst = [ar.tile([512], BF16) for _ in range(2)]
        kst = [ar.tile([512], BF16) for _ in range(2)]
        vbst = [ar.tile([4, 256], BF16) for _ in range(1)]

        for c in range(8):
            dma("pool", w_in[:, c, :], w_in_in[l, c * 128:(c + 1) * 128, :], [], ["w_in%d" % c], "w_in%d" % c)
        dma("pool", wuq, w_uq_in[l].rearrange("(c p) n -> p c n", p=128), [], ["wuq"], "wuq")
        dma("pool", wukv, w_ukv_in[l], [], ["wukv"], "wukv")
        allw = ["w_in%d" % c for c in range(8)]
        P.op("dve", lambda e: e.memset(wkrot, 0.0), writes=["wkrot"])
        P.op("dve", lambda e: e.tensor_scalar(wkrot[:, :, 64:80], w_in[:, :, 2704:2720], -1.0, 0.0, ALU.mult, ALU.add),
             reads=allw, writes=["wkrot"])
        P.op("dve", lambda e: e.tensor_copy(wkrot[:, :, 80:96], w_in[:, :, 2688:2704]), reads=allw, writes=["wkrot"])
        P.op("dve", lambda e: e.memset(wuqr, 0.0), writes=["wuqr"])
        for h in range(4):
            P.op("dve", lambda e, h=h: e.tensor_scalar(wuqr[:, :, h, 64:80], wuq[:, :, h * 96 + 80:h * 96 + 96],
                                                       -1.0, 0.0, ALU.mult, ALU.add), reads=["wuq"], writes=["wuqr"])
            P.op("dve", lambda e, h=h: e.tensor_copy(wuqr[:, :, h, 80:96], wuq[:, :, h * 96 + 64:h * 96 + 80]),
                 reads=["wuq"], writes=["wuqr"])

        tiles = [(s, T) for s in range(NS) for T in range(8)]

        def load(idx):
            s, T = tiles[idx]
            b = idx % 2
            t0 = s * S + T * 512
            dma("sp", xt[b], x_src.rearrange("(c p) n -> p c n", p=128)[:, :, t0:t0 + 512], [], ["xt%d" % b],
                "xt%d" % b)

        evac_rr = [0]

        def evac(dst, src, reads, writes, partial=False):
            evac_rr[0] += 1
            if evac_rr[0] % 2:
                P.op("act", lambda e: e.copy(dst, src), reads=reads, writes=writes, partial=partial)
            else:
                P.op("dve", lambda e: e.tensor_copy(dst, src), reads=reads, writes=writes, partial=partial)

        stg_rr = [0]

        def norm_a(idx):
            s_, T_ = tiles[idx]
            b_ = idx % 2
            t0_ = s_ * S + T_ * 512
            rms_norm(xt[b_], "xt%d" % b_, 8, 512, lambda c: gmix[:, l * 8 + c:l * 8 + c + 1], sq, "sq", rt, "rt",
                     rstd, "rstd", hT[b_], "hT%d" % b_, float(D))
            dma("pool", hT_d.rearrange("(c p) n -> p c n", p=128)[:, :, t0_:t0_ + 512], hT[b_], ["hT%d" % b_], [],
                "sthT%d" % b_)

        load(0)
        if len(tiles) > 1:
            load(1)
        norm_a(0)
        for idx, (s, T) in enumerate(tiles):
            b = idx % 2
            t0 = s * S + T * 512
            if idx + 1 < len(tiles):
                norm_a(idx + 1)
            if idx + 2 < len(tiles):
                load(idx + 2)
            dma("sp", cs[0][64:96, 0, :], c_cos[:, T * 512:(T + 1) * 512], [], ["cs0"], "cs0", partial=True)
            dma("sp", cs[0][64:96, 1, :], c_sin[:, T * 512:(T + 1) * 512], [], ["cs0"], "cs0", partial=True)
            xk, hk_ = "xt%d" % b, "hT%d" % b

            def fm_chunk(col, M, w=None, wkey=None):
                ps, pk = psum()
                for c in range(8):
                    if w is None:
                        lhsT = w_in[:, c, col:col + M]
                        rk = "w_in%d" % c
                    else:
                        lhsT = w[:, c, col:col + M]
                        rk = wkey
                    P.op("pe", lambda e, c=c, lhsT=lhsT: e.matmul(ps[0:M, :], lhsT, hT[b][:, c, :],
                                                                  start=(c == 0), stop=(c == 7)),
                         reads=[rk, hk_], writes=[pk])
                return ps, pk

            def out_chunk(ps, pk, dst_dram, dil=1):
                i = stg_rr[0] % NSTG
                stg_rr[0] += 1
                sk = "stg%d" % i
                if dil == 1:
                    evac(stg[i], ps, [pk], [sk])
                    dma("pool", dst_dram[:, t0:t0 + 512], stg[i], [sk], [], "st" + sk)
                else:
                    J = 512 // dil
                    evac(stg[i].rearrange("p (r j) -> p j r", r=dil), ps.rearrange("p (j r) -> p j r", r=dil),
                         [pk], [sk])
                    Lg = S // dil
                    dst = dst_dram[:, s * S:(s + 1) * S].rearrange("p (r j) -> p r j", r=dil)[:, :, T * J:(T + 1) * J]
                    dma("pool", dst, stg[i].rearrange("p (r j) -> p r j", r=dil), [sk], [], "st" + sk)

            for g in range(3):
                for hp in range(2):
                    ps, pk = fm_chunk((g * 4 + hp * 2) * 64, 128)
                    out_chunk(ps, pk, QA_d[g, hp], DIL[g])
                    ps, pk = fm_chunk(768 + (g * 4 + hp * 2) * 64, 128)
                    out_chunk(ps, pk, KA_d[g, hp], DIL[g])
            for hp in range(2):
                ps, pk = fm_chunk(2720 + hp * 128, 128)
                out_chunk(ps, pk, QC_d[hp])
                ps, pk = fm_chunk(2976 + hp * 128, 128)
                out_chunk(ps, pk, KC_d[hp])
                ps, pk = fm_chunk(3488 + hp * 128, 128)
                out_chunk(ps, pk, QD_d[hp])
            ps, pk = fm_chunk(3744, 128)
            out_chunk(ps, pk, KD_d)

            for c2 in range(2):
                ps, pk = fm_chunk(2304 + c2 * 128, 128)
                evac(cq[:, c2, :], ps, [pk], ["cq"], partial=True)
            ps, pk = fm_chunk(2560, 128)
            evac(ckv[:, 0, :], ps, [pk], ["ckv"])
            rms_norm(cq, "cq", 2, 512, lambda c: gq[:, l * 2 + c:l * 2 + c + 1], cqsq, "cqsq", rt2, "rt2", rs2, "rs2",
                     cqn, "cqn", 256.0)
            rms_norm(ckv, "ckv", 1, 512, lambda c: gkv[:, l:l + 1], ckvsq, "ckvsq", rt2, "rt2", rs2, "rs2",
                     ckvn, "ckvn", 128.0)
            psk, pkk = fm_chunk(2624, 96)
            psr, pkr = fm_chunk(0, 96, w=wkrot, wkey="wkrot")
            csk = "cs0"
            P.op("dve", lambda e: e.tensor_tensor(t1[64:96], psk[64:96, :], cs[0][64:96, 0, :], ALU.mult),
                 reads=[pkk, csk], writes=["t1"])
            P.op("dve", lambda e: e.tensor_tensor(t2[64:96], psr[64:96, :], cs[0][64:96, 1, :], ALU.mult),
                 reads=[pkr, csk], writes=["t2"])
            P.op("dve", lambda e: e.tensor_tensor(kr[64:96], t1[64:96], t2[64:96], ALU.add),
                 reads=["t1", "t2"], writes=["kr"])
            for h in range(4):
                hb = h % 2
                psq, pkq = psum()
                psq2, pkq2 = psum()
                for c2 in range(2):
                    P.op("pe", lambda e, c2=c2: e.matmul(psq[0:96, :], wuq[:, c2, h * 96:(h + 1) * 96], cqn[:, c2, :],
                                                         start=(c2 == 0), stop=(c2 == 1)),
                         reads=["wuq", "cqn"], writes=[pkq])
                for c2 in range(2):
                    P.op("pe", lambda e, c2=c2: e.matmul(psq2[0:96, :], wuqr[:, c2, h, :], cqn[:, c2, :],
                                                         start=(c2 == 0), stop=(c2 == 1)),
                         reads=["wuqr", "cqn"], writes=[pkq2])
                qk = "qst%d" % hb
                P.op("dve", lambda e: e.tensor_copy(qst[hb][0:64], psq[0:64, :]), reads=[pkq], writes=[qk])
                P.op("dve", lambda e: e.tensor_tensor(t1[64:96], psq[64:96, :], cs[0][64:96, 0, :], ALU.mult),
                     reads=[pkq, csk], writes=["t1"])
                P.op("dve", lambda e: e.tensor_tensor(t2[64:96], psq2[64:96, :], cs[0][64:96, 1, :], ALU.mult),
                     reads=[pkq2, csk], writes=["t2"])
                P.op("dve", lambda e: e.tensor_tensor(qst[hb][64:96], t1[64:96], t2[64:96], ALU.add),
                     reads=["t1", "t2", qk], writes=[qk], partial=True)
                dma("pool", QB_d[h][:, t0:t0 + 512], qst[hb][0:96], [qk], [], "st" + qk)
                psn, pkn = psum()
                P.op("pe", lambda e: e.matmul(psn[0:64, :], wukv[:, h * 128:h * 128 + 64], ckvn[:, 0, :],
                                              start=True, stop=True), reads=["wukv", "ckvn"], writes=[pkn])
                kk = "kst%d" % hb
                evac(kst[hb][0:64], psn[0:64, :], [pkn], [kk])
                P.op("pool", lambda e: e.tensor_copy(kst[hb][64:96], kr[64:96]), reads=["kr", kk], writes=[kk],
                     partial=True)
                dma("pool", KB_d[h][:, t0:t0 + 512], kst[hb][0:96], [kk], [], "st" + kk)
            vb = 0
            vbk = "vbst%d" % vb
            wv = wukv.rearrange("p (h x) -> p h x", h=4)[:, :, 64:128]
            for tb in range(4):
                ps, pk = psum()
                P.op("pe", lambda e, tb=tb, ps=ps: e.matmul(ps[:, 0:256].rearrange("p (h x) -> p h x", h=4),
                                                            ckvn[:, 0, tb * 128:(tb + 1) * 128], wv,
                                                            start=True, stop=True),
                     reads=["wukv", "ckvn"], writes=[pk])
                evac(vbst[vb][:, tb, :], ps[:, 0:256], [pk], [vbk], partial=True)
            dma("pool", VB_d[t0:t0 + 512, :].rearrange("(tb p) f -> p tb f", p=128), vbst[vb], [vbk], [], "st" + vbk)

            vk = "vst%d" % vb
            groups = [(1536, 512, 0), (2048, 256, 512), (3232, 256, 768), (3872, 128, 1024)]
            for tb in range(4):
                for (col, n, so) in groups:
                    ps, pk = psum()
                    for c in range(8):
                        P.op("pe", lambda e, c=c, ps=ps, col=col, n=n: e.matmul(
                            ps[:, 0:n], hT[b][:, c, tb * 128:(tb + 1) * 128], w_in[:, c, col:col + n],
                            start=(c == 0), stop=(c == 7)), reads=["w_in%d" % c, hk_], writes=[pk])
                    evac(vst[vb][:, tb, so:so + n], ps[:, 0:n], [pk], [vk], partial=True)
            dma("pool", VA_d[t0:t0 + 512, :].rearrange("(tb p) f -> p tb f", p=128), vst[vb][:, :, 0:768], [vk], [],
                "stva%d" % vb)
            dma("pool", VC_d[t0:t0 + 512, :].rearrange("(tb p) f -> p tb f", p=128), vst[vb][:, :, 768:1024], [vk], [],
                "stvc%d" % vb)
            dma("pool", VD_d[t0:t0 + 512, :].rearrange("(tb p) f -> p tb f", p=128), vst[vb][:, :, 1024:1152], [vk], [],
                "stvd%d" % vb)
        P.barrier()

    def build_ebc(l, ar):
        EBC = ar.tile([4 * NCT, 128], BF16)
        sub = Arena(ar.off, SB_BYTES)
        nbf = sub.tile([60], F32)
        nbhi = sub.tile([60], BF16)
        nbhf = sub.tile([60], F32)
        nblo = sub.tile([60], BF16)
        ohc = sub.tile([4096], BF16)
        mcs = sub.tile([4096], F32)
        mcolS = sub.tile([60, 64], BF16)
        mk2 = sub.tile([64], BF16)
        dma("sp", nbf[0:31], nab_in[l], [], ["nbf"], "nbf")
        dma("pool", ohc[0:31], c_ohc, [], ["ohc"], "ohc")
        dma("pool", mk2, c_maskc, [], ["mk2"], "mk2")
        P.op("dve", lambda e: e.tensor_copy(nbhi[0:31], nbf[0:31]), reads=["nbf"], writes=["nbhi"])
        P.op("dve", lambda e: e.tensor_copy(nbhf[0:31], nbhi[0:31]), reads=["nbhi"], writes=["nbhf"])
        P.op("dve", lambda e: e.tensor_tensor(nblo[0:31], nbf[0:31], nbhf[0:31], ALU.subtract),
             reads=["nbf", "nbhf"], writes=["nblo"])
        for k in range(8):
            ps, pk = psum()
            P.op("pe", lambda e, ps=ps, k=k: e.matmul(ps[0:60, :], nbhi[0:31, :], ohc[0:31, k * 512:(k + 1) * 512],
                                                      start=True, stop=False), reads=["nbhi", "ohc"], writes=[pk])
            P.op("pe", lambda e, ps=ps, k=k: e.matmul(ps[0:60, :], nblo[0:31, :], ohc[0:31, k * 512:(k + 1) * 512],
                                                      start=False, stop=True), reads=["nblo", "ohc"], writes=[pk])
            P.op("act", lambda e, ps=ps, k=k: e.activation(out=mcs[0:60, k * 512:(k + 1) * 512], in_=ps[0:60, :],
                                                           func=AF.Exp), reads=[pk], writes=["mcs"], partial=True)
        dma("sp", mcol_d.rearrange("m a b -> m (a b)"), mcs[0:60], ["mcs"], [], "stmcs")
        P.barrier()
        src = mcol_d.rearrange("m kc qc -> kc m qc")
        dma("pool", mcolS[0:64], src, [], ["mcolS"], "mcolS", partial=True)
        dma("pool", mcolS[64:128], src, [], ["mcolS"], "mcolS", partial=True)
        P.op("dve", lambda e: e.tensor_tensor(mcolS, mcolS, mk2.unsqueeze(1).to_broadcast([128, 60, 64]), ALU.mult),
             reads=["mcolS", "mk2"], writes=["mcolS"])
        P.op("pool", lambda e: e.memset(EBC, 0.0), writes=["EBC"])
        rr = 0
        for h in range(4):
            for ti, sg in enumerate(C_SIGS):
                k = 0
                for krl in range(2):
                    for qrl in range(2):
                        dr = sg[k]
                        k += 1
                        if dr is None:
                            continue
                        dst = EBC[krl * 64:(krl + 1) * 64, h * NCT + ti, qrl * 64:(qrl + 1) * 64]
                        srcv = mcolS[krl * 64:(krl + 1) * 64, h * 15 + dr, :]
                        eng = ("dve", "pool")[rr % 2]
                        rr += 1
                        P.op(eng, lambda e, dst=dst, srcv=srcv: e.tensor_copy(dst, srcv), reads=["mcolS", "EBC"],
                             writes=["EBC"], partial=True)
        P.barrier()
        return EBC

    def phase_b(l):
        ar0 = Arena(PBASE, SB_BYTES)
        EBC = build_ebc(l, ar0)
        base = ar0.off
        NPE = 12
        pexp = [ar0.tile([512], BF16) for _ in range(NPE)]
        pt = [ar0.tile([512], BF16) for _ in range(NPE)]
        rec = [ar0.tile([512], F32) for _ in range(2)]
        yst = [ar0.tile([2, 512], BF16) for _ in range(2)]
        wbase = ar0.off
        cnt = {"pe": 0, "rec": 0, "mul": 0}

        def score_slot(qbs, eb, scale, ebkey, zero=()):
            ps, pk = psum()
            for q, ent in enumerate(qbs):
                if ent is None:
                    continue
                kT, qT, lo, hi, rds = ent
                if lo == 0 and hi == 128:
                    P.op("pe", lambda e, kT=kT, qT=qT, q=q: e.matmul(ps[:, q * 128:(q + 1) * 128], kT, qT,
                                                                     start=True, stop=True), reads=rds, writes=[pk])
                else:
                    pb = kT.base_partition()
                    P.op("pe", lambda e, kT=kT, qT=qT, q=q, lo=lo, hi=hi, pb=pb: e.matmul(
                        ps[lo:hi, q * 128:(q + 1) * 128], kT, qT, start=True, stop=True, tile_position=(pb, lo)),
                        reads=rds, writes=[pk])
            i = cnt["pe"] % NPE
            cnt["pe"] += 1
            if eb is None:
                P.op("act", lambda e: e.activation(out=pt[i], in_=ps, func=AF.Exp, scale=scale), reads=[pk],
                     writes=["pt%d" % i])
                return pt[i], "pt%d" % i
            P.op("act", lambda e: e.activation(out=pexp[i], in_=ps, func=AF.Exp, scale=scale), reads=[pk],
                 writes=["pexp%d" % i])
            eng = ("dve", "pool")[cnt["mul"] % 2]
            cnt["mul"] += 1
            if isinstance(eb, list):
                for q, ebq in enumerate(eb):
                    if ebq is None or qbs[q] is None:
                        continue
                    P.op(eng, lambda e, q=q, ebq=ebq: e.tensor_tensor(pt[i][:, q * 128:(q + 1) * 128],
                                                                      pexp[i][:, q * 128:(q + 1) * 128], ebq, ALU.mult),
                         reads=["pexp%d" % i, ebkey], writes=["pt%d" % i], partial=(q > 0))
            else:
                P.op(eng, lambda e: e.tensor_tensor(pt[i].rearrange("p (a b) -> p a b", a=4),
                                                    pexp[i].rearrange("p (a b) -> p a b", a=4),
                                                    eb.unsqueeze(1).to_broadcast([128, 4, 128]), ALU.mult),
                     reads=["pexp%d" % i, ebkey], writes=["pt%d" % i])
                for (q, zlo, zhi) in zero:
                    P.op(eng, lambda e, q=q, zlo=zlo, zhi=zhi: e.memset(pt[i][zlo:zhi, q * 128:(q + 1) * 128], 0.0),
                         writes=["pt%d" % i], partial=True)
            return pt[i], "pt%d" % i

        def pv(ot, otk, pts, vents):
            for q in range(4):
                lst = [(sl, vents[sl][q]) for sl in range(len(pts)) if vents[sl][q] is not None]
                for n, (sl, (va, lo, hi, rds)) in enumerate(lst):
                    ptile, ptk = pts[sl]
                    P.op("pe", lambda e, va=va, lo=lo, hi=hi, ptile=ptile, q=q, n=n, last=len(lst) - 1: e.matmul(
                        ot[:, q * 128:(q + 1) * 128], va, ptile[lo:hi, q * 128:(q + 1) * 128],
                        start=(n == 0), stop=(n == last)), reads=rds + [ptk], writes=[otk])

        def normalize(ot, otk, dst, dstkey, addcol=None, partial=True):
            i = cnt["rec"] % 2
            cnt["rec"] += 1
            rk = "rec%d" % i
            if addcol is not None:
                P.op("dve", lambda e: e.tensor_scalar(rec[i][0:64], ot[64:128, :], addcol, 1.0, ALU.add, ALU.mult),
                     reads=[otk], writes=[rk])
                P.op("dve", lambda e: e.reciprocal(rec[i][0:64], rec[i][0:64]), reads=[rk], writes=[rk])
            else:
                P.op("dve", lambda e: e.reciprocal(rec[i][0:64], ot[64:128, :]), reads=[otk], writes=[rk])
            P.op("dve", lambda e: e.tensor_tensor(dst, ot[0:64, :], rec[i][0:64], ALU.mult), reads=[otk, rk],
                 writes=[dstkey], partial=partial)

        def store_y(m, s, c2, ch, ysti, yk):
            dma("pool", yT_d[m, c2 * 128:(c2 + 1) * 128, s * S + ch * 512:s * S + (ch + 1) * 512], yst[ysti][:, 0, :],
                [yk], [], "st" + yk)

        ycnt = [0]

        def run_units(units, lag=1):
            for ui in range(len(units) + lag):
                if ui < len(units):
                    units[ui][0]()
                if ui >= lag:
                    units[ui - lag][1]()

        for s in range(NS):
            if "d" in mixers:
                ar = Arena(wbase, SB_BYTES)
                QT = [ar.tile([S], BF16) for _ in range(2)]
                KT = [ar.tile([S], BF16) for _ in range(2)]
                VG = ar.tile([32, 2, 128], BF16)
                for hp in range(2):
                    dma("sp", QT[hp], QD_d[hp][:, s * S:(s + 1) * S], [], ["QT%d" % hp], "QT%d" % hp)
                    for hb in range(2):
                        dma("sp", KT[hp][hb * 64:(hb + 1) * 64], KD_d[hp * 64:(hp + 1) * 64, s * S:(s + 1) * S], [],
                            ["KT%d" % hp], "KT%d" % hp, partial=True)
                P.op("pool", lambda e: e.memset(VG[:, :, :, 64:128], 1.0), writes=["VG"])
                for g_ in range(2):
                    dma("sp", VG[:, :, g_, 0:64],
                        VD_d[s * S:(s + 1) * S, g_ * 64:(g_ + 1) * 64].rearrange("(kb p) x -> p kb x", p=128),
                        [], ["VG"], "VG", partial=True)
                units = []
                for hp in range(2):
                    for ch in range(8):
                        yi = ycnt[0] % 2
                        ycnt[0] += 1
                        yk = "yst%d" % yi
                        for hb in range(2):
                            h = hp * 2 + hb
                            kvh = h // 2
                            box = {}

                            def st1(hp=hp, ch=ch, hb=hb, h=h, kvh=kvh, box=box):
                                pts = []
                                vents = []
                                for d in (-1, 0, 1):
                                    qbs = []
                                    vv = []
                                    for q in range(4):
                                        i = ch * 4 + q
                                        j = i + d
                                        if j < 0 or j > 31:
                                            qbs.append(None)
                                            vv.append(None)
                                            continue
                                        qbs.append((KT[hp][hb * 64:(hb + 1) * 64, j * 128:(j + 1) * 128],
                                                    QT[hp][hb * 64:(hb + 1) * 64, i * 128:(i + 1) * 128], 0, 128,
                                                    ["KT%d" % hp, "QT%d" % hp]))
                                        vv.append((VG[:, j, kvh, :], 0, 128, ["VG"]))
                                    pts.append(score_slot(qbs, EB_D[:, h * 3 + d + 1, :], 0.125, "EB"))
                                    vents.append(vv)
                                box["pts"] = pts
                                box["vents"] = vents

                            def st2(hp=hp, ch=ch, hb=hb, h=h, box=box, yi=yi, yk=yk):
                                ot, otk = psum_acc()
                                pv(ot, otk, box["pts"], box["vents"])
                                normalize(ot, otk, yst[yi][hb * 64:(hb + 1) * 64, 0, :], yk,
                                          addcol=snk[64:128, l * 4 + h:l * 4 + h + 1], partial=(hb == 1))
                                if hb == 1:
                                    store_y(3, s, hp, ch, yi, yk)

                            units.append((st1, st2))
                run_units(units)
                P.barrier()
            if "c" in mixers:
                ar = Arena(wbase, SB_BYTES)
                QT = [ar.tile([S], BF16) for _ in range(2)]
                KT = [ar.tile([S], BF16) for _ in range(2)]
                VG = ar.tile([32, 4, 128], BF16)
                for hp in range(2):
                    dma("sp", QT[hp], QC_d[hp][:, s * S:(s + 1) * S], [], ["QT%d" % hp], "QT%d" % hp)
                    dma("sp", KT[hp], KC_d[hp][:, s * S:(s + 1) * S], [], ["KT%d" % hp], "KT%d" % hp)
                P.op("pool", lambda e: e.memset(VG[:, :, :, 64:128], 1.0), writes=["VG"])
                for g_ in range(4):
                    dma("sp", VG[:, :, g_, 0:64],
                        VC_d[s * S:(s + 1) * S, g_ * 64:(g_ + 1) * 64].rearrange("(kb p) x -> p kb x", p=128),
                        [], ["VG"], "VG", partial=True)
                units = []
                for hp in range(2):
                    for ch in range(8):
                        yi = ycnt[0] % 2
                        ycnt[0] += 1
                        yk = "yst%d" % yi
                        dlist = sorted({j - (ch * 4 + q) for q in range(4) for (j, _) in C_TABLE[ch * 4 + q]})
                        for hb in range(2):
                            h = hp * 2 + hb
                            box = {}

                            def st1(hp=hp, ch=ch, hb=hb, h=h, box=box, dlist=dlist):
                                pts = []
                                vents = []
                                for d in dlist:
                                    qbs = []
                                    vv = []
                                    ebl = []
                                    for q in range(4):
                                        i = ch * 4 + q
                                        j = i + d
                                        ti = dict(C_TABLE[i]).get(j)
                                        if ti is None:
                                            qbs.append(None)
                                            vv.append(None)
                                            ebl.append(None)
                                            continue
                                        qbs.append((KT[hp][hb * 64:(hb + 1) * 64, j * 128:(j + 1) * 128],
                                                    QT[hp][hb * 64:(hb + 1) * 64, i * 128:(i + 1) * 128], 0, 128,
                                                    ["KT%d" % hp, "QT%d" % hp]))
                                        vv.append((VG[:, j, h, :], 0, 128, ["VG"]))
                                        ebl.append(EBC[:, h * NCT + ti, :])
                                    tis = {dict(C_TABLE[ch * 4 + q]).get(ch * 4 + q + d) for q in range(4)}
                                    if len(tis) == 1 and None not in tis:
                                        eb = ebl[0]
                                    else:
                                        eb = ebl
                                    pts.append(score_slot(qbs, eb, 0.125, "EBC"))
                                    vents.append(vv)
                                box["pts"] = pts
                                box["vents"] = vents

                            def st2(hp=hp, ch=ch, hb=hb, box=box, yi=yi, yk=yk):
                                ot, otk = psum_acc()
                                pv(ot, otk, box["pts"], box["vents"])
                                normalize(ot, otk, yst[yi][hb * 64:(hb + 1) * 64, 0, :], yk, partial=(hb == 1))
                                if hb == 1:
                                    store_y(2, s, hp, ch, yi, yk)

                            units.append((st1, st2))
                run_units(units)
                P.barrier()
            if "a" in mixers:
                for hp in range(2):
                    ar = Arena(wbase, SB_BYTES)
                    QT = ar.tile([S], BF16)
                    KT = ar.tile([S + 128], BF16)
                    VG = ar.tile([48, 2, 128], BF16)
                    OA = [ar.tile([S], F32) for _ in range(2)]
                    P.op("pool", lambda e: e.memset(VG, 0.0), writes=["VG"])
                    P.op("pool", lambda e: e.memset(VG[:, :, :, 64:128], 1.0), writes=["VG"])
                    P.op("pool", lambda e: e.memset(KT[:, 0:64], 0.0), writes=["KTpad"])
                    P.op("pool", lambda e: e.memset(KT[:, S + 64:S + 128], 0.0), writes=["KTpad"])
                    for g in range(3):
                        dil = DIL[g]
                        Lg = S // dil
                        nb = Lg // 128
                        dma("sp", QT, QA_d[g, hp][:, s * S:(s + 1) * S], [], ["QT"], "QT")
                        dma("sp", KT[:, 64:64 + S], KA_d[g, hp][:, s * S:(s + 1) * S], [], ["KT"], "KT")
                        vsrc = VA_d[s * S:(s + 1) * S, :].rearrange("(j r) (g x) -> r j g x", r=dil, g=3)
                        for r in range(dil):
                            for m in range(nb + 1):
                                lo = 64 if m == 0 else 0
                                hi = 64 if m == nb else 128
                                j0 = 128 * m - 64 + lo
                                src = vsrc[r, j0:j0 + (hi - lo), g, :].rearrange("j (h x) -> j h x", h=4)[
                                    :, hp * 2:hp * 2 + 2, :]
                                dma("sp", VG[lo:hi, r * (nb + 1) + m, :, 0:64], src, [], ["VG"], "VG", partial=True)
                        nqb = S // 128
                        units = []
                        for ch in range(nqb // 4):
                            for hb in range(2):
                                h = hp * 2 + hb
                                box = {}

                                def st1(ch=ch, hb=hb, h=h, box=box, g=g, Lg=Lg, nb=nb):
                                    pts = []
                                    vents = []
                                    for ab in range(2):
                                        qbs = []
                                        vv = []
                                        zer = []
                                        for q in range(4):
                                            gi = ch * 4 + q
                                            r, i = divmod(gi, nb)
                                            m = i + ab
                                            k0 = 64 + r * Lg + 128 * m - 64
                                            qbs.append((KT[hb * 64:(hb + 1) * 64, k0:k0 + 128],
                                                        QT[hb * 64:(hb + 1) * 64, gi * 128:(gi + 1) * 128], 0, 128,
                                                        ["KT", "KTpad", "QT"]))
                                            vv.append((VG[:, r * (nb + 1) + m, hb, :], 0, 128, ["VG"]))
                                            if m == 0:
                                                zer.append((q, 0, 64))
                                            elif m == nb:
                                                zer.append((q, 64, 128))
                                        pts.append(score_slot(qbs, EB_A[:, (g * 4 + h) * 2 + ab, :], 0.125, "EB",
                                                              zero=zer))
                                        vents.append(vv)
                                    box["pts"] = pts
                                    box["vents"] = vents

                                def st2(ch=ch, hb=hb, box=box, dil=dil, Lg=Lg):
                                    ot, otk = psum_acc()
                                    pv(ot, otk, box["pts"], box["vents"])
                                    if dil == 1:
                                        dst = OA[hb][:, ch * 512:(ch + 1) * 512]
                                        P.op("act", lambda e, dst=dst, ot=ot: e.copy(dst, ot), reads=[otk],
                                             writes=["OA%d" % hb], partial=True)
                                    else:
                                        oav = OA[hb].rearrange("p (j r) -> p r j", r=dil)
                                        pos0 = ch * 512
                                        done = 0
                                        while done < 512:
                                            r, j = divmod(pos0 + done, Lg)
                                            n = min(512 - done, Lg - j)
                                            dst = oav[:, r, j:j + n]
                                            srcv = ot[:, done:done + n]
                                            P.op("dve", lambda e, dst=dst, srcv=srcv: e.tensor_tensor(dst, dst, srcv,
                                                                                                      ALU.add),
                                                 reads=[otk, "OA%d" % hb], writes=["OA%d" % hb], partial=True)
                                            done += n

                                units.append((st1, st2))
                        run_units(units)
                    for ch in range(8):
                        yi = ycnt[0] % 2
                        ycnt[0] += 1
                        yk = "yst%d" % yi
                        for hb in range(2):
                            i = cnt["rec"] % 2
                            cnt["rec"] += 1
                            rk = "rec%d" % i
                            oa = OA[hb][:, ch * 512:(ch + 1) * 512]
                            P.op("dve", lambda e, oa=oa, i=i: e.reciprocal(rec[i][0:64], oa[64:128]),
                                 reads=["OA%d" % hb], writes=[rk])
                            P.op("dve", lambda e, oa=oa, i=i, hb=hb, yi=yi: e.tensor_tensor(
                                yst[yi][hb * 64:(hb + 1) * 64, 0, :], oa[0:64], rec[i][0:64], ALU.mult),
                                reads=["OA%d" % hb, rk], writes=[yk], partial=(hb == 1))
                        store_y(0, s, hp, ch, yi, yk)
                    P.barrier()
            if "b" in mixers:
                ar = Arena(wbase, SB_BYTES)
                QT = [ar.tile([S], BF16) for _ in range(2)]
                KT = [ar.tile([S], BF16) for _ in range(2)]
                VG = [ar.tile([32, 128], BF16) for _ in range(2)]
                for bb in range(2):
                    P.op("pool", lambda e, bb=bb: e.memset(VG[bb][:, :, 64:128], 1.0), writes=["VG%d" % bb])

                def load_b(h):
                    bb = h % 2
                    dma("sp", QT[bb][0:96], QB_d[h][:, s * S:(s + 1) * S], [], ["QT%d" % bb], "QT%d" % bb)
                    dma("sp", KT[bb][0:96], KB_d[h][:, s * S:(s + 1) * S], [], ["KT%d" % bb], "KT%d" % bb)
                    dma("sp", VG[bb][:, :, 0:64],
                        VB_d[s * S:(s + 1) * S, h * 64:(h + 1) * 64].rearrange("(kb p) x -> p kb x", p=128),
                        [], ["VG%d" % bb], "VG%d" % bb, partial=True)

                LAG = 3
                work = [(h, ch, kb) for h in range(4) for ch in range(8) for kb in range(32)]
                pend = []
                ots = {}
                load_b(0)
                for wi in range(len(work) + LAG):
                    if wi < len(work):
                        h, ch, kb = work[wi]
                        bb = h % 2
                        if ch == 0 and kb == 0 and h + 1 < 4:
                            pass
                        ps, pk = psum()
                        P.op("pe", lambda e, ps=ps, kb=kb, bb=bb, ch=ch: e.matmul(
                            ps, KT[bb][0:96, kb * 128:(kb + 1) * 128], QT[bb][0:96, ch * 512:(ch + 1) * 512],
                            start=True, stop=True), reads=["KT%d" % bb, "QT%d" % bb], writes=[pk])
                        i = cnt["pe"] % NPE
                        cnt["pe"] += 1
                        P.op("act", lambda e, ps=ps, i=i: e.activation(out=pt[i], in_=ps, func=AF.Exp,
                                                                       scale=96.0 ** -0.5),
                             reads=[pk], writes=["pt%d" % i])
                        pend.append((h, ch, kb, i))
                    if wi >= LAG:
                        h, ch, kb, i = pend.pop(0)
                        bb = h % 2
                        if kb == 0:
                            ots[(h, ch)] = psum_acc()
                        ot, otk = ots[(h, ch)]
                        P.op("pe", lambda e, kb=kb, bb=bb, i=i, ot=ot: e.matmul(
                            ot, VG[bb][:, kb, :], pt[i], start=(kb == 0), stop=(kb == 31)),
                            reads=["VG%d" % bb, "pt%d" % i], writes=[otk])
                        if kb == 31:
                            i2 = cnt["rec"] % 2
                            cnt["rec"] += 1
                            rk = "rec%d" % i2
                            P.op("dve", lambda e, ot=ot, i2=i2: e.reciprocal(rec[i2][0:64], ot[64:128, :]),
                                 reads=[otk], writes=[rk])
                            ysi = ycnt[0] % 2
                            ycnt[0] += 1
                            yk = "yst%d" % ysi
                            P.op("dve", lambda e, ot=ot, i2=i2, ysi=ysi: e.tensor_tensor(
                                yst[ysi][0:64, 0, :], ot[0:64, :], rec[i2][0:64], ALU.mult),
                                reads=[otk, rk], writes=[yk])
                            dma("pool", yT_d[1, h * 64:(h + 1) * 64, s * S + ch * 512:s * S + (ch + 1) * 512],
                                yst[ysi][0:64, 0, :], [yk], [], "st" + yk)
                            if ch == 0 and h + 1 < 4:
                                load_b(h + 1)
                P.barrier()

    def phase_c(l, x_src, x_dst):
        ar = Arena(PBASE, SB_BYTES)
        wg = ar.tile([32, D], BF16)
        wb = ar.tile([8, D], BF16)
        wo = ar.tile([8, D], BF16)
        hT = [ar.tile([8, 512], BF16) for _ in range(2)]
        yT = [ar.tile([8, 512], BF16) for _ in range(2)]
        xt = ar.tile([8, 512], F32)
        sg = [ar.tile([512], F32) for _ in range(2)]
        tm = [ar.tile([512], F32) for _ in range(2)]
        acc = ar.tile([512], F32)
        mg = ar.tile([8, 512], BF16)
        for i in range(4):
            for c in range(8):
                k = "wg%d" % (i * 8 + c)
                dma("pool", wg[:, i * 8 + c, :], w_gate_in[l, i, c * 128:(c + 1) * 128, :], [], [k], k)
        dma("pool", wb, w_br_in[l].rearrange("i (c p) n -> p (i c) n", p=128), [], ["wb"], "wb")
        dma("pool", wo, w_out_in[l].rearrange("(c p) n -> p c n", p=128), [], ["wo"], "wo")
        tiles = [(s, T) for s in range(NS) for T in range(8)]

        def load(idx):
            s, T = tiles[idx]
            b = idx % 2
            t0 = s * S + T * 512
            dma("sp", hT[b], hT_d.rearrange("(c p) n -> p c n", p=128)[:, :, t0:t0 + 512], [], ["hT%d" % b], "hT%d" % b)
            dma("sp", yT[b], yT_d.rearrange("m (c p) n -> p (m c) n", p=128)[:, :, t0:t0 + 512], [], ["yT%d" % b],
                "yT%d" % b)

        load(0)
        k2 = 0
        for idx, (s, T) in enumerate(tiles):
            b = idx % 2
            t0 = s * S + T * 512
            if idx + 1 < len(tiles):
                load(idx + 1)
            dma("sp", xt, x_src.rearrange("(c p) n -> p c n", p=128)[:, :, t0:t0 + 512], [], ["xt"], "xt")
            for oc in range(8):
                for i in range(4):
                    psg, pkg = psum()
                    for c in range(8):
                        P.op("pe", lambda e, c=c, i=i, oc=oc, psg=psg: e.matmul(
                            psg, wg[:, i * 8 + c, oc * 128:(oc + 1) * 128], hT[b][:, c, :],
                            start=(c == 0), stop=(c == 7)), reads=["wg%d" % (i * 8 + c), "hT%d" % b], writes=[pkg])
                    psb_, pkb = psum()
                    for c2 in range(2):
                        P.op("pe", lambda e, c2=c2, i=i, oc=oc, psb_=psb_: e.matmul(
                            psb_, wb[:, i * 2 + c2, oc * 128:(oc + 1) * 128], yT[b][:, i * 2 + c2, :],
                            start=(c2 == 0), stop=(c2 == 1)), reads=["wb", "yT%d" % b], writes=[pkb])
                    j = k2 % 2
                    k2 += 1
                    P.op("act", lambda e, psg=psg, j=j: e.activation(out=sg[j], in_=psg, func=AF.Sigmoid), reads=[pkg],
                         writes=["sg%d" % j])
                    if i == 0:
                        P.op("dve", lambda e, psb_=psb_, j=j: e.tensor_tensor(acc, sg[j], psb_, ALU.mult),
                             reads=["sg%d" % j, pkb], writes=["acc"])
                    else:
                        P.op("dve", lambda e, psb_=psb_, j=j: e.tensor_tensor(tm[j], sg[j], psb_, ALU.mult),
                             reads=["sg%d" % j, pkb], writes=["tm%d" % j])
                        if i < 3:
                            P.op("pool", lambda e, j=j: e.tensor_tensor(acc, acc, tm[j], ALU.add),
                                 reads=["tm%d" % j, "acc"], writes=["acc"])
                        else:
                            P.op("pool", lambda e, j=j, oc=oc: e.tensor_tensor(mg[:, oc, :], acc, tm[j], ALU.add),
                                 reads=["tm%d" % j, "acc"], writes=["mg"], partial=True)
            for oc in range(8):
                ps, pk = psum()
                for c in range(8):
                    P.op("pe", lambda e, c=c, oc=oc, ps=ps: e.matmul(ps, wo[:, c, oc * 128:(oc + 1) * 128], mg[:, c, :],
                                                                     start=(c == 0), stop=(c == 7)),
                         reads=["wo", "mg"], writes=[pk])
                P.op("dve", lambda e, oc=oc, ps=ps: e.tensor_tensor(xt[:, oc, :], xt[:, oc, :], ps, ALU.add),
                     reads=[pk, "xt"], writes=["xt"], partial=True)
            dma("pool", x_dst.rearrange("(c p) n -> p c n", p=128)[:, :, t0:t0 + 512], xt, ["xt"], [], "stxt")
        P.barrier()

    def phase_d(l, x_src, x_dst, final):
        ar = Arena(PBASE, SB_BYTES)
        wg = ar.tile([8, DFF], BF16)
        wu = ar.tile([8, DFF], BF16)
        wd = ar.tile([NFF, D], BF16)
        TW = 256
        xt = [ar.tile([8, TW + 2], F32) for _ in range(2)]
        sq = ar.tile([8, TW + 2], BF16)
        rt = ar.tile([TW + 2], F32)
        rstd = ar.tile([TW + 2], F32)
        h2s = [ar.tile([8, TW + 2], BF16) for _ in range(2)]
        cv = [ar.tile([TW], F32) for _ in range(2)]
        ge = [ar.tile([TW], F32) for _ in range(2)]
        uT = ar.tile([NFF, TW], BF16)
        for c in range(8):
            dma("pool", wg[:, c, :], w_fg_in[l, c * 128:(c + 1) * 128, :], [], ["fg%d" % c], "fg%d" % c)
            dma("pool", wu[:, c, :], w_fu_in[l, c * 128:(c + 1) * 128, :], [], ["fu%d" % c], "fu%d" % c)
        for f in range(NFF):
            dma("pool", wd[:, f, :], w_fd_in[l, f * 128:(f + 1) * 128, :], [], ["fd%d" % (f % 4)], "fd%d" % (f % 4),
                partial=True)
        fdk = ["fd%d" % i for i in range(4)]
        NTI = S // TW
        tiles = [(s, T) for s in range(NS) for T in range(NTI)]
        xs = x_src.rearrange("(c p) n -> p c n", p=128)

        def load(idx):
            s, T = tiles[idx]
            b = idx % 2
            t0 = s * S + T * TW
            lo = 1 if T == 0 else 0
            hi = TW + 1 if T == NTI - 1 else TW + 2
            k = "xt%d" % b
            if lo == 1:
                P.op("pool", lambda e: e.memset(xt[b][:, :, 0:1], 0.0), writes=[k])
            if hi == TW + 1:
                P.op("pool", lambda e: e.memset(xt[b][:, :, TW + 1:TW + 2], 0.0), writes=[k])
            dma("sp", xt[b][:, :, lo:hi], xs[:, :, t0 - 1 + lo:t0 - 1 + hi], [], [k], k)

        def norm_d(idx):
            b_ = idx % 2
            rms_norm(xt[b_], "xt%d" % b_, 8, TW + 2, lambda c: gffn[:, l * 8 + c:l * 8 + c + 1], sq, "sq", rt, "rt",
                     rstd, "rstd", h2s[b_], "h2%d" % b_, float(D))

        load(0)
        norm_d(0)
        k2 = 0
        for idx, (s, T) in enumerate(tiles):
            b = idx % 2
            t0 = s * S + T * TW
            if idx + 1 < len(tiles):
                load(idx + 1)
                norm_d(idx + 1)
            xk = "xt%d" % b
            h2 = h2s[b]
            h2k = "h2%d" % b
            for f in range(NFF):
                psg, pkg = psum()
                for c in range(8):
                    P.op("pe", lambda e, c=c, f=f, psg=psg: e.matmul(psg[:, 0:TW + 2], wg[:, c, f * 128:(f + 1) * 128],
                                                                     h2[:, c, :], start=(c == 0), stop=(c == 7)),
                         reads=["fg%d" % c, h2k], writes=[pkg])
                psu, pku = psum()
                for c in range(8):
                    P.op("pe", lambda e, c=c, f=f, psu=psu: e.matmul(psu[:, 0:TW], wu[:, c, f * 128:(f + 1) * 128],
                                                                     h2[:, c, 1:TW + 1], start=(c == 0), stop=(c == 7)),
                         reads=["fu%d" % c, h2k], writes=[pku])
                j = k2 % 2
                k2 += 1
                cwb = (l * NFF + f) * 3
                P.op("act", lambda e, psg=psg, j=j, cwb=cwb, f=f: e.activation(
                    out=cv[j], in_=psg[:, 1:TW + 1], func=AF.Identity, scale=cw[:, cwb + 1:cwb + 2],
                    bias=cb[:, l * NFF + f:l * NFF + f + 1]), reads=[pkg], writes=["cv%d" % j])
                P.op("dve", lambda e, psg=psg, j=j, cwb=cwb: e.scalar_tensor_tensor(
                    cv[j], psg[:, 0:TW], cw[:, cwb:cwb + 1], cv[j], ALU.mult, ALU.add), reads=[pkg, "cv%d" % j],
                    writes=["cv%d" % j])
                P.op("dve", lambda e, psg=psg, j=j, cwb=cwb: e.scalar_tensor_tensor(
                    cv[j], psg[:, 2:TW + 2], cw[:, cwb + 2:cwb + 3], cv[j], ALU.mult, ALU.add), reads=[pkg, "cv%d" % j],
                    writes=["cv%d" % j])
                P.op("act", lambda e, j=j: e.activation(out=ge[j], in_=cv[j], func=AF.Gelu_apprx_tanh),
                     reads=["cv%d" % j], writes=["ge%d" % j])
                P.op("dve", lambda e, j=j, f=f, psu=psu: e.tensor_tensor(uT[:, f, :], ge[j], psu[:, 0:TW], ALU.mult),
                     reads=["ge%d" % j, pku], writes=["uT"], partial=True)
            xo = xt[b][:, :, 1:TW + 1]
            for oc in range(8):
                ps, pk = psum()
                for f in range(NFF):
                    P.op("pe", lambda e, f=f, oc=oc, ps=ps: e.matmul(ps[:, 0:TW], wd[:, f, oc * 128:(oc + 1) * 128],
                                                                     uT[:, f, :], start=(f == 0), stop=(f == NFF - 1)),
                         reads=fdk + ["uT"], writes=[pk])
                P.op("dve", lambda e, oc=oc, ps=ps: e.tensor_tensor(xt[b][:, oc, 1:TW + 1], xt[b][:, oc, 1:TW + 1],
                                                                    ps[:, 0:TW], ALU.add),
                     reads=[pk, xk], writes=[xk], partial=True)
            if not final:
                dma("pool", x_dst.rearrange("(c p) n -> p c n", p=128)[:, :, t0:t0 + TW], xo, [xk], [], "st" + xk)
            else:
                sqf = sq[:, :, 0:TW]
                P.op("act", lambda e: e.activation(out=sqf, in_=xo, func=AF.Square), reads=[xk, "sq"], writes=["sq"])
                ps, pk = psum()
                for c in range(8):
                    P.op("pe", lambda e, c=c, ps=ps: e.matmul(ps[:, 0:TW], ones_bf, sq[:, c, 0:TW], start=(c == 0),
                                                              stop=(c == 7)), reads=["sq", "ones"], writes=[pk])
                P.op("act", lambda e, ps=ps: e.activation(out=rt[:, 0:TW], in_=ps[:, 0:TW], func=AF.Sqrt,
                                                          scale=1.0 / D, bias=epsc), reads=[pk, "epsc"], writes=["rt"])
                P.op("dve", lambda e: e.reciprocal(rstd[:, 0:TW], rt[:, 0:TW]), reads=["rt"], writes=["rstd"])
                for c in range(8):
                    P.op("dve", lambda e, c=c: e.scalar_tensor_tensor(xt[b][:, c, 1:TW + 1], xt[b][:, c, 1:TW + 1],
                                                                      gfin[:, c:c + 1], rstd[:, 0:TW], ALU.mult,
                                                                      ALU.mult),
                         reads=[xk, "rstd"], writes=[xk], partial=True)
                dma("pool", outT.rearrange("(c p) n -> p c n", p=128)[:, :, t0:t0 + TW], xo, [xk], [], "st" + xk)
        P.barrier()

    setup()
    x_cur = xT_in
    for l in range(L):
        if "A" in phases:
            phase_a(l, x_cur)
        if "B" in phases:
            phase_b(l)
        if "C" in phases:
            x_c = xa_d if l == 0 else x_cur
            phase_c(l, x_cur, x_c)
        else:
            x_c = x_cur
        if "D" in phases:
            x_n = xb_d if x_c is xa_d else xa_d
            phase_d(l, x_c, x_n, final=(l == L - 1))
            x_cur = x_n
    P.barrier()
    P.emit(st)
    st.close()
    return nc


_NC_CACHE = {}


def _host_inputs(inp, core, NS, L):
    f = np.float32
    x = np.asarray(inp["x"], f)
    xs = x[core * NS:(core + 1) * NS]
    xT = np.ascontiguousarray(xs.reshape(NS * S, D).T)

    def cols(v, n):
        v = np.asarray(v, f).reshape(-1, n, 128)
        return np.ascontiguousarray(v.transpose(2, 0, 1).reshape(128, -1))

    m = dict(
        xT=xT,
        t5=np.asarray(inp["t5_table"], f),
        gmix=cols(inp["norm_mix_g"][:L], 8),
        gffn=cols(inp["norm_ffn_g"][:L], 8),
        gfin=cols(np.asarray(inp["final_g"])[None], 8),
        gq=cols(inp["q_norm_g"][:L], 2),
        gkv=cols(inp["kv_norm_g"][:L], 1),
        cb=cols(inp["conv_b"][:L], NFF),
        snk=np.ascontiguousarray(np.broadcast_to(np.asarray(inp["sink_logit"][:L], f).reshape(1, -1), (128, L * 4))),
        nab=np.ascontiguousarray(np.asarray(inp["na_bias"][:L], f).transpose(0, 3, 1, 2).reshape(L, 31, 60)),
        w_in=np.asarray(inp["w_in"][:L], f), w_uq=np.asarray(inp["w_uq"][:L], f),
        w_ukv=np.asarray(inp["w_ukv"][:L], f), w_gate=np.asarray(inp["w_gate"][:L], f),
        w_branch=np.asarray(inp["w_branch"][:L], f), w_out=np.asarray(inp["w_out"][:L], f),
        w_ffn_gate=np.asarray(inp["w_ffn_gate"][:L], f), w_ffn_up=np.asarray(inp["w_ffn_up"][:L], f),
        w_ffn_down=np.asarray(inp["w_ffn_down"][:L], f),
    )
    cwv = np.asarray(inp["conv_w"][:L], f)
    cwv = cwv.reshape(L, 3, NFF, 128).transpose(3, 0, 2, 1)
    m["cw"] = np.ascontiguousarray(cwv.reshape(128, -1))
    m.update(_consts())
    return m


def kernel(**inputs):
    NS, L = 2, DEPTH
    key = (NS, L)
    if key not in _NC_CACHE:
        _NC_CACHE[key] = build_nc(NS, L)
    nc = _NC_CACHE[key]
    n = 8
    in_maps = [_host_inputs(inputs, c, NS, L) for c in range(n)]
    res = run_bass_kernel_spmd(nc, in_maps, core_ids=list(range(n)))
    outs = []
    for c in range(n):
        oT = np.asarray(res.results[c]["outT"], np.float32)
        outs.append(oT.T.reshape(NS, S, D))
    return np.ascontiguousarray(np.concatenate(outs, 0))
```

```python
import math
from contextlib import ExitStack

import numpy as np
import concourse.bass as bass
import concourse.mybir as mybir
from concourse.bass_utils import run_bass_kernel_spmd

F32 = mybir.dt.float32
BF16 = mybir.dt.bfloat16
U8 = mybir.dt.uint8
AF = mybir.ActivationFunctionType
ALU = mybir.AluOpType

S = 4096
D = 1024
NCH = 8
DEPTH = 2
INC = 4000
DFF = 2816
NFF = 22
EPS = 1e-6
SEM_EPOCH = 24000
SB_BYTES = 189 * 1024

RT_D = 511
RT_A = 383
RT = RT_D + 3 * RT_A
DIL = (1, 4, 16)


class _Op:
    __slots__ = ("idx", "eng", "fn", "lane", "deps", "inc", "count", "dma_key")

    def __init__(self, idx, eng, fn, lane, dma_key):
        self.idx = idx
        self.eng = eng
        self.fn = fn
        self.lane = lane
        self.deps = {}
        self.inc = False
        self.count = None
        self.dma_key = dma_key


class _Rec:
    def __init__(self):
        self.call = None

    def __getattr__(self, name):
        def f(*a, **k):
            self.call = (name, a, k)
            return None
        return f


class Prog:
    ENGS = ("pe", "act", "dve", "pool", "sp")

    def __init__(self, nc):
        self.nc = nc
        self.ops = []
        self.by_eng = {e: [] for e in self.ENGS}
        self.res = {}
        self.dma_count = {}
        self.pool_map = {}
        self.last = {}

    def _state(self, key):
        st = self.res.get(key)
        if st is None:
            st = [{}, {}]
            self.res[key] = st
        return st

    def op(self, eng, fn, reads=(), writes=(), sem=None, partial=False):
        dma = sem is not None
        if dma:
            key = self.pool_map.get(sem)
            if key is None:
                key = len(self.pool_map)
                self.pool_map[sem] = key
            lane = ("dma", key)
        else:
            key = None
            lane = eng
        if fn is not None:
            rec = _Rec()
            fn(rec)
            fn = rec.call
        o = _Op(len(self.ops), eng, fn, lane, key)
        if dma:
            self.dma_count[key] = self.dma_count.get(key, 0) + 1
            o.count = self.dma_count[key]

        def add_dep(d, raw=False):
            if d.lane == o.lane and not dma:
                if not raw or eng == "pe":
                    return
            cur = o.deps.get(d.lane)
            if cur is None or d.idx > cur.idx:
                o.deps[d.lane] = d

        for r in reads:
            w, rd = self._state(r)
            for d in w.values():
                add_dep(d, raw=True)
        for wkey in writes:
            w, rd = self._state(wkey)
            if not (partial and not rd):
                for d in w.values():
                    add_dep(d)
            for d in rd.values():
                add_dep(d)
        for r in reads:
            w, rd = self._state(r)
            rd[o.lane] = o
        for wkey in writes:
            st = self._state(wkey)
            if partial and not st[1]:
                st[0][o.lane] = o
            else:
                st[0] = {o.lane: o}
                st[1] = {}
        for d in o.deps.values():
            if d.dma_key is None:
                d.inc = True
        self.ops.append(o)
        self.by_eng[eng].append(o)
        self.last[lane] = o
        return o

    def barrier(self, engs=None):
        lasts = list(self.last.values())
        for e in (engs or self.ENGS):
            o = _Op(len(self.ops), e, None, e, None)
            for d in lasts:
                if d.lane == e:
                    continue
                o.deps[d.lane] = d
                if d.dma_key is None:
                    d.inc = True
            self.ops.append(o)
            self.by_eng[e].append(o)
        self.res = {}
        self.pool_map = {}
        self.last = {}

    def emit(self, stack):
        nc = self.nc
        sems = {}

        def get_sem(name):
            s = sems.get(name)
            if s is None:
                s = stack.enter_context(nc.semaphore("s%d" % len(sems)))
                sems[name] = s
            return s

        for e in self.ENGS:
            c = 0
            for o in self.by_eng[e]:
                if o.dma_key is None and o.inc:
                    c += 1
                    o.count = c

        def sem_for(o):
            if o.dma_key is None:
                ep, v = divmod(o.count - 1, SEM_EPOCH)
                return get_sem(("c", o.lane, ep)), v + 1
            per = SEM_EPOCH // 16
            ep, v = divmod(o.count - 1, per)
            return get_sem(("d", o.dma_key, ep)), (v + 1) * 16

        for o in self.ops:
            if o.dma_key is not None or o.inc:
                sem_for(o)
        block = stack.enter_context(nc.Block())
        deco = {"pe": block.tensor, "act": block.scalar, "dve": block.vector,
                "pool": block.gpsimd, "sp": block.sync}
        for e in self.ENGS:
            ops = self.by_eng[e]
            if not ops:
                continue

            def body(engh, ops=ops):
                waited = {}
                for o in ops:
                    for d in o.deps.values():
                        s, v = sem_for(d)
                        k = id(s)
                        if waited.get(k, 0) >= v:
                            continue
                        waited[k] = v
                        engh.wait_ge(s, v)
                    if o.fn is None:
                        continue
                    name, a, k = o.fn
                    ins = getattr(engh, name)(*a, **k)
                    if o.dma_key is not None:
                        s, v = sem_for(o)
                        ins.then_inc(s, 16)
                    elif o.inc:
                        s, v = sem_for(o)
                        ins.then_inc(s, 1)

            deco[e](body)
        self.n_sems = len(sems)


def _t5_bucket(rel):
    half = 16
    exact = 8
    n = np.abs(rel)
    large = exact + (np.log(np.maximum(n, 1) / exact) / math.log(1024 / exact)
                     * (half - exact)).astype(np.int32)
    large = np.minimum(large, half - 1)
    return (np.where(rel > 0, half, 0) + np.where(n < exact, n, large)).astype(np.int32)


def _c_rs(qr):
    return min(max(qr - 4, 0), 56)


def _c_sig(i, j):
    sig = []
    for krl in range(2):
        for qrl in range(2):
            kr = 2 * j + krl
            qr = 2 * i + qrl
            rs = _c_rs(qr)
            if rs <= kr < rs + 8:
                sig.append(kr - qr + 7)
            else:
                sig.append(None)
    return tuple(sig)


def _c_tiles():
    sigs = []
    table = []
    for i in range(32):
        row = []
        for j in range(32):
            sg = _c_sig(i, j)
            if all(v is None for v in sg):
                continue
            if sg not in sigs:
                sigs.append(sg)
            row.append((j, sigs.index(sg)))
        table.append(row)
    return sigs, table


C_SIGS, C_TABLE = _c_tiles()
NCT = len(C_SIGS)


def _consts():
    ohv = np.zeros((32, RT), np.float32)
    msk = np.zeros((16, RT), np.float32)
    u = np.arange(RT_D)
    rel = u - 255
    val = np.abs(rel) <= 128
    b = _t5_bucket(rel)
    ohv[b[val], u[val]] = 1.0
    msk[:, u[val]] = 1.0
    for g, dil in enumerate(DIL):
        off = RT_D + g * RT_A
        u = np.arange(RT_A)
        j = u - 191
        val = np.abs(j) <= 64
        b = _t5_bucket(j * dil)
        ohv[b[val], off + u[val]] = 1.0
        msk[:, off + u[val]] = 1.0
    jf = np.zeros((128, 128), np.float32)
    jf[np.arange(128), 127 - np.arange(128)] = 1.0
    ohc = np.zeros((31, 64, 64), np.float32)
    mc = np.zeros((64, 64), np.float32)
    for qc in range(64):
        cs = min(max(qc - 8, 0), 48)
        for kc in range(cs, cs + 16):
            ohc[kc - qc + 15, kc, qc] = 1.0
            mc[kc, qc] = 1.0
    maskc2 = np.concatenate([mc, mc], 0)
    inv = 10000.0 ** (-np.arange(16, dtype=np.float32) / 16)
    ang = np.arange(S, dtype=np.float32)[None, :] * inv[:, None]
    cos2 = np.concatenate([np.cos(ang), np.cos(ang)], 0).astype(np.float32)
    sin2 = np.concatenate([np.sin(ang), np.sin(ang)], 0).astype(np.float32)
    return dict(c_ohv=ohv, c_msk=msk, c_jf=jf, c_ohc=ohc.reshape(31, 4096),
                c_maskc=maskc2, c_cos=cos2, c_sin=sin2)


def build_nc(NS=2, L=DEPTH, dbg=False, phases="ABCD", mixers="abcd"):
    nc = bass.Bass("TRN2", target_bir_lowering=False)
    NT = NS * S

    def din(name, shape, dt=F32):
        return nc.dram_tensor(name, list(shape), dt, kind="ExternalInput").ap()

    def dscr(name, shape, dt=BF16):
        kind = "ExternalOutput" if dbg else "Internal"
        return nc.dram_tensor(name, list(shape), dt, kind=kind).ap()

    xT_in = din("xT", [D, NT])
    t5_in = din("t5", [32, 16])
    gmix_in = din("gmix", [128, L * 8])
    gffn_in = din("gffn", [128, L * 8])
    gfin_in = din("gfin", [128, 8])
    gq_in = din("gq", [128, L * 2])
    gkv_in = din("gkv", [128, L])
    cw_in = din("cw", [128, L * NFF * 3])
    cb_in = din("cb", [128, L * NFF])
    snk_in = din("snk", [128, L * 4])
    nab_in = din("nab", [L, 31, 60])
    w_in_in = din("w_in", [L, D, INC])
    w_uq_in = din("w_uq", [L, 256, 384])
    w_ukv_in = din("w_ukv", [L, 128, 512])
    w_gate_in = din("w_gate", [L, 4, D, D])
    w_br_in = din("w_branch", [L, 4, 256, D])
    w_out_in = din("w_out", [L, D, D])
    w_fg_in = din("w_ffn_gate", [L, D, DFF])
    w_fu_in = din("w_ffn_up", [L, D, DFF])
    w_fd_in = din("w_ffn_down", [L, DFF, D])
    c_ohv = din("c_ohv", [32, RT])
    c_msk = din("c_msk", [16, RT])
    c_jf = din("c_jf", [128, 128])
    c_ohc = din("c_ohc", [31, 4096])
    c_maskc = din("c_maskc", [128, 64])
    c_cos = din("c_cos", [32, S])
    c_sin = din("c_sin", [32, S])

    outT = nc.dram_tensor("outT", [D, NT], F32, kind="ExternalOutput").ap()

    evec_d = dscr("evec_d", [16, RT], F32)
    mcol_d = dscr("mcol_d", [60, 64, 64], F32)
    hT_d = dscr("hT_d", [D, NT])
    QA_d = dscr("QA_d", [3, 2, 128, NT])
    KA_d = dscr("KA_d", [3, 2, 128, NT])
    VA_d = dscr("VA_d", [NT, 768])
    QB_d = dscr("QB_d", [4, 96, NT])
    KB_d = dscr("KB_d", [4, 96, NT])
    VB_d = dscr("VB_d", [NT, 256])
    QC_d = dscr("QC_d", [2, 128, NT])
    KC_d = dscr("KC_d", [2, 128, NT])
    VC_d = dscr("VC_d", [NT, 256])
    QD_d = dscr("QD_d", [2, 128, NT])
    KD_d = dscr("KD_d", [128, NT])
    VD_d = dscr("VD_d", [NT, 128])
    yT_d = dscr("yT_d", [4, 256, NT])
    xa_d = dscr("xa_d", [D, NT], F32)
    xb_d = dscr("xb_d", [D, NT], F32)

    st = ExitStack()
    P = Prog(nc)
    big = st.enter_context(nc.sbuf_tensor("big", [128, SB_BYTES], U8))
    psb = [st.enter_context(nc.psum_tensor("ps%d" % i, [128, 512], F32))[:, :] for i in range(8)]

    class Arena:
        def __init__(self, base, limit):
            self.off = base
            self.limit = limit

        def tile(self, shape, dt):
            n = 1
            for v in shape:
                n *= v
            nb = n * (4 if dt == F32 else 2)
            nb_al = (nb + 63) // 64 * 64
            assert self.off + nb_al <= self.limit, ("SBUF arena overflow", self.off, nb_al, self.limit)
            ap = big[:, self.off:self.off + nb].bitcast(dt)
            self.off += nb_al
            if len(shape) == 2:
                ap = ap.rearrange("p (a b) -> p a b", a=shape[0])
            elif len(shape) == 3:
                ap = ap.rearrange("p (a b c) -> p a b c", a=shape[0], b=shape[1])
            return ap

    pa = Arena(0, 16 * 1024)
    gmix = pa.tile([L * 8], F32)
    gffn = pa.tile([L * 8], F32)
    gfin = pa.tile([8], F32)
    gq = pa.tile([L * 2], F32)
    gkv = pa.tile([L], F32)
    cw = pa.tile([L * NFF * 3], F32)
    cb = pa.tile([L * NFF], F32)
    snk = pa.tile([L * 4], F32)
    epsc = pa.tile([1], F32)
    ones_bf = pa.tile([128], BF16)
    jf_bf = pa.tile([128], BF16)
    EB_A = pa.tile([24, 128], BF16)
    EB_D = pa.tile([12, 128], BF16)
    PBASE = pa.off

    pscnt = [0, 0]

    psn_pool = [6]

    def psum():
        i = pscnt[0] % psn_pool[0]
        pscnt[0] += 1
        return psb[i], "ps%d" % i

    def psum_acc():
        i = 6 + pscnt[1] % 2
        pscnt[1] += 1
        return psb[i], "ps%d" % i

    def dma(eng, out, in_, reads, writes, sem, partial=False):
        P.op(eng, lambda e: e.dma_start(out=out, in_=in_), reads=reads, writes=writes,
             sem=sem, partial=partial)

    def setup():
        ar = Arena(PBASE, SB_BYTES)
        for i, (t, src) in enumerate([(gmix, gmix_in), (gffn, gffn_in), (gfin, gfin_in), (gq, gq_in),
                                      (gkv, gkv_in), (cw, cw_in), (cb, cb_in), (snk, snk_in)]):
            dma("sp", t, src, [], ["sv%d" % i], "sv%d" % i)
        dma("pool", jf_bf, c_jf, [], ["jf"], "jf")
        P.op("dve", lambda e: e.memset(ones_bf, 1.0), writes=["ones"])
        P.op("dve", lambda e: e.memset(epsc, EPS), writes=["epsc"])
        P.op("act", lambda e: e.activation(out=snk, in_=snk, func=AF.Exp), reads=["sv7"], writes=["sv7"])
        t5f = ar.tile([16], F32)
        t5hi = ar.tile([16], BF16)
        t5hf = ar.tile([16], F32)
        t5lo = ar.tile([16], BF16)
        ohv = ar.tile([RT], BF16)
        mskt = ar.tile([RT], F32)
        evec = ar.tile([RT], F32)
        hk = ar.tile([36, 128], BF16)
        dma("sp", t5f[0:32], t5_in, [], ["t5f"], "t5f")
        dma("pool", ohv[0:32], c_ohv, [], ["ohv"], "ohv")
        dma("sp", mskt[0:16], c_msk, [], ["mskt"], "mskt")
        P.op("dve", lambda e: e.tensor_copy(t5hi[0:32], t5f[0:32]), reads=["t5f"], writes=["t5hi"])
        P.op("dve", lambda e: e.tensor_copy(t5hf[0:32], t5hi[0:32]), reads=["t5hi"], writes=["t5hf"])
        P.op("dve", lambda e: e.tensor_tensor(t5lo[0:32], t5f[0:32], t5hf[0:32], ALU.subtract),
             reads=["t5f", "t5hf"], writes=["t5lo"])
        ncol = 415
        for k in range(4):
            ps, pk = psum()
            c0 = k * ncol
            P.op("pe", lambda e, ps=ps, c0=c0: e.matmul(ps[0:16, 0:ncol], t5hi[0:32, :], ohv[0:32, c0:c0 + ncol],
                                                        start=True, stop=False),
                 reads=["t5hi", "ohv"], writes=[pk])
            P.op("pe", lambda e, ps=ps, c0=c0: e.matmul(ps[0:16, 0:ncol], t5lo[0:32, :], ohv[0:32, c0:c0 + ncol],
                                                        start=False, stop=True),
                 reads=["t5lo", "ohv"], writes=[pk])
            P.op("act", lambda e, ps=ps, c0=c0: e.activation(out=evec[0:16, c0:c0 + ncol], in_=ps[0:16, 0:ncol],
                                                             func=AF.Exp),
                 reads=[pk], writes=["evec"], partial=True)
        P.op("dve", lambda e: e.tensor_tensor(evec[0:16], evec[0:16], mskt[0:16], ALU.mult),
             reads=["evec", "mskt"], writes=["evec"])
        dma("sp", evec_d, evec[0:16], ["evec"], [], "evst")
        P.barrier()
        tiles = []
        for h in range(4):
            for d in (-1, 0, 1):
                tiles.append((12 + h, 0 + d * 128 + 128))
        for g in range(3):
            for h in range(4):
                for ab in range(2):
                    tiles.append((4 * g + h, RT_D + g * RT_A + (0 if ab == 0 else 128)))
        for t, (row, u0) in enumerate(tiles):
            src = bass.AP(evec_d.tensor, row * RT + u0, [[1, 128], [1, 128]])
            dma("pool", hk[:, t, :], src, [], ["hk%d" % t], "hk%d" % t)
        for b in range(9):
            ps, pk = psum()
            for q in range(4):
                t = b * 4 + q
                P.op("pe", lambda e, ps=ps, t=t, q=q: e.matmul(ps[:, q * 128:(q + 1) * 128], hk[:, t, :], jf_bf,
                                                               start=True, stop=True),
                     reads=["hk%d" % t, "jf"], writes=[pk])
            if b < 3:
                dst = EB_D[:, b * 4:(b + 1) * 4, :]
            else:
                dst = EB_A[:, (b - 3) * 4:(b - 2) * 4, :]
            P.op("dve", lambda e, ps=ps, dst=dst: e.tensor_copy(dst, ps.rearrange("p (a b) -> p a b", a=4)),
                 reads=[pk], writes=["EB"], partial=True)
        P.barrier()

    def norm_stages(xt, xkey, nch, n, gcol, sq, sqkey, rt, rtkey, rstd, rstdkey, hT, hkey, feat):
        box = {}

        def s1():
            P.op("act", lambda e: e.activation(out=sq, in_=xt, func=AF.Square), reads=[xkey], writes=[sqkey])

        def s2():
            ps, pk = psum()
            box["ps"] = (ps, pk)
            for c in range(nch):
                P.op("pe", lambda e, c=c: e.matmul(ps[:, 0:n], ones_bf, sq[:, c, :], start=(c == 0), stop=(c == nch - 1)),
                     reads=[sqkey, "ones"], writes=[pk])

        def s3():
            ps, pk = box["ps"]
            P.op("act", lambda e: e.activation(out=rt, in_=ps[:, 0:n], func=AF.Ln, scale=1.0 / feat, bias=epsc),
                 reads=[pk, "epsc"], writes=[rtkey])
            P.op("act", lambda e: e.activation(out=rstd, in_=rt, func=AF.Exp, scale=-0.5), reads=[rtkey],
                 writes=[rstdkey])
            for c in range(nch):
                P.op("dve", lambda e, c=c: e.scalar_tensor_tensor(hT[:, c, :], xt[:, c, :], gcol(c), rstd,
                                                                  ALU.mult, ALU.mult),
                     reads=[xkey, rstdkey], writes=[hkey], partial=True)

        return s1, s2, s3

    def rms_norm(*a):
        for st_ in norm_stages(*a):
            st_()

    def phase_a(l, x_src):
        psn_pool[0] = 8
        ar = Arena(PBASE, SB_BYTES)
        w_in = ar.tile([8, INC], BF16)
        wkrot = ar.tile([8, 96], BF16)
        wuq = ar.tile([2, 384], BF16)
        wuqr = ar.tile([2, 4, 96], BF16)
        wukv = ar.tile([512], BF16)
        xt = [ar.tile([8, 512], F32) for _ in range(2)]
        sq = ar.tile([8, 512], BF16)
        rt = ar.tile([512], F32)
        rstd = ar.tile([512], F32)
        hT = [ar.tile([8, 512], BF16) for _ in range(2)]
        cs = [ar.tile([2, 512], F32) for _ in range(1)]
        NSTG = 4
        stg = [ar.tile([512], BF16) for _ in range(NSTG)]
        vst = [ar.tile([4, 1152], BF16) for _ in range(1)]
        cq = ar.tile([2, 512], F32)
        cqsq = ar.tile([2, 512], BF16)
        cqn = ar.tile([2, 512], BF16)
        ckv = ar.tile([1, 512], F32)
        ckvsq = ar.tile([1, 512], BF16)
        ckvn = ar.tile([1, 512], BF16)
        rt2 = ar.tile([512], F32)
        rs2 = ar.tile([512], F32)
        rt3 = ar.tile([512], F32)
        rs3 = ar.tile([512], F32)
        kr = ar.tile([512], F32)
        t1 = ar.tile([512], F32)
        t2 = ar.tile([512], F32)
        qst = [ar.tile([512], BF16) for _ in range(2)]
        kst = [ar.tile([512], BF16) for _ in range(2)]
        vbst = [ar.tile([4, 256], BF16) for _ in range(1)]

        for c in range(8):
            dma("pool", w_in[:, c, :], w_in_in[l, c * 128:(c + 1) * 128, :], [], ["w_in%d" % c], "w_in%d" % c)
        dma("pool", wuq, w_uq_in[l].rearrange("(c p) n -> p c n", p=128), [], ["wuq"], "wuq")
        dma("pool", wukv, w_ukv_in[l], [], ["wukv"], "wukv")
        allw = ["w_in%d" % c for c in range(8)]
        P.op("dve", lambda e: e.memset(wkrot, 0.0), writes=["wkrot"])
        P.op("dve", lambda e: e.tensor_scalar(wkrot[:, :, 64:80], w_in[:, :, 2704:2720], -1.0, 0.0, ALU.mult, ALU.add),
             reads=allw, writes=["wkrot"])
        P.op("dve", lambda e: e.tensor_copy(wkrot[:, :, 80:96], w_in[:, :, 2688:2704]), reads=allw, writes=["wkrot"])
        P.op("dve", lambda e: e.memset(wuqr, 0.0), writes=["wuqr"])
        for h in range(4):
            P.op("dve", lambda e, h=h: e.tensor_scalar(wuqr[:, :, h, 64:80], wuq[:, :, h * 96 + 80:h * 96 + 96],
                                                       -1.0, 0.0, ALU.mult, ALU.add), reads=["wuq"], writes=["wuqr"])
            P.op("dve", lambda e, h=h: e.tensor_copy(wuqr[:, :, h, 80:96], wuq[:, :, h * 96 + 64:h * 96 + 80]),
                 reads=["wuq"], writes=["wuqr"])

        tiles = [(s, T) for s in range(NS) for T in range(8)]

        def load(idx):
            s, T = tiles[idx]
            b = idx % 2
            t0 = s * S + T * 512
            dma("sp", xt[b], x_src.rearrange("(c p) n -> p c n", p=128)[:, :, t0:t0 + 512], [], ["xt%d" % b],
                "xt%d" % b)

        evac_rr = [0]

        def evac(dst, src, reads, writes, partial=False):
            evac_rr[0] += 1
            if evac_rr[0] % 2:
                P.op("act", lambda e: e.copy(dst, src), reads=reads, writes=writes, partial=partial)
            else:
                P.op("dve", lambda e: e.tensor_copy(dst, src), reads=reads, writes=writes, partial=partial)

        stg_rr = [0]

        def norm_a(idx):
            s_, T_ = tiles[idx]
            b_ = idx % 2
            t0_ = s_ * S + T_ * 512
            rms_norm(xt[b_], "xt%d" % b_, 8, 512, lambda c: gmix[:, l * 8 + c:l * 8 + c + 1], sq, "sq", rt, "rt",
                     rstd, "rstd", hT[b_], "hT%d" % b_, float(D))
            dma("pool", hT_d.rearrange("(c p) n -> p c n", p=128)[:, :, t0_:t0_ + 512], hT[b_], ["hT%d" % b_], [],
                "sthT%d" % b_)

        load(0)
        if len(tiles) > 1:
            load(1)
        norm_a(0)
        for idx, (s, T) in enumerate(tiles):
            b = idx % 2
            t0 = s * S + T * 512
            if idx + 1 < len(tiles):
                norm_a(idx + 1)
            if idx + 2 < len(tiles):
                load(idx + 2)
            dma("sp", cs[0][64:96, 0, :], c_cos[:, T * 512:(T + 1) * 512], [], ["cs0"], "cs0", partial=True)
            dma("sp", cs[0][64:96, 1, :], c_sin[:, T * 512:(T + 1) * 512], [], ["cs0"], "cs0", partial=True)
            xk, hk_ = "xt%d" % b, "hT%d" % b

            def fm_chunk(col, M, w=None, wkey=None):
                ps, pk = psum()
                for c in range(8):
                    if w is None:
                        lhsT = w_in[:, c, col:col + M]
                        rk = "w_in%d" % c
                    else:
                        lhsT = w[:, c, col:col + M]
                        rk = wkey
                    P.op("pe", lambda e, c=c, lhsT=lhsT: e.matmul(ps[0:M, :], lhsT, hT[b][:, c, :],
                                                                  start=(c == 0), stop=(c == 7)),
                         reads=[rk, hk_], writes=[pk])
                return ps, pk

            def out_chunk(ps, pk, dst_dram, dil=1):
                i = stg_rr[0] % NSTG
                stg_rr[0] += 1
                sk = "stg%d" % i
                if dil == 1:
                    evac(stg[i], ps, [pk], [sk])
                    dma("pool", dst_dram[:, t0:t0 + 512], stg[i], [sk], [], "st" + sk)
                else:
                    J = 512 // dil
                    evac(stg[i].rearrange("p (r j) -> p j r", r=dil), ps.rearrange("p (j r) -> p j r", r=dil),
                         [pk], [sk])
                    Lg = S // dil
                    dst = dst_dram[:, s * S:(s + 1) * S].rearrange("p (r j) -> p r j", r=dil)[:, :, T * J:(T + 1) * J]
                    dma("pool", dst, stg[i].rearrange("p (r j) -> p r j", r=dil), [sk], [], "st" + sk)

            for c2 in range(2):
                ps, pk = fm_chunk(2304 + c2 * 128, 128)
                evac(cq[:, c2, :], ps, [pk], ["cq"], partial=True)
            ps, pk = fm_chunk(2560, 128)
            evac(ckv[:, 0, :], ps, [pk], ["ckv"])
            nq = norm_stages(cq, "cq", 2, 512, lambda c: gq[:, l * 2 + c:l * 2 + c + 1], cqsq, "cqsq", rt2, "rt2",
                             rs2, "rs2", cqn, "cqn", 256.0)
            nkv = norm_stages(ckv, "ckv", 1, 512, lambda c: gkv[:, l:l + 1], ckvsq, "ckvsq", rt3, "rt3", rs3, "rs3",
                              ckvn, "ckvn", 128.0)
            nq[0]()
            nkv[0]()
            psk, pkk = fm_chunk(2624, 96)
            psr, pkr = fm_chunk(0, 96, w=wkrot, wkey="wkrot")
            csk = "cs0"
            P.op("dve", lambda e: e.tensor_tensor(t1[64:96], psk[64:96, :], cs[0][64:96, 0, :], ALU.mult),
                 reads=[pkk, csk], writes=["t1"])
            P.op("dve", lambda e: e.tensor_tensor(t2[64:96], psr[64:96, :], cs[0][64:96, 1, :], ALU.mult),
                 reads=[pkr, csk], writes=["t2"])
            P.op("dve", lambda e: e.tensor_tensor(kr[64:96], t1[64:96], t2[64:96], ALU.add),
                 reads=["t1", "t2"], writes=["kr"])
            for g in range(3):
                for hp in range(1):
                    ps, pk = fm_chunk((g * 4 + hp * 2) * 64, 128)
                    out_chunk(ps, pk, QA_d[g, hp], DIL[g])
                    ps, pk = fm_chunk(768 + (g * 4 + hp * 2) * 64, 128)
                    out_chunk(ps, pk, KA_d[g, hp], DIL[g])
            nq[1]()
            nq[2]()
            nkv[1]()
            nkv[2]()
            for g in range(3):
                for hp in range(1, 2):
                    ps, pk = fm_chunk((g * 4 + hp * 2) * 64, 128)
                    out_chunk(ps, pk, QA_d[g, hp], DIL[g])
                    ps, pk = fm_chunk(768 + (g * 4 + hp * 2) * 64, 128)
                    out_chunk(ps, pk, KA_d[g, hp], DIL[g])
            for h in range(4):
                hb = h % 2
                psq, pkq = psum()
                psq2, pkq2 = psum()
                for c2 in range(2):
                    P.op("pe", lambda e, c2=c2: e.matmul(psq[0:96, :], wuq[:, c2, h * 96:(h + 1) * 96], cqn[:, c2, :],
                                                         start=(c2 == 0), stop=(c2 == 1)),
                         reads=["wuq", "cqn"], writes=[pkq])
                for c2 in range(2):
                    P.op("pe", lambda e, c2=c2: e.matmul(psq2[0:96, :], wuqr[:, c2, h, :], cqn[:, c2, :],
                                                         start=(c2 == 0), stop=(c2 == 1)),
                         reads=["wuqr", "cqn"], writes=[pkq2])
                qk = "qst%d" % hb
                P.op("dve", lambda e: e.tensor_copy(qst[hb][0:64], psq[0:64, :]), reads=[pkq], writes=[qk])
                P.op("dve", lambda e: e.tensor_tensor(t1[64:96], psq[64:96, :], cs[0][64:96, 0, :], ALU.mult),
                     reads=[pkq, csk], writes=["t1"])
                P.op("dve", lambda e: e.tensor_tensor(t2[64:96], psq2[64:96, :], cs[0][64:96, 1, :], ALU.mult),
                     reads=[pkq2, csk], writes=["t2"])
                P.op("dve", lambda e: e.tensor_tensor(qst[hb][64:96], t1[64:96], t2[64:96], ALU.add),
                     reads=["t1", "t2", qk], writes=[qk], partial=True)
                dma("pool", QB_d[h][:, t0:t0 + 512], qst[hb][0:96], [qk], [], "st" + qk)
                psn, pkn = psum()
                P.op("pe", lambda e: e.matmul(psn[0:64, :], wukv[:, h * 128:h * 128 + 64], ckvn[:, 0, :],
                                              start=True, stop=True), reads=["wukv", "ckvn"], writes=[pkn])
                kk = "kst%d" % hb
                evac(kst[hb][0:64], psn[0:64, :], [pkn], [kk])
                P.op("pool", lambda e: e.tensor_copy(kst[hb][64:96], kr[64:96]), reads=["kr", kk], writes=[kk],
                     partial=True)
                dma("pool", KB_d[h][:, t0:t0 + 512], kst[hb][0:96], [kk], [], "st" + kk)
            vb = 0
            vbk = "vbst%d" % vb
            wv = wukv.rearrange("p (h x) -> p h x", h=4)[:, :, 64:128]
            for tb in range(4):
                ps, pk = psum()
                P.op("pe", lambda e, tb=tb, ps=ps: e.matmul(ps[:, 0:256].rearrange("p (h x) -> p h x", h=4),
                                                            ckvn[:, 0, tb * 128:(tb + 1) * 128], wv,
                                                            start=True, stop=True),
                     reads=["wukv", "ckvn"], writes=[pk])
                evac(vbst[vb][:, tb, :], ps[:, 0:256], [pk], [vbk], partial=True)
            dma("pool", VB_d[t0:t0 + 512, :].rearrange("(tb p) f -> p tb f", p=128), vbst[vb], [vbk], [], "st" + vbk)

            for hp in range(2):
                ps, pk = fm_chunk(2720 + hp * 128, 128)
                out_chunk(ps, pk, QC_d[hp])
                ps, pk = fm_chunk(2976 + hp * 128, 128)
                out_chunk(ps, pk, KC_d[hp])
                ps, pk = fm_chunk(3488 + hp * 128, 128)
                out_chunk(ps, pk, QD_d[hp])
            ps, pk = fm_chunk(3744, 128)
            out_chunk(ps, pk, KD_d)

            vk = "vst%d" % vb
            groups = [(1536, 512, 0), (2048, 256, 512), (3232, 256, 768), (3872, 128, 1024)]
            for tb in range(4):
                for (col, n, so) in groups:
                    ps, pk = psum()
                    for c in range(8):
                        P.op("pe", lambda e, c=c, ps=ps, col=col, n=n: e.matmul(
                            ps[:, 0:n], hT[b][:, c, tb * 128:(tb + 1) * 128], w_in[:, c, col:col + n],
                            start=(c == 0), stop=(c == 7)), reads=["w_in%d" % c, hk_], writes=[pk])
                    evac(vst[vb][:, tb, so:so + n], ps[:, 0:n], [pk], [vk], partial=True)
            dma("pool", VA_d[t0:t0 + 512, :].rearrange("(tb p) f -> p tb f", p=128), vst[vb][:, :, 0:768], [vk], [],
                "stva%d" % vb)
            dma("pool", VC_d[t0:t0 + 512, :].rearrange("(tb p) f -> p tb f", p=128), vst[vb][:, :, 768:1024], [vk], [],
                "stvc%d" % vb)
            dma("pool", VD_d[t0:t0 + 512, :].rearrange("(tb p) f -> p tb f", p=128), vst[vb][:, :, 1024:1152], [vk], [],
                "stvd%d" % vb)
        P.barrier()

    def build_ebc(l, ar):
        EBC = ar.tile([4 * NCT, 128], BF16)
        sub = Arena(ar.off, SB_BYTES)
        nbf = sub.tile([60], F32)
        nbhi = sub.tile([60], BF16)
        nbhf = sub.tile([60], F32)
        nblo = sub.tile([60], BF16)
        ohc = sub.tile([4096], BF16)
        mcs = sub.tile([4096], F32)
        mcolS = sub.tile([60, 64], BF16)
        mk2 = sub.tile([64], BF16)
        dma("sp", nbf[0:31], nab_in[l], [], ["nbf"], "nbf")
        dma("pool", ohc[0:31], c_ohc, [], ["ohc"], "ohc")
        dma("pool", mk2, c_maskc, [], ["mk2"], "mk2")
        P.op("dve", lambda e: e.tensor_copy(nbhi[0:31], nbf[0:31]), reads=["nbf"], writes=["nbhi"])
        P.op("dve", lambda e: e.tensor_copy(nbhf[0:31], nbhi[0:31]), reads=["nbhi"], writes=["nbhf"])
        P.op("dve", lambda e: e.tensor_tensor(nblo[0:31], nbf[0:31], nbhf[0:31], ALU.subtract),
             reads=["nbf", "nbhf"], writes=["nblo"])
        for k in range(8):
            ps, pk = psum()
            P.op("pe", lambda e, ps=ps, k=k: e.matmul(ps[0:60, :], nbhi[0:31, :], ohc[0:31, k * 512:(k + 1) * 512],
                                                      start=True, stop=False), reads=["nbhi", "ohc"], writes=[pk])
            P.op("pe", lambda e, ps=ps, k=k: e.matmul(ps[0:60, :], nblo[0:31, :], ohc[0:31, k * 512:(k + 1) * 512],
                                                      start=False, stop=True), reads=["nblo", "ohc"], writes=[pk])
            P.op("act", lambda e, ps=ps, k=k: e.activation(out=mcs[0:60, k * 512:(k + 1) * 512], in_=ps[0:60, :],
                                                           func=AF.Exp), reads=[pk], writes=["mcs"], partial=True)
        dma("sp", mcol_d.rearrange("m a b -> m (a b)"), mcs[0:60], ["mcs"], [], "stmcs")
        P.barrier()
        src = mcol_d.rearrange("m kc qc -> kc m qc")
        dma("pool", mcolS[0:64], src, [], ["mcolS"], "mcolS", partial=True)
        dma("pool", mcolS[64:128], src, [], ["mcolS"], "mcolS", partial=True)
        P.op("dve", lambda e: e.tensor_tensor(mcolS, mcolS, mk2.unsqueeze(1).to_broadcast([128, 60, 64]), ALU.mult),
             reads=["mcolS", "mk2"], writes=["mcolS"])
        P.op("pool", lambda e: e.memset(EBC, 0.0), writes=["EBC"])
        rr = 0
        for h in range(4):
            for ti, sg in enumerate(C_SIGS):
                k = 0
                for krl in range(2):
                    for qrl in range(2):
                        dr = sg[k]
                        k += 1
                        if dr is None:
                            continue
                        dst = EBC[krl * 64:(krl + 1) * 64, h * NCT + ti, qrl * 64:(qrl + 1) * 64]
                        srcv = mcolS[krl * 64:(krl + 1) * 64, h * 15 + dr, :]
                        eng = ("dve", "pool")[rr % 2]
                        rr += 1
                        P.op(eng, lambda e, dst=dst, srcv=srcv: e.tensor_copy(dst, srcv), reads=["mcolS", "EBC"],
                             writes=["EBC"], partial=True)
        P.barrier()
        return EBC

    def phase_b(l):
        psn_pool[0] = 6
        ar0 = Arena(PBASE, SB_BYTES)
        EBC = build_ebc(l, ar0)
        base = ar0.off
        NPE = 12
        pexp = [ar0.tile([512], BF16) for _ in range(NPE)]
        pt = [ar0.tile([512], BF16) for _ in range(NPE)]
        rec = [ar0.tile([512], F32) for _ in range(2)]
        yst = [ar0.tile([2, 512], BF16) for _ in range(2)]
        wbase = ar0.off
        cnt = {"pe": 0, "rec": 0, "mul": 0}

        def score_slot(qbs, eb, scale, ebkey, zero=()):
            ps, pk = psum()
            for q, ent in enumerate(qbs):
                if ent is None:
                    continue
                kT, qT, lo, hi, rds = ent
                if lo == 0 and hi == 128:
                    P.op("pe", lambda e, kT=kT, qT=qT, q=q: e.matmul(ps[:, q * 128:(q + 1) * 128], kT, qT,
                                                                     start=True, stop=True), reads=rds, writes=[pk])
                else:
                    pb = kT.base_partition()
                    P.op("pe", lambda e, kT=kT, qT=qT, q=q, lo=lo, hi=hi, pb=pb: e.matmul(
                        ps[lo:hi, q * 128:(q + 1) * 128], kT, qT, start=True, stop=True, tile_position=(pb, lo)),
                        reads=rds, writes=[pk])
            i = cnt["pe"] % NPE
            cnt["pe"] += 1
            if eb is None:
                P.op("act", lambda e: e.activation(out=pt[i], in_=ps, func=AF.Exp, scale=scale), reads=[pk],
                     writes=["pt%d" % i])
                return pt[i], "pt%d" % i
            P.op("act", lambda e: e.activation(out=pexp[i], in_=ps, func=AF.Exp, scale=scale), reads=[pk],
                 writes=["pexp%d" % i])
            eng = ("dve", "pool")[cnt["mul"] % 2]
            cnt["mul"] += 1
            if isinstance(eb, list):
                for q, ebq in enumerate(eb):
                    if ebq is None or qbs[q] is None:
                        continue
                    P.op(eng, lambda e, q=q, ebq=ebq: e.tensor_tensor(pt[i][:, q * 128:(q + 1) * 128],
                                                                      pexp[i][:, q * 128:(q + 1) * 128], ebq, ALU.mult),
                         reads=["pexp%d" % i, ebkey], writes=["pt%d" % i], partial=(q > 0))
            else:
                P.op(eng, lambda e: e.tensor_tensor(pt[i].rearrange("p (a b) -> p a b", a=4),
                                                    pexp[i].rearrange("p (a b) -> p a b", a=4),
                                                    eb.unsqueeze(1).to_broadcast([128, 4, 128]), ALU.mult),
                     reads=["pexp%d" % i, ebkey], writes=["pt%d" % i])
                for (q, zlo, zhi) in zero:
                    P.op(eng, lambda e, q=q, zlo=zlo, zhi=zhi: e.memset(pt[i][zlo:zhi, q * 128:(q + 1) * 128], 0.0),
                         writes=["pt%d" % i], partial=True)
            return pt[i], "pt%d" % i

        def pv(ot, otk, pts, vents):
            for q in range(4):
                lst = [(sl, vents[sl][q]) for sl in range(len(pts)) if vents[sl][q] is not None]
                for n, (sl, (va, lo, hi, rds)) in enumerate(lst):
                    ptile, ptk = pts[sl]
                    P.op("pe", lambda e, va=va, lo=lo, hi=hi, ptile=ptile, q=q, n=n, last=len(lst) - 1: e.matmul(
                        ot[:, q * 128:(q + 1) * 128], va, ptile[lo:hi, q * 128:(q + 1) * 128],
                        start=(n == 0), stop=(n == last)), reads=rds + [ptk], writes=[otk])

        def normalize(ot, otk, dst, dstkey, addcol=None, partial=True):
            i = cnt["rec"] % 2
            cnt["rec"] += 1
            rk = "rec%d" % i
            if addcol is not None:
                P.op("act", lambda e: e.activation(out=rec[i][0:64], in_=ot[64:128, :], func=AF.Ln, bias=addcol),
                     reads=[otk], writes=[rk])
            else:
                P.op("act", lambda e: e.activation(out=rec[i][0:64], in_=ot[64:128, :], func=AF.Ln),
                     reads=[otk], writes=[rk])
            P.op("act", lambda e: e.activation(out=rec[i][0:64], in_=rec[i][0:64], func=AF.Exp, scale=-1.0),
                 reads=[rk], writes=[rk])
            P.op("dve", lambda e: e.tensor_tensor(dst, ot[0:64, :], rec[i][0:64], ALU.mult), reads=[otk, rk],
                 writes=[dstkey], partial=partial)

        def store_y(m, s, c2, ch, ysti, yk):
            dma("pool", yT_d[m, c2 * 128:(c2 + 1) * 128, s * S + ch * 512:s * S + (ch + 1) * 512], yst[ysti][:, 0, :],
                [yk], [], "st" + yk)

        ycnt = [0]

        def run_units(units, lag=1):
            for ui in range(len(units) + lag):
                if ui < len(units):
                    units[ui][0]()
                if ui >= lag:
                    units[ui - lag][1]()

        for s in range(NS):
            if "d" in mixers:
                ar = Arena(wbase, SB_BYTES)
                QT = [ar.tile([S], BF16) for _ in range(2)]
                KT = [ar.tile([S], BF16) for _ in range(2)]
                VG = ar.tile([32, 2, 128], BF16)
                for hp in range(2):
                    dma("sp", QT[hp], QD_d[hp][:, s * S:(s + 1) * S], [], ["QT%d" % hp], "QT%d" % hp)
                    for hb in range(2):
                        dma("sp", KT[hp][hb * 64:(hb + 1) * 64], KD_d[hp * 64:(hp + 1) * 64, s * S:(s + 1) * S], [],
                            ["KT%d" % hp], "KT%d" % hp, partial=True)
                P.op("pool", lambda e: e.memset(VG[:, :, :, 64:128], 1.0), writes=["VG"])
                for g_ in range(2):
                    dma("sp", VG[:, :, g_, 0:64],
                        VD_d[s * S:(s + 1) * S, g_ * 64:(g_ + 1) * 64].rearrange("(kb p) x -> p kb x", p=128),
                        [], ["VG"], "VG", partial=True)
                units = []
                for hp in range(2):
                    for ch in range(8):
                        yi = ycnt[0] % 2
                        ycnt[0] += 1
                        yk = "yst%d" % yi
                        for hb in range(2):
                            h = hp * 2 + hb
                            kvh = h // 2
                            box = {}

                            def st1(hp=hp, ch=ch, hb=hb, h=h, kvh=kvh, box=box):
                                pts = []
                                vents = []
                                for d in (-1, 0, 1):
                                    qbs = []
                                    vv = []
                                    for q in range(4):
                                        i = ch * 4 + q
                                        j = i + d
                                        if j < 0 or j > 31:
                                            qbs.append(None)
                                            vv.append(None)
                                            continue
                                        qbs.append((KT[hp][hb * 64:(hb + 1) * 64, j * 128:(j + 1) * 128],
                                                    QT[hp][hb * 64:(hb + 1) * 64, i * 128:(i + 1) * 128], 0, 128,
                                                    ["KT%d" % hp, "QT%d" % hp]))
                                        vv.append((VG[:, j, kvh, :], 0, 128, ["VG"]))
                                    pts.append(score_slot(qbs, EB_D[:, h * 3 + d + 1, :], 0.125, "EB"))
                                    vents.append(vv)
                                box["pts"] = pts
                                box["vents"] = vents

                            def st2(hp=hp, ch=ch, hb=hb, h=h, box=box, yi=yi, yk=yk):
                                ot, otk = psum_acc()
                                pv(ot, otk, box["pts"], box["vents"])
                                normalize(ot, otk, yst[yi][hb * 64:(hb + 1) * 64, 0, :], yk,
                                          addcol=snk[64:128, l * 4 + h:l * 4 + h + 1], partial=(hb == 1))
                                if hb == 1:
                                    store_y(3, s, hp, ch, yi, yk)

                            units.append((st1, st2))
                run_units(units)
                P.barrier()
            if "c" in mixers:
                ar = Arena(wbase, SB_BYTES)
                QT = [ar.tile([S], BF16) for _ in range(2)]
                KT = [ar.tile([S], BF16) for _ in range(2)]
                VG = ar.tile([32, 4, 128], BF16)
                for hp in range(2):
                    dma("sp", QT[hp], QC_d[hp][:, s * S:(s + 1) * S], [], ["QT%d" % hp], "QT%d" % hp)
                    dma("sp", KT[hp], KC_d[hp][:, s * S:(s + 1) * S], [], ["KT%d" % hp], "KT%d" % hp)
                P.op("pool", lambda e: e.memset(VG[:, :, :, 64:128], 1.0), writes=["VG"])
                for g_ in range(4):
                    dma("sp", VG[:, :, g_, 0:64],
                        VC_d[s * S:(s + 1) * S, g_ * 64:(g_ + 1) * 64].rearrange("(kb p) x -> p kb x", p=128),
                        [], ["VG"], "VG", partial=True)
                units = []
                for hp in range(2):
                    for ch in range(8):
                        yi = ycnt[0] % 2
                        ycnt[0] += 1
                        yk = "yst%d" % yi
                        dlist = sorted({j - (ch * 4 + q) for q in range(4) for (j, _) in C_TABLE[ch * 4 + q]})
                        for hb in range(2):
                            h = hp * 2 + hb
                            box = {}

                            def st1(hp=hp, ch=ch, hb=hb, h=h, box=box, dlist=dlist):
                                pts = []
                                vents = []
                                for d in dlist:
                                    qbs = []
                                    vv = []
                                    ebl = []
                                    for q in range(4):
                                        i = ch * 4 + q
                                        j = i + d
                                        ti = dict(C_TABLE[i]).get(j)
                                        if ti is None:
                                            qbs.append(None)
                                            vv.append(None)
                                            ebl.append(None)
                                            continue
                                        qbs.append((KT[hp][hb * 64:(hb + 1) * 64, j * 128:(j + 1) * 128],
                                                    QT[hp][hb * 64:(hb + 1) * 64, i * 128:(i + 1) * 128], 0, 128,
                                                    ["KT%d" % hp, "QT%d" % hp]))
                                        vv.append((VG[:, j, h, :], 0, 128, ["VG"]))
                                        ebl.append(EBC[:, h * NCT + ti, :])
                                    tis = {dict(C_TABLE[ch * 4 + q]).get(ch * 4 + q + d) for q in range(4)}
                                    if len(tis) == 1 and None not in tis:
                                        eb = ebl[0]
                                    else:
                                        eb = ebl
                                    pts.append(score_slot(qbs, eb, 0.125, "EBC"))
                                    vents.append(vv)
                                box["pts"] = pts
                                box["vents"] = vents

                            def st2(hp=hp, ch=ch, hb=hb, box=box, yi=yi, yk=yk):
                                ot, otk = psum_acc()
                                pv(ot, otk, box["pts"], box["vents"])
                                normalize(ot, otk, yst[yi][hb * 64:(hb + 1) * 64, 0, :], yk, partial=(hb == 1))
                                if hb == 1:
                                    store_y(2, s, hp, ch, yi, yk)

                            units.append((st1, st2))
                run_units(units)
                P.barrier()
            if "a" in mixers:
                for hp in range(2):
                    ar = Arena(wbase, SB_BYTES)
                    QT = ar.tile([S], BF16)
                    KT = ar.tile([S + 128], BF16)
                    VG = ar.tile([48, 2, 128], BF16)
                    OA = [ar.tile([S], F32) for _ in range(2)]
                    P.op("pool", lambda e: e.memset(VG, 0.0), writes=["VG"])
                    P.op("pool", lambda e: e.memset(VG[:, :, :, 64:128], 1.0), writes=["VG"])
                    P.op("pool", lambda e: e.memset(KT[:, 0:64], 0.0), writes=["KTpad"])
                    P.op("pool", lambda e: e.memset(KT[:, S + 64:S + 128], 0.0), writes=["KTpad"])
                    for g in range(3):
                        dil = DIL[g]
                        Lg = S // dil
                        nb = Lg // 128
                        dma("sp", QT, QA_d[g, hp][:, s * S:(s + 1) * S], [], ["QT"], "QT")
                        dma("sp", KT[:, 64:64 + S], KA_d[g, hp][:, s * S:(s + 1) * S], [], ["KT"], "KT")
                        vsrc = VA_d[s * S:(s + 1) * S, :].rearrange("(j r) (g x) -> r j g x", r=dil, g=3)
                        for r in range(dil):
                            for m in range(nb + 1):
                                lo = 64 if m == 0 else 0
                                hi = 64 if m == nb else 128
                                j0 = 128 * m - 64 + lo
                                src = vsrc[r, j0:j0 + (hi - lo), g, :].rearrange("j (h x) -> j h x", h=4)[
                                    :, hp * 2:hp * 2 + 2, :]
                                dma("sp", VG[lo:hi, r * (nb + 1) + m, :, 0:64], src, [], ["VG"], "VG", partial=True)
                        nqb = S // 128
                        units = []
                        for ch in range(nqb // 4):
                            for hb in range(2):
                                h = hp * 2 + hb
                                box = {}

                                def st1(ch=ch, hb=hb, h=h, box=box, g=g, Lg=Lg, nb=nb):
                                    pts = []
                                    vents = []
                                    for ab in range(2):
                                        qbs = []
                                        vv = []
                                        zer = []
                                        for q in range(4):
                                            gi = ch * 4 + q
                                            r, i = divmod(gi, nb)
                                            m = i + ab
                                            k0 = 64 + r * Lg + 128 * m - 64
                                            qbs.append((KT[hb * 64:(hb + 1) * 64, k0:k0 + 128],
                                                        QT[hb * 64:(hb + 1) * 64, gi * 128:(gi + 1) * 128], 0, 128,
                                                        ["KT", "KTpad", "QT"]))
                                            vv.append((VG[:, r * (nb + 1) + m, hb, :], 0, 128, ["VG"]))
                                            if m == 0:
                                                zer.append((q, 0, 64))
                                            elif m == nb:
                                                zer.append((q, 64, 128))
                                        pts.append(score_slot(qbs, EB_A[:, (g * 4 + h) * 2 + ab, :], 0.125, "EB",
                                                              zero=zer))
                                        vents.append(vv)
                                    box["pts"] = pts
                                    box["vents"] = vents

                                def st2(ch=ch, hb=hb, box=box, dil=dil, Lg=Lg):
                                    ot, otk = psum_acc()
                                    pv(ot, otk, box["pts"], box["vents"])
                                    if dil == 1:
                                        dst = OA[hb][:, ch * 512:(ch + 1) * 512]
                                        P.op("act", lambda e, dst=dst, ot=ot: e.copy(dst, ot), reads=[otk],
                                             writes=["OA%d" % hb], partial=True)
                                    else:
                                        oav = OA[hb].rearrange("p (j r) -> p r j", r=dil)
                                        pos0 = ch * 512
                                        done = 0
                                        while done < 512:
                                            r, j = divmod(pos0 + done, Lg)
                                            n = min(512 - done, Lg - j)
                                            dst = oav[:, r, j:j + n]
                                            srcv = ot[:, done:done + n]
                                            P.op("dve", lambda e, dst=dst, srcv=srcv: e.tensor_tensor(dst, dst, srcv,
                                                                                                      ALU.add),
                                                 reads=[otk, "OA%d" % hb], writes=["OA%d" % hb], partial=True)
                                            done += n

                                units.append((st1, st2))
                        run_units(units)
                    for ch in range(8):
                        yi = ycnt[0] % 2
                        ycnt[0] += 1
                        yk = "yst%d" % yi
                        for hb in range(2):
                            i = cnt["rec"] % 2
                            cnt["rec"] += 1
                            rk = "rec%d" % i
                            oa = OA[hb][:, ch * 512:(ch + 1) * 512]
                            P.op("act", lambda e, oa=oa, i=i: e.activation(out=rec[i][0:64], in_=oa[64:128], func=AF.Ln),
                                 reads=["OA%d" % hb], writes=[rk])
                            P.op("act", lambda e, i=i: e.activation(out=rec[i][0:64], in_=rec[i][0:64], func=AF.Exp,
                                                                    scale=-1.0), reads=[rk], writes=[rk])
                            P.op("dve", lambda e, oa=oa, i=i, hb=hb, yi=yi: e.tensor_tensor(
                                yst[yi][hb * 64:(hb + 1) * 64, 0, :], oa[0:64], rec[i][0:64], ALU.mult),
                                reads=["OA%d" % hb, rk], writes=[yk], partial=(hb == 1))
                        store_y(0, s, hp, ch, yi, yk)
                    P.barrier()
            if "b" in mixers:
                ar = Arena(wbase, SB_BYTES)
                QT = [ar.tile([S], BF16) for _ in range(2)]
                KT = [ar.tile([S], BF16) for _ in range(2)]
                VG = [ar.tile([32, 128], BF16) for _ in range(2)]
                for bb in range(2):
                    P.op("pool", lambda e, bb=bb: e.memset(VG[bb][:, :, 64:128], 1.0), writes=["VG%d" % bb])

                def load_b(h):
                    bb = h % 2
                    dma("sp", QT[bb][0:96], QB_d[h][:, s * S:(s + 1) * S], [], ["QT%d" % bb], "QT%d" % bb)
                    dma("sp", KT[bb][0:96], KB_d[h][:, s * S:(s + 1) * S], [], ["KT%d" % bb], "KT%d" % bb)
                    dma("sp", VG[bb][:, :, 0:64],
                        VB_d[s * S:(s + 1) * S, h * 64:(h + 1) * 64].rearrange("(kb p) x -> p kb x", p=128),
                        [], ["VG%d" % bb], "VG%d" % bb, partial=True)

                LAG = 3
                work = [(h, ch, kb) for h in range(4) for ch in range(8) for kb in range(32)]
                pend = []
                ots = {}
                load_b(0)
                for wi in range(len(work) + LAG):
                    if wi < len(work):
                        h, ch, kb = work[wi]
                        bb = h % 2
                        if ch == 0 and kb == 0 and h + 1 < 4:
                            pass
                        ps, pk = psum()
                        P.op("pe", lambda e, ps=ps, kb=kb, bb=bb, ch=ch: e.matmul(
                            ps, KT[bb][0:96, kb * 128:(kb + 1) * 128], QT[bb][0:96, ch * 512:(ch + 1) * 512],
                            start=True, stop=True), reads=["KT%d" % bb, "QT%d" % bb], writes=[pk])
                        i = cnt["pe"] % NPE
                        cnt["pe"] += 1
                        P.op("act", lambda e, ps=ps, i=i: e.activation(out=pt[i], in_=ps, func=AF.Exp,
                                                                       scale=96.0 ** -0.5),
                             reads=[pk], writes=["pt%d" % i])
                        pend.append((h, ch, kb, i))
                    if wi >= LAG:
                        h, ch, kb, i = pend.pop(0)
                        bb = h % 2
                        if kb == 0:
                            ots[(h, ch)] = psum_acc()
                        ot, otk = ots[(h, ch)]
                        P.op("pe", lambda e, kb=kb, bb=bb, i=i, ot=ot: e.matmul(
                            ot, VG[bb][:, kb, :], pt[i], start=(kb == 0), stop=(kb == 31)),
                            reads=["VG%d" % bb, "pt%d" % i], writes=[otk])
                        if kb == 31:
                            i2 = cnt["rec"] % 2
                            cnt["rec"] += 1
                            rk = "rec%d" % i2
                            P.op("act", lambda e, ot=ot, i2=i2: e.activation(out=rec[i2][0:64], in_=ot[64:128, :],
                                                                            func=AF.Ln), reads=[otk], writes=[rk])
                            P.op("act", lambda e, i2=i2: e.activation(out=rec[i2][0:64], in_=rec[i2][0:64], func=AF.Exp,
                                                                     scale=-1.0), reads=[rk], writes=[rk])
                            ysi = ycnt[0] % 2
                            ycnt[0] += 1
                            yk = "yst%d" % ysi
                            P.op("dve", lambda e, ot=ot, i2=i2, ysi=ysi: e.tensor_tensor(
                                yst[ysi][0:64, 0, :], ot[0:64, :], rec[i2][0:64], ALU.mult),
                                reads=[otk, rk], writes=[yk])
                            dma("pool", yT_d[1, h * 64:(h + 1) * 64, s * S + ch * 512:s * S + (ch + 1) * 512],
                                yst[ysi][0:64, 0, :], [yk], [], "st" + yk)
                            if ch == 0 and h + 1 < 4:
                                load_b(h + 1)
                P.barrier()

    def phase_c(l, x_src, x_dst):
        psn_pool[0] = 8
        ar = Arena(PBASE, SB_BYTES)
        wg = ar.tile([32, D], BF16)
        wb = ar.tile([8, D], BF16)
        wo = ar.tile([8, D], BF16)
        hT = [ar.tile([8, 512], BF16) for _ in range(2)]
        yT = [ar.tile([8, 512], BF16) for _ in range(2)]
        xt = ar.tile([8, 512], F32)
        sg = [ar.tile([512], F32) for _ in range(2)]
        tm = [ar.tile([512], F32) for _ in range(2)]
        acc = ar.tile([512], F32)
        mg = ar.tile([8, 512], BF16)
        for i in range(4):
            for c in range(8):
                k = "wg%d" % (i * 8 + c)
                dma("pool", wg[:, i * 8 + c, :], w_gate_in[l, i, c * 128:(c + 1) * 128, :], [], [k], k)
        dma("pool", wb, w_br_in[l].rearrange("i (c p) n -> p (i c) n", p=128), [], ["wb"], "wb")
        dma("pool", wo, w_out_in[l].rearrange("(c p) n -> p c n", p=128), [], ["wo"], "wo")
        tiles = [(s, T) for s in range(NS) for T in range(8)]

        def load(idx):
            s, T = tiles[idx]
            b = idx % 2
            t0 = s * S + T * 512
            dma("sp", hT[b], hT_d.rearrange("(c p) n -> p c n", p=128)[:, :, t0:t0 + 512], [], ["hT%d" % b], "hT%d" % b)
            dma("sp", yT[b], yT_d.rearrange("m (c p) n -> p (m c) n", p=128)[:, :, t0:t0 + 512], [], ["yT%d" % b],
                "yT%d" % b)

        load(0)
        k2 = 0
        for idx, (s, T) in enumerate(tiles):
            b = idx % 2
            t0 = s * S + T * 512
            if idx + 1 < len(tiles):
                load(idx + 1)
            dma("sp", xt, x_src.rearrange("(c p) n -> p c n", p=128)[:, :, t0:t0 + 512], [], ["xt"], "xt")
            for oc in range(8):
                for i in range(4):
                    psg, pkg = psum()
                    for c in range(8):
                        P.op("pe", lambda e, c=c, i=i, oc=oc, psg=psg: e.matmul(
                            psg, wg[:, i * 8 + c, oc * 128:(oc + 1) * 128], hT[b][:, c, :],
                            start=(c == 0), stop=(c == 7)), reads=["wg%d" % (i * 8 + c), "hT%d" % b], writes=[pkg])
                    psb_, pkb = psum()
                    for c2 in range(2):
                        P.op("pe", lambda e, c2=c2, i=i, oc=oc, psb_=psb_: e.matmul(
                            psb_, wb[:, i * 2 + c2, oc * 128:(oc + 1) * 128], yT[b][:, i * 2 + c2, :],
                            start=(c2 == 0), stop=(c2 == 1)), reads=["wb", "yT%d" % b], writes=[pkb])
                    j = k2 % 2
                    k2 += 1
                    P.op("act", lambda e, psg=psg, j=j: e.activation(out=sg[j], in_=psg, func=AF.Sigmoid), reads=[pkg],
                         writes=["sg%d" % j])
                    if i == 0:
                        P.op("dve", lambda e, psb_=psb_, j=j: e.tensor_tensor(acc, sg[j], psb_, ALU.mult),
                             reads=["sg%d" % j, pkb], writes=["acc"])
                    else:
                        P.op("dve", lambda e, psb_=psb_, j=j: e.tensor_tensor(tm[j], sg[j], psb_, ALU.mult),
                             reads=["sg%d" % j, pkb], writes=["tm%d" % j])
                        if i < 3:
                            P.op("pool", lambda e, j=j: e.tensor_tensor(acc, acc, tm[j], ALU.add),
                                 reads=["tm%d" % j, "acc"], writes=["acc"])
                        else:
                            P.op("pool", lambda e, j=j, oc=oc: e.tensor_tensor(mg[:, oc, :], acc, tm[j], ALU.add),
                                 reads=["tm%d" % j, "acc"], writes=["mg"], partial=True)
            for oc in range(8):
                ps, pk = psum()
                for c in range(8):
                    P.op("pe", lambda e, c=c, oc=oc, ps=ps: e.matmul(ps, wo[:, c, oc * 128:(oc + 1) * 128], mg[:, c, :],
                                                                     start=(c == 0), stop=(c == 7)),
                         reads=["wo", "mg"], writes=[pk])
                P.op("dve", lambda e, oc=oc, ps=ps: e.tensor_tensor(xt[:, oc, :], xt[:, oc, :], ps, ALU.add),
                     reads=[pk, "xt"], writes=["xt"], partial=True)
            dma("pool", x_dst.rearrange("(c p) n -> p c n", p=128)[:, :, t0:t0 + 512], xt, ["xt"], [], "stxt")
        P.barrier()

    def phase_d(l, x_src, x_dst, final):
        psn_pool[0] = 8
        ar = Arena(PBASE, SB_BYTES)
        wg = ar.tile([8, DFF], BF16)
        wu = ar.tile([8, DFF], BF16)
        wd = ar.tile([NFF, D], BF16)
        TW = 256
        xt = [ar.tile([8, TW + 2], F32) for _ in range(2)]
        sq = ar.tile([8, TW + 2], BF16)
        rt = ar.tile([TW + 2], F32)
        rstd = ar.tile([TW + 2], F32)
        h2s = [ar.tile([8, TW + 2], BF16) for _ in range(2)]
        cv = [ar.tile([TW], F32) for _ in range(2)]
        ge = [ar.tile([TW], F32) for _ in range(2)]
        uT = ar.tile([NFF, TW], BF16)
        sqfin, rtf, rstdf = sq, rt, rstd
        for c in range(8):
            dma("pool", wg[:, c, :], w_fg_in[l, c * 128:(c + 1) * 128, :], [], ["fg%d" % c], "fg%d" % c)
            dma("pool", wu[:, c, :], w_fu_in[l, c * 128:(c + 1) * 128, :], [], ["fu%d" % c], "fu%d" % c)
        for f in range(NFF):
            dma("pool", wd[:, f, :], w_fd_in[l, f * 128:(f + 1) * 128, :], [], ["fd%d" % (f % 4)], "fd%d" % (f % 4),
                partial=True)
        fdk = ["fd%d" % i for i in range(4)]
        NTI = S // TW
        tiles = [(s, T) for s in range(NS) for T in range(NTI)]
        xs = x_src.rearrange("(c p) n -> p c n", p=128)

        def load(idx):
            s, T = tiles[idx]
            b = idx % 2
            t0 = s * S + T * TW
            lo = 1 if T == 0 else 0
            hi = TW + 1 if T == NTI - 1 else TW + 2
            k = "xt%d" % b
            if lo == 1:
                P.op("pool", lambda e: e.memset(xt[b][:, :, 0:1], 0.0), writes=[k])
            if hi == TW + 1:
                P.op("pool", lambda e: e.memset(xt[b][:, :, TW + 1:TW + 2], 0.0), writes=[k])
            dma("sp", xt[b][:, :, lo:hi], xs[:, :, t0 - 1 + lo:t0 - 1 + hi], [], [k], k)

        def norm_d(idx):
            b_ = idx % 2
            return norm_stages(xt[b_], "xt%d" % b_, 8, TW + 2, lambda c: gffn[:, l * 8 + c:l * 8 + c + 1], sq, "sq",
                               rt, "rt", rstd, "rstd", h2s[b_], "h2%d" % b_, float(D))

        load(0)
        for st_ in norm_d(0):
            st_()
        k2 = 0
        for idx, (s, T) in enumerate(tiles):
            b = idx % 2
            t0 = s * S + T * TW
            nst = None
            if idx + 1 < len(tiles):
                load(idx + 1)
                nst = norm_d(idx + 1)
            xk = "xt%d" % b
            h2 = h2s[b]
            h2k = "h2%d" % b
            for f in range(NFF):
                if nst is not None and f == 2:
                    nst[0]()
                if nst is not None and f == 8:
                    nst[1]()
                    nst[2]()
                psg, pkg = psum()
                for c in range(8):
                    P.op("pe", lambda e, c=c, f=f, psg=psg: e.matmul(psg[:, 0:TW + 2], wg[:, c, f * 128:(f + 1) * 128],
                                                                     h2[:, c, :], start=(c == 0), stop=(c == 7)),
                         reads=["fg%d" % c, h2k], writes=[pkg])
                psu, pku = psum()
                for c in range(8):
                    P.op("pe", lambda e, c=c, f=f, psu=psu: e.matmul(psu[:, 0:TW], wu[:, c, f * 128:(f + 1) * 128],
                                                                     h2[:, c, 1:TW + 1], start=(c == 0), stop=(c == 7)),
                         reads=["fu%d" % c, h2k], writes=[pku])
                j = k2 % 2
                k2 += 1
                cwb = (l * NFF + f) * 3
                P.op("act", lambda e, psg=psg, j=j, cwb=cwb, f=f: e.activation(
                    out=cv[j], in_=psg[:, 1:TW + 1], func=AF.Identity, scale=cw[:, cwb + 1:cwb + 2],
                    bias=cb[:, l * NFF + f:l * NFF + f + 1]), reads=[pkg], writes=["cv%d" % j])
                P.op("dve", lambda e, psg=psg, j=j, cwb=cwb: e.scalar_tensor_tensor(
                    cv[j], psg[:, 0:TW], cw[:, cwb:cwb + 1], cv[j], ALU.mult, ALU.add), reads=[pkg, "cv%d" % j],
                    writes=["cv%d" % j])
                P.op("dve", lambda e, psg=psg, j=j, cwb=cwb: e.scalar_tensor_tensor(
                    cv[j], psg[:, 2:TW + 2], cw[:, cwb + 2:cwb + 3], cv[j], ALU.mult, ALU.add), reads=[pkg, "cv%d" % j],
                    writes=["cv%d" % j])
                P.op("act", lambda e, j=j: e.activation(out=ge[j], in_=cv[j], func=AF.Gelu_apprx_tanh),
                     reads=["cv%d" % j], writes=["ge%d" % j])
                P.op("dve", lambda e, j=j, f=f, psu=psu: e.tensor_tensor(uT[:, f, :], ge[j], psu[:, 0:TW], ALU.mult),
                     reads=["ge%d" % j, pku], writes=["uT"], partial=True)
            xo = xt[b][:, :, 1:TW + 1]
            for oc in range(8):
                ps, pk = psum()
                for f in range(NFF):
                    P.op("pe", lambda e, f=f, oc=oc, ps=ps: e.matmul(ps[:, 0:TW], wd[:, f, oc * 128:(oc + 1) * 128],
                                                                     uT[:, f, :], start=(f == 0), stop=(f == NFF - 1)),
                         reads=fdk + ["uT"], writes=[pk])
                P.op("dve", lambda e, oc=oc, ps=ps: e.tensor_tensor(xt[b][:, oc, 1:TW + 1], xt[b][:, oc, 1:TW + 1],
                                                                    ps[:, 0:TW], ALU.add),
                     reads=[pk, xk], writes=[xk], partial=True)
            if not final:
                dma("pool", x_dst.rearrange("(c p) n -> p c n", p=128)[:, :, t0:t0 + TW], xo, [xk], [], "st" + xk)
            else:
                sqf = sqfin[:, :, 0:TW]
                P.op("act", lambda e: e.activation(out=sqf, in_=xo, func=AF.Square), reads=[xk], writes=["sq"])
                ps, pk = psum()
                for c in range(8):
                    P.op("pe", lambda e, c=c, ps=ps: e.matmul(ps[:, 0:TW], ones_bf, sqfin[:, c, 0:TW], start=(c == 0),
                                                              stop=(c == 7)), reads=["sq", "ones"], writes=[pk])
                P.op("act", lambda e, ps=ps: e.activation(out=rtf[:, 0:TW], in_=ps[:, 0:TW], func=AF.Ln,
                                                          scale=1.0 / D, bias=epsc), reads=[pk, "epsc"], writes=["rt"])
                P.op("act", lambda e: e.activation(out=rstdf[:, 0:TW], in_=rtf[:, 0:TW], func=AF.Exp, scale=-0.5),
                     reads=["rt"], writes=["rstd"])
                for c in range(8):
                    P.op("dve", lambda e, c=c: e.scalar_tensor_tensor(xt[b][:, c, 1:TW + 1], xt[b][:, c, 1:TW + 1],
                                                                      gfin[:, c:c + 1], rstdf[:, 0:TW], ALU.mult,
                                                                      ALU.mult),
                         reads=[xk, "rstd"], writes=[xk], partial=True)
                dma("pool", outT.rearrange("(c p) n -> p c n", p=128)[:, :, t0:t0 + TW], xo, [xk], [], "st" + xk)
        P.barrier()

    setup()
    x_cur = xT_in
    for l in range(L):
        if "A" in phases:
            phase_a(l, x_cur)
        if "B" in phases:
            phase_b(l)
        if "C" in phases:
            x_c = xa_d if l == 0 else x_cur
            phase_c(l, x_cur, x_c)
        else:
            x_c = x_cur
        if "D" in phases:
            x_n = xb_d if x_c is xa_d else xa_d
            phase_d(l, x_c, x_n, final=(l == L - 1))
            x_cur = x_n
    P.barrier()
    P.emit(st)
    st.close()
    return nc


_NC_CACHE = {}


def _host_inputs(inp, core, NS, L):
    f = np.float32
    x = np.asarray(inp["x"], f)
    xs = x[core * NS:(core + 1) * NS]
    xT = np.ascontiguousarray(xs.reshape(NS * S, D).T)

    def cols(v, n):
        v = np.asarray(v, f).reshape(-1, n, 128)
        return np.ascontiguousarray(v.transpose(2, 0, 1).reshape(128, -1))

    m = dict(
        xT=xT,
        t5=np.asarray(inp["t5_table"], f),
        gmix=cols(inp["norm_mix_g"][:L], 8),
        gffn=cols(inp["norm_ffn_g"][:L], 8),
        gfin=cols(np.asarray(inp["final_g"])[None], 8),
        gq=cols(inp["q_norm_g"][:L], 2),
        gkv=cols(inp["kv_norm_g"][:L], 1),
        cb=cols(inp["conv_b"][:L], NFF),
        snk=np.ascontiguousarray(np.broadcast_to(np.asarray(inp["sink_logit"][:L], f).reshape(1, -1), (128, L * 4))),
        nab=np.ascontiguousarray(np.asarray(inp["na_bias"][:L], f).transpose(0, 3, 1, 2).reshape(L, 31, 60)),
        w_in=np.asarray(inp["w_in"][:L], f), w_uq=np.asarray(inp["w_uq"][:L], f),
        w_ukv=np.asarray(inp["w_ukv"][:L], f), w_gate=np.asarray(inp["w_gate"][:L], f),
        w_branch=np.asarray(inp["w_branch"][:L], f), w_out=np.asarray(inp["w_out"][:L], f),
        w_ffn_gate=np.asarray(inp["w_ffn_gate"][:L], f), w_ffn_up=np.asarray(inp["w_ffn_up"][:L], f),
        w_ffn_down=np.asarray(inp["w_ffn_down"][:L], f),
    )
    cwv = np.asarray(inp["conv_w"][:L], f)
    cwv = cwv.reshape(L, 3, NFF, 128).transpose(3, 0, 2, 1)
    m["cw"] = np.ascontiguousarray(cwv.reshape(128, -1))
    m.update(_consts())
    return m


def kernel(**inputs):
    NS, L = 2, DEPTH
    key = (NS, L)
    if key not in _NC_CACHE:
        _NC_CACHE[key] = build_nc(NS, L)
    nc = _NC_CACHE[key]
    n = 8
    in_maps = [_host_inputs(inputs, c, NS, L) for c in range(n)]
    res = run_bass_kernel_spmd(nc, in_maps, core_ids=list(range(n)))
    outs = []
    for c in range(n):
        oT = np.asarray(res.results[c]["outT"], np.float32)
        outs.append(oT.T.reshape(NS, S, D))
    return np.ascontiguousarray(np.concatenate(outs, 0))
```

```python
import math
from contextlib import ExitStack

import numpy as np
import concourse.bass as bass
import concourse.mybir as mybir
from concourse.bass_utils import run_bass_kernel_spmd

F32 = mybir.dt.float32
BF16 = mybir.dt.bfloat16
U8 = mybir.dt.uint8
AF = mybir.ActivationFunctionType
ALU = mybir.AluOpType

S = 4096
D = 1024
NCH = 8
DEPTH = 2
INC = 4000
DFF = 2816
NFF = 22
EPS = 1e-6
SEM_EPOCH = 24000
SB_BYTES = 189 * 1024

RT_D = 511
RT_A = 383
RT = RT_D + 3 * RT_A
DIL = (1, 4, 16)


class _Op:
    __slots__ = ("idx", "eng", "fn", "lane", "deps", "inc", "count", "dma_key")

    def __init__(self, idx, eng, fn, lane, dma_key):
        self.idx = idx
        self.eng = eng
        self.fn = fn
        self.lane = lane
        self.deps = {}
        self.inc = False
        self.count = None
        self.dma_key = dma_key


class _Rec:
    def __init__(self):
        self.call = None

    def __getattr__(self, name):
        def f(*a, **k):
            self.call = (name, a, k)
            return None
        return f


class Prog:
    ENGS = ("pe", "act", "dve", "pool", "sp")

    def __init__(self, nc):
        self.nc = nc
        self.ops = []
        self.by_eng = {e: [] for e in self.ENGS}
        self.res = {}
        self.dma_count = {}
        self.pool_map = {}
        self.last = {}

    def _state(self, key):
        st = self.res.get(key)
        if st is None:
            st = [{}, {}]
            self.res[key] = st
        return st

    def op(self, eng, fn, reads=(), writes=(), sem=None, partial=False):
        dma = sem is not None
        if dma:
            key = self.pool_map.get(sem)
            if key is None:
                key = len(self.pool_map)
                self.pool_map[sem] = key
            lane = ("dma", key)
        else:
            key = None
            lane = eng
        if fn is not None:
            rec = _Rec()
            fn(rec)
            fn = rec.call
        o = _Op(len(self.ops), eng, fn, lane, key)
        if dma:
            self.dma_count[key] = self.dma_count.get(key, 0) + 1
            o.count = self.dma_count[key]

        def add_dep(d, raw=False):
            if d.lane == o.lane and not dma:
                if not raw or eng == "pe":
                    return
            cur = o.deps.get(d.lane)
            if cur is None or d.idx > cur.idx:
                o.deps[d.lane] = d

        for r in reads:
            w, rd = self._state(r)
            for d in w.values():
                add_dep(d, raw=True)
        for wkey in writes:
            w, rd = self._state(wkey)
            if not (partial and not rd):
                for d in w.values():
                    add_dep(d)
            for d in rd.values():
                add_dep(d)
        for r in reads:
            w, rd = self._state(r)
            rd[o.lane] = o
        for wkey in writes:
            st = self._state(wkey)
            if partial and not st[1]:
                st[0][o.lane] = o
            else:
                st[0] = {o.lane: o}
                st[1] = {}
        for d in o.deps.values():
            if d.dma_key is None:
                d.inc = True
        self.ops.append(o)
        self.by_eng[eng].append(o)
        self.last[lane] = o
        return o

    def barrier(self, engs=None):
        lasts = list(self.last.values())
        for e in (engs or self.ENGS):
            o = _Op(len(self.ops), e, None, e, None)
            for d in lasts:
                if d.lane == e:
                    continue
                o.deps[d.lane] = d
                if d.dma_key is None:
                    d.inc = True
            self.ops.append(o)
            self.by_eng[e].append(o)
        self.res = {}
        self.pool_map = {}
        self.last = {}

    def emit(self, stack):
        nc = self.nc
        sems = {}

        def get_sem(name):
            s = sems.get(name)
            if s is None:
                s = stack.enter_context(nc.semaphore("s%d" % len(sems)))
                sems[name] = s
            return s

        for e in self.ENGS:
            c = 0
            for o in self.by_eng[e]:
                if o.dma_key is None and o.inc:
                    c += 1
                    o.count = c

        def sem_for(o):
            if o.dma_key is None:
                ep, v = divmod(o.count - 1, SEM_EPOCH)
                return get_sem(("c", o.lane, ep)), v + 1
            per = SEM_EPOCH // 16
            ep, v = divmod(o.count - 1, per)
            return get_sem(("d", o.dma_key, ep)), (v + 1) * 16

        for o in self.ops:
            if o.dma_key is not None or o.inc:
                sem_for(o)
        block = stack.enter_context(nc.Block())
        deco = {"pe": block.tensor, "act": block.scalar, "dve": block.vector,
                "pool": block.gpsimd, "sp": block.sync}
        for e in self.ENGS:
            ops = self.by_eng[e]
            if not ops:
                continue

            def body(engh, ops=ops):
                waited = {}
                for o in ops:
                    for d in o.deps.values():
                        s, v = sem_for(d)
                        k = id(s)
                        if waited.get(k, 0) >= v:
                            continue
                        waited[k] = v
                        engh.wait_ge(s, v)
                    if o.fn is None:
                        continue
                    name, a, k = o.fn
                    ins = getattr(engh, name)(*a, **k)
                    if o.dma_key is not None:
                        s, v = sem_for(o)
                        ins.then_inc(s, 16)
                    elif o.inc:
                        s, v = sem_for(o)
                        ins.then_inc(s, 1)

            deco[e](body)
        self.n_sems = len(sems)


def _t5_bucket(rel):
    half = 16
    exact = 8
    n = np.abs(rel)
    large = exact + (np.log(np.maximum(n, 1) / exact) / math.log(1024 / exact)
                     * (half - exact)).astype(np.int32)
    large = np.minimum(large, half - 1)
    return (np.where(rel > 0, half, 0) + np.where(n < exact, n, large)).astype(np.int32)


def _c_rs(qr):
    return min(max(qr - 4, 0), 56)


def _c_sig(i, j):
    sig = []
    for krl in range(2):
        for qrl in range(2):
            kr = 2 * j + krl
            qr = 2 * i + qrl
            rs = _c_rs(qr)
            if rs <= kr < rs + 8:
                sig.append(kr - qr + 7)
            else:
                sig.append(None)
    return tuple(sig)


def _c_tiles():
    sigs = []
    table = []
    for i in range(32):
        row = []
        for j in range(32):
            sg = _c_sig(i, j)
            if all(v is None for v in sg):
                continue
            if sg not in sigs:
                sigs.append(sg)
            row.append((j, sigs.index(sg)))
        table.append(row)
    return sigs, table


C_SIGS, C_TABLE = _c_tiles()
NCT = len(C_SIGS)


def _consts():
    ohv = np.zeros((32, RT), np.float32)
    msk = np.zeros((16, RT), np.float32)
    u = np.arange(RT_D)
    rel = u - 255
    val = np.abs(rel) <= 128
    b = _t5_bucket(rel)
    ohv[b[val], u[val]] = 1.0
    msk[:, u[val]] = 1.0
    for g, dil in enumerate(DIL):
        off = RT_D + g * RT_A
        u = np.arange(RT_A)
        j = u - 191
        val = np.abs(j) <= 64
        b = _t5_bucket(j * dil)
        ohv[b[val], off + u[val]] = 1.0
        msk[:, off + u[val]] = 1.0
    jf = np.zeros((128, 128), np.float32)
    jf[np.arange(128), 127 - np.arange(128)] = 1.0
    ohc = np.zeros((31, 64, 64), np.float32)
    mc = np.zeros((64, 64), np.float32)
    for qc in range(64):
        cs = min(max(qc - 8, 0), 48)
        for kc in range(cs, cs + 16):
            ohc[kc - qc + 15, kc, qc] = 1.0
            mc[kc, qc] = 1.0
    maskc2 = np.concatenate([mc, mc], 0)
    inv = 10000.0 ** (-np.arange(16, dtype=np.float32) / 16)
    ang = np.arange(S, dtype=np.float32)[None, :] * inv[:, None]
    cos2 = np.concatenate([np.cos(ang), np.cos(ang)], 0).astype(np.float32)
    sin2 = np.concatenate([np.sin(ang), np.sin(ang)], 0).astype(np.float32)
    return dict(c_ohv=ohv, c_msk=msk, c_jf=jf, c_ohc=ohc.reshape(31, 4096),
                c_maskc=maskc2, c_cos=cos2, c_sin=sin2)


def build_nc(NS=2, L=DEPTH, dbg=False, phases="ABCD", mixers="abcd"):
    nc = bass.Bass("TRN2", target_bir_lowering=False)
    NT = NS * S

    def din(name, shape, dt=F32):
        return nc.dram_tensor(name, list(shape), dt, kind="ExternalInput").ap()

    def dscr(name, shape, dt=BF16):
        kind = "ExternalOutput" if dbg else "Internal"
        return nc.dram_tensor(name, list(shape), dt, kind=kind).ap()

    xT_in = din("xT", [D, NT])
    t5_in = din("t5", [32, 16])
    gmix_in = din("gmix", [128, L * 8])
    gffn_in = din("gffn", [128, L * 8])
    gfin_in = din("gfin", [128, 8])
    gq_in = din("gq", [128, L * 2])
    gkv_in = din("gkv", [128, L])
    cw_in = din("cw", [128, L * NFF * 3])
    cb_in = din("cb", [128, L * NFF])
    snk_in = din("snk", [128, L * 4])
    nab_in = din("nab", [L, 31, 60])
    w_in_in = din("w_in", [L, D, INC])
    w_uq_in = din("w_uq", [L, 256, 384])
    w_ukv_in = din("w_ukv", [L, 128, 512])
    w_gate_in = din("w_gate", [L, 4, D, D])
    w_br_in = din("w_branch", [L, 4, 256, D])
    w_out_in = din("w_out", [L, D, D])
    w_fg_in = din("w_ffn_gate", [L, D, DFF])
    w_fu_in = din("w_ffn_up", [L, D, DFF])
    w_fd_in = din("w_ffn_down", [L, DFF, D])
    c_ohv = din("c_ohv", [32, RT])
    c_msk = din("c_msk", [16, RT])
    c_jf = din("c_jf", [128, 128])
    c_ohc = din("c_ohc", [31, 4096])
    c_maskc = din("c_maskc", [128, 64])
    c_cos = din("c_cos", [32, S])
    c_sin = din("c_sin", [32, S])

    outT = nc.dram_tensor("outT", [D, NT], F32, kind="ExternalOutput").ap()

    evec_d = dscr("evec_d", [16, RT], F32)
    mcol_d = dscr("mcol_d", [60, 64, 64], F32)
    hT_d = dscr("hT_d", [D, NT])
    QA_d = dscr("QA_d", [3, 2, 128, NT])
    KA_d = dscr("KA_d", [3, 2, 128, NT])
    VA_d = dscr("VA_d", [NT, 768])
    QB_d = dscr("QB_d", [4, 96, NT])
    KB_d = dscr("KB_d", [4, 96, NT])
    VB_d = dscr("VB_d", [NT, 256])
    QC_d = dscr("QC_d", [2, 128, NT])
    KC_d = dscr("KC_d", [2, 128, NT])
    VC_d = dscr("VC_d", [NT, 256])
    QD_d = dscr("QD_d", [2, 128, NT])
    KD_d = dscr("KD_d", [128, NT])
    VD_d = dscr("VD_d", [NT, 128])
    yT_d = dscr("yT_d", [4, 256, NT])
    xa_d = dscr("xa_d", [D, NT], F32)
    xb_d = dscr("xb_d", [D, NT], F32)

    st = ExitStack()
    P = Prog(nc)
    big = st.enter_context(nc.sbuf_tensor("big", [128, SB_BYTES], U8))
    psb = [st.enter_context(nc.psum_tensor("ps%d" % i, [128, 512], F32))[:, :] for i in range(8)]

    class Arena:
        def __init__(self, base, limit):
            self.off = base
            self.limit = limit

        def tile(self, shape, dt):
            n = 1
            for v in shape:
                n *= v
            nb = n * (4 if dt == F32 else 2)
            nb_al = (nb + 63) // 64 * 64
            assert self.off + nb_al <= self.limit, ("SBUF arena overflow", self.off, nb_al, self.limit)
            ap = big[:, self.off:self.off + nb].bitcast(dt)
            self.off += nb_al
            if len(shape) == 2:
                ap = ap.rearrange("p (a b) -> p a b", a=shape[0])
            elif len(shape) == 3:
                ap = ap.rearrange("p (a b c) -> p a b c", a=shape[0], b=shape[1])
            return ap

    pa = Arena(0, 16 * 1024)
    gmix = pa.tile([L * 8], F32)
    gffn = pa.tile([L * 8], F32)
    gfin = pa.tile([8], F32)
    gq = pa.tile([L * 2], F32)
    gkv = pa.tile([L], F32)
    cw = pa.tile([L * NFF * 3], F32)
    cb = pa.tile([L * NFF], F32)
    snk = pa.tile([L * 4], F32)
    epsc = pa.tile([1], F32)
    ones_bf = pa.tile([128], BF16)
    jf_bf = pa.tile([128], BF16)
    EB_A = pa.tile([24, 128], BF16)
    EB_D = pa.tile([12, 128], BF16)
    PBASE = pa.off

    pscnt = [0, 0]

    psn_pool = [6]

    def psum():
        i = pscnt[0] % psn_pool[0]
        pscnt[0] += 1
        return psb[i], "ps%d" % i

    def psum_acc():
        i = 6 + pscnt[1] % 2
        pscnt[1] += 1
        return psb[i], "ps%d" % i

    def dma(eng, out, in_, reads, writes, sem, partial=False):
        P.op(eng, lambda e: e.dma_start(out=out, in_=in_), reads=reads, writes=writes,
             sem=sem, partial=partial)

    def setup():
        ar = Arena(PBASE, SB_BYTES)
        for i, (t, src) in enumerate([(gmix, gmix_in), (gffn, gffn_in), (gfin, gfin_in), (gq, gq_in),
                                      (gkv, gkv_in), (cw, cw_in), (cb, cb_in), (snk, snk_in)]):
            dma("sp", t, src, [], ["sv%d" % i], "sv%d" % i)
        dma("pool", jf_bf, c_jf, [], ["jf"], "jf")
        P.op("dve", lambda e: e.memset(ones_bf, 1.0), writes=["ones"])
        P.op("dve", lambda e: e.memset(epsc, EPS), writes=["epsc"])
        P.op("act", lambda e: e.activation(out=snk, in_=snk, func=AF.Exp), reads=["sv7"], writes=["sv7"])
        t5f = ar.tile([16], F32)
        t5hi = ar.tile([16], BF16)
        t5hf = ar.tile([16], F32)
        t5lo = ar.tile([16], BF16)
        ohv = ar.tile([RT], BF16)
        mskt = ar.tile([RT], F32)
        evec = ar.tile([RT], F32)
        hk = ar.tile([36, 128], BF16)
        dma("sp", t5f[0:32], t5_in, [], ["t5f"], "t5f")
        dma("pool", ohv[0:32], c_ohv, [], ["ohv"], "ohv")
        dma("sp", mskt[0:16], c_msk, [], ["mskt"], "mskt")
        P.op("dve", lambda e: e.tensor_copy(t5hi[0:32], t5f[0:32]), reads=["t5f"], writes=["t5hi"])
        P.op("dve", lambda e: e.tensor_copy(t5hf[0:32], t5hi[0:32]), reads=["t5hi"], writes=["t5hf"])
        P.op("dve", lambda e: e.tensor_tensor(t5lo[0:32], t5f[0:32], t5hf[0:32], ALU.subtract),
             reads=["t5f", "t5hf"], writes=["t5lo"])
        ncol = 415
        for k in range(4):
            ps, pk = psum()
            c0 = k * ncol
            P.op("pe", lambda e, ps=ps, c0=c0: e.matmul(ps[0:16, 0:ncol], t5hi[0:32, :], ohv[0:32, c0:c0 + ncol],
                                                        start=True, stop=False),
                 reads=["t5hi", "ohv"], writes=[pk])
            P.op("pe", lambda e, ps=ps, c0=c0: e.matmul(ps[0:16, 0:ncol], t5lo[0:32, :], ohv[0:32, c0:c0 + ncol],
                                                        start=False, stop=True),
                 reads=["t5lo", "ohv"], writes=[pk])
            P.op("act", lambda e, ps=ps, c0=c0: e.activation(out=evec[0:16, c0:c0 + ncol], in_=ps[0:16, 0:ncol],
                                                             func=AF.Exp),
                 reads=[pk], writes=["evec"], partial=True)
        P.op("dve", lambda e: e.tensor_tensor(evec[0:16], evec[0:16], mskt[0:16], ALU.mult),
             reads=["evec", "mskt"], writes=["evec"])
        dma("sp", evec_d, evec[0:16], ["evec"], [], "evst")
        P.barrier()
        tiles = []
        for h in range(4):
            for d in (-1, 0, 1):
                tiles.append((12 + h, 0 + d * 128 + 128))
        for g in range(3):
            for h in range(4):
                for ab in range(2):
                    tiles.append((4 * g + h, RT_D + g * RT_A + (0 if ab == 0 else 128)))
        for t, (row, u0) in enumerate(tiles):
            src = bass.AP(evec_d.tensor, row * RT + u0, [[1, 128], [1, 128]])
            dma("pool", hk[:, t, :], src, [], ["hk%d" % t], "hk%d" % t)
        for b in range(9):
            ps, pk = psum()
            for q in range(4):
                t = b * 4 + q
                P.op("pe", lambda e, ps=ps, t=t, q=q: e.matmul(ps[:, q * 128:(q + 1) * 128], hk[:, t, :], jf_bf,
                                                               start=True, stop=True),
                     reads=["hk%d" % t, "jf"], writes=[pk])
            if b < 3:
                dst = EB_D[:, b * 4:(b + 1) * 4, :]
            else:
                dst = EB_A[:, (b - 3) * 4:(b - 2) * 4, :]
            P.op("dve", lambda e, ps=ps, dst=dst: e.tensor_copy(dst, ps.rearrange("p (a b) -> p a b", a=4)),
                 reads=[pk], writes=["EB"], partial=True)
        P.barrier()

    def norm_stages(xt, xkey, nch, n, gcol, sq, sqkey, rt, rtkey, rstd, rstdkey, hT, hkey, feat):
        box = {}

        def s1():
            P.op("act", lambda e: e.activation(out=sq, in_=xt, func=AF.Square), reads=[xkey], writes=[sqkey])

        def s2():
            ps, pk = psum()
            box["ps"] = (ps, pk)
            for c in range(nch):
                P.op("pe", lambda e, c=c: e.matmul(ps[:, 0:n], ones_bf, sq[:, c, :], start=(c == 0), stop=(c == nch - 1)),
                     reads=[sqkey, "ones"], writes=[pk])

        def s3():
            ps, pk = box["ps"]
            P.op("act", lambda e: e.activation(out=rt, in_=ps[:, 0:n], func=AF.Ln, scale=1.0 / feat, bias=epsc),
                 reads=[pk, "epsc"], writes=[rtkey])
            P.op("act", lambda e: e.activation(out=rstd, in_=rt, func=AF.Exp, scale=-0.5), reads=[rtkey],
                 writes=[rstdkey])
            for c in range(nch):
                P.op("dve", lambda e, c=c: e.scalar_tensor_tensor(hT[:, c, :], xt[:, c, :], gcol(c), rstd,
                                                                  ALU.mult, ALU.mult),
                     reads=[xkey, rstdkey], writes=[hkey], partial=True)

        return s1, s2, s3

    def rms_norm(*a):
        for st_ in norm_stages(*a):
            st_()

    def phase_a(l, x_src):
        psn_pool[0] = 8
        ar = Arena(PBASE, SB_BYTES)
        w_in = ar.tile([8, INC], BF16)
        wkrot = ar.tile([8, 96], BF16)
        wuq = ar.tile([2, 384], BF16)
        wuqr = ar.tile([2, 4, 96], BF16)
        wukv = ar.tile([512], BF16)
        xt = [ar.tile([8, 512], F32) for _ in range(2)]
        sq = ar.tile([8, 512], BF16)
        rt = ar.tile([512], F32)
        rstd = ar.tile([512], F32)
        hT = [ar.tile([8, 512], BF16) for _ in range(2)]
        cs = [ar.tile([2, 512], F32) for _ in range(1)]
        NSTG = 4
        stg = [ar.tile([512], BF16) for _ in range(NSTG)]
        vst = [ar.tile([4, 1152], BF16) for _ in range(1)]
        cq = ar.tile([2, 512], F32)
        cqsq = ar.tile([2, 512], BF16)
        cqn = ar.tile([2, 512], BF16)
        ckv = ar.tile([1, 512], F32)
        ckvsq = ar.tile([1, 512], BF16)
        ckvn = ar.tile([1, 512], BF16)
        rt2 = ar.tile([512], F32)
        rs2 = ar.tile([512], F32)
        rt3 = ar.tile([512], F32)
        rs3 = ar.tile([512], F32)
        kr = ar.tile([512], F32)
        t1 = ar.tile([512], F32)
        t2 = ar.tile([512], F32)
        qst = [ar.tile([512], BF16) for _ in range(2)]
        kst = [ar.tile([512], BF16) for _ in range(2)]
        vbst = [ar.tile([4, 256], BF16) for _ in range(1)]

        for c in range(8):
            dma("pool", w_in[:, c, :], w_in_in[l, c * 128:(c + 1) * 128, :], [], ["w_in%d" % c], "w_in%d" % c)
        dma("pool", wuq, w_uq_in[l].rearrange("(c p) n -> p c n", p=128), [], ["wuq"], "wuq")
        dma("pool", wukv, w_ukv_in[l], [], ["wukv"], "wukv")
        allw = ["w_in%d" % c for c in range(8)]
        P.op("dve", lambda e: e.memset(wkrot, 0.0), writes=["wkrot"])
        P.op("dve", lambda e: e.tensor_scalar(wkrot[:, :, 64:80], w_in[:, :, 2704:2720], -1.0, 0.0, ALU.mult, ALU.add),
             reads=allw, writes=["wkrot"])
        P.op("dve", lambda e: e.tensor_copy(wkrot[:, :, 80:96], w_in[:, :, 2688:2704]), reads=allw, writes=["wkrot"])
        P.op("dve", lambda e: e.memset(wuqr, 0.0), writes=["wuqr"])
        for h in range(4):
            P.op("dve", lambda e, h=h: e.tensor_scalar(wuqr[:, :, h, 64:80], wuq[:, :, h * 96 + 80:h * 96 + 96],
                                                       -1.0, 0.0, ALU.mult, ALU.add), reads=["wuq"], writes=["wuqr"])
            P.op("dve", lambda e, h=h: e.tensor_copy(wuqr[:, :, h, 80:96], wuq[:, :, h * 96 + 64:h * 96 + 80]),
                 reads=["wuq"], writes=["wuqr"])

        tiles = [(s, T) for s in range(NS) for T in range(8)]

        def load(idx):
            s, T = tiles[idx]
            b = idx % 2
            t0 = s * S + T * 512
            dma("sp", xt[b], x_src.rearrange("(c p) n -> p c n", p=128)[:, :, t0:t0 + 512], [], ["xt%d" % b],
                "xt%d" % b)

        evac_rr = [0]

        def evac(dst, src, reads, writes, partial=False):
            evac_rr[0] += 1
            if evac_rr[0] % 2:
                P.op("act", lambda e: e.copy(dst, src), reads=reads, writes=writes, partial=partial)
            else:
                P.op("dve", lambda e: e.tensor_copy(dst, src), reads=reads, writes=writes, partial=partial)

        stg_rr = [0]

        def norm_a(idx):
            s_, T_ = tiles[idx]
            b_ = idx % 2
            t0_ = s_ * S + T_ * 512
            rms_norm(xt[b_], "xt%d" % b_, 8, 512, lambda c: gmix[:, l * 8 + c:l * 8 + c + 1], sq, "sq", rt, "rt",
                     rstd, "rstd", hT[b_], "hT%d" % b_, float(D))
            dma("pool", hT_d.rearrange("(c p) n -> p c n", p=128)[:, :, t0_:t0_ + 512], hT[b_], ["hT%d" % b_], [],
                "sthT%d" % b_)

        load(0)
        if len(tiles) > 1:
            load(1)
        norm_a(0)
        for idx, (s, T) in enumerate(tiles):
            b = idx % 2
            t0 = s * S + T * 512
            if idx + 1 < len(tiles):
                norm_a(idx + 1)
            if idx + 2 < len(tiles):
                load(idx + 2)
            dma("sp", cs[0][64:96, 0, :], c_cos[:, T * 512:(T + 1) * 512], [], ["cs0"], "cs0", partial=True)
            dma("sp", cs[0][64:96, 1, :], c_sin[:, T * 512:(T + 1) * 512], [], ["cs0"], "cs0", partial=True)
            xk, hk_ = "xt%d" % b, "hT%d" % b

            def fm_chunk(col, M, w=None, wkey=None):
                ps, pk = psum()
                for c in range(8):
                    if w is None:
                        lhsT = w_in[:, c, col:col + M]
                        rk = "w_in%d" % c
                    else:
                        lhsT = w[:, c, col:col + M]
                        rk = wkey
                    P.op("pe", lambda e, c=c, lhsT=lhsT: e.matmul(ps[0:M, :], lhsT, hT[b][:, c, :],
                                                                  start=(c == 0), stop=(c == 7)),
                         reads=[rk, hk_], writes=[pk])
                return ps, pk

            def out_chunk(ps, pk, dst_dram, dil=1):
                i = stg_rr[0] % NSTG
                stg_rr[0] += 1
                sk = "stg%d" % i
                if dil == 1:
                    evac(stg[i], ps, [pk], [sk])
                    dma("pool", dst_dram[:, t0:t0 + 512], stg[i], [sk], [], "st" + sk)
                else:
                    J = 512 // dil
                    evac(stg[i].rearrange("p (r j) -> p j r", r=dil), ps.rearrange("p (j r) -> p j r", r=dil),
                         [pk], [sk])
                    Lg = S // dil
                    dst = dst_dram[:, s * S:(s + 1) * S].rearrange("p (r j) -> p r j", r=dil)[:, :, T * J:(T + 1) * J]
                    dma("pool", dst, stg[i].rearrange("p (r j) -> p r j", r=dil), [sk], [], "st" + sk)

            for c2 in range(2):
                ps, pk = fm_chunk(2304 + c2 * 128, 128)
                evac(cq[:, c2, :], ps, [pk], ["cq"], partial=True)
            ps, pk = fm_chunk(2560, 128)
            evac(ckv[:, 0, :], ps, [pk], ["ckv"])
            nq = norm_stages(cq, "cq", 2, 512, lambda c: gq[:, l * 2 + c:l * 2 + c + 1], cqsq, "cqsq", rt2, "rt2",
                             rs2, "rs2", cqn, "cqn", 256.0)
            nkv = norm_stages(ckv, "ckv", 1, 512, lambda c: gkv[:, l:l + 1], ckvsq, "ckvsq", rt3, "rt3", rs3, "rs3",
                              ckvn, "ckvn", 128.0)
            nq[0]()
            nkv[0]()
            psk, pkk = fm_chunk(2624, 96)
            psr, pkr = fm_chunk(0, 96, w=wkrot, wkey="wkrot")
            csk = "cs0"
            P.op("dve", lambda e: e.tensor_tensor(t1[64:96], psk[64:96, :], cs[0][64:96, 0, :], ALU.mult),
                 reads=[pkk, csk], writes=["t1"])
            P.op("dve", lambda e: e.tensor_tensor(t2[64:96], psr[64:96, :], cs[0][64:96, 1, :], ALU.mult),
                 reads=[pkr, csk], writes=["t2"])
            P.op("dve", lambda e: e.tensor_tensor(kr[64:96], t1[64:96], t2[64:96], ALU.add),
                 reads=["t1", "t2"], writes=["kr"])
            for g in range(3):
                for hp in range(1):
                    ps, pk = fm_chunk((g * 4 + hp * 2) * 64, 128)
                    out_chunk(ps, pk, QA_d[g, hp], DIL[g])
                    ps, pk = fm_chunk(768 + (g * 4 + hp * 2) * 64, 128)
                    out_chunk(ps, pk, KA_d[g, hp], DIL[g])
            nq[1]()
            nq[2]()
            nkv[1]()
            nkv[2]()
            for g in range(3):
                for hp in range(1, 2):
                    ps, pk = fm_chunk((g * 4 + hp * 2) * 64, 128)
                    out_chunk(ps, pk, QA_d[g, hp], DIL[g])
                    ps, pk = fm_chunk(768 + (g * 4 + hp * 2) * 64, 128)
                    out_chunk(ps, pk, KA_d[g, hp], DIL[g])
            for h in range(4):
                hb = h % 2
                psq, pkq = psum()
                psq2, pkq2 = psum()
                for c2 in range(2):
                    P.op("pe", lambda e, c2=c2: e.matmul(psq[0:96, :], wuq[:, c2, h * 96:(h + 1) * 96], cqn[:, c2, :],
                                                         start=(c2 == 0), stop=(c2 == 1)),
                         reads=["wuq", "cqn"], writes=[pkq])
                for c2 in range(2):
                    P.op("pe", lambda e, c2=c2: e.matmul(psq2[0:96, :], wuqr[:, c2, h, :], cqn[:, c2, :],
                                                         start=(c2 == 0), stop=(c2 == 1)),
                         reads=["wuqr", "cqn"], writes=[pkq2])
                qk = "qst%d" % hb
                P.op("dve", lambda e: e.tensor_copy(qst[hb][0:64], psq[0:64, :]), reads=[pkq], writes=[qk])
                P.op("dve", lambda e: e.tensor_tensor(t1[64:96], psq[64:96, :], cs[0][64:96, 0, :], ALU.mult),
                     reads=[pkq, csk], writes=["t1"])
                P.op("dve", lambda e: e.tensor_tensor(t2[64:96], psq2[64:96, :], cs[0][64:96, 1, :], ALU.mult),
                     reads=[pkq2, csk], writes=["t2"])
                P.op("dve", lambda e: e.tensor_tensor(qst[hb][64:96], t1[64:96], t2[64:96], ALU.add),
                     reads=["t1", "t2", qk], writes=[qk], partial=True)
                dma("pool", QB_d[h][:, t0:t0 + 512], qst[hb][0:96], [qk], [], "st" + qk)
                psn, pkn = psum()
                P.op("pe", lambda e: e.matmul(psn[0:64, :], wukv[:, h * 128:h * 128 + 64], ckvn[:, 0, :],
                                              start=True, stop=True), reads=["wukv", "ckvn"], writes=[pkn])
                kk = "kst%d" % hb
                evac(kst[hb][0:64], psn[0:64, :], [pkn], [kk])
                P.op("pool", lambda e: e.tensor_copy(kst[hb][64:96], kr[64:96]), reads=["kr", kk], writes=[kk],
                     partial=True)
                dma("pool", KB_d[h][:, t0:t0 + 512], kst[hb][0:96], [kk], [], "st" + kk)
            vb = 0
            vbk = "vbst%d" % vb
            wv = wukv.rearrange("p (h x) -> p h x", h=4)[:, :, 64:128]
            for tb in range(4):
                ps, pk = psum()
                P.op("pe", lambda e, tb=tb, ps=ps: e.matmul(ps[:, 0:256].rearrange("p (h x) -> p h x", h=4),
                                                            ckvn[:, 0, tb * 128:(tb + 1) * 128], wv,
                                                            start=True, stop=True),
                     reads=["wukv", "ckvn"], writes=[pk])
                evac(vbst[vb][:, tb, :], ps[:, 0:256], [pk], [vbk], partial=True)
            dma("pool", VB_d[t0:t0 + 512, :].rearrange("(tb p) f -> p tb f", p=128), vbst[vb], [vbk], [], "st" + vbk)

            for hp in range(2):
                ps, pk = fm_chunk(2720 + hp * 128, 128)
                out_chunk(ps, pk, QC_d[hp])
                ps, pk = fm_chunk(2976 + hp * 128, 128)
                out_chunk(ps, pk, KC_d[hp])
                ps, pk = fm_chunk(3488 + hp * 128, 128)
                out_chunk(ps, pk, QD_d[hp])
            ps, pk = fm_chunk(3744, 128)
            out_chunk(ps, pk, KD_d)

            vk = "vst%d" % vb
            groups = [(1536, 512, 0), (2048, 256, 512), (3232, 256, 768), (3872, 128, 1024)]
            for tb in range(4):
                for (col, n, so) in groups:
                    ps, pk = psum()
                    for c in range(8):
                        P.op("pe", lambda e, c=c, ps=ps, col=col, n=n: e.matmul(
                            ps[:, 0:n], hT[b][:, c, tb * 128:(tb + 1) * 128], w_in[:, c, col:col + n],
                            start=(c == 0), stop=(c == 7)), reads=["w_in%d" % c, hk_], writes=[pk])
                    evac(vst[vb][:, tb, so:so + n], ps[:, 0:n], [pk], [vk], partial=True)
            dma("pool", VA_d[t0:t0 + 512, :].rearrange("(tb p) f -> p tb f", p=128), vst[vb][:, :, 0:768], [vk], [],
                "stva%d" % vb)
            dma("pool", VC_d[t0:t0 + 512, :].rearrange("(tb p) f -> p tb f", p=128), vst[vb][:, :, 768:1024], [vk], [],
                "stvc%d" % vb)
            dma("pool", VD_d[t0:t0 + 512, :].rearrange("(tb p) f -> p tb f", p=128), vst[vb][:, :, 1024:1152], [vk], [],
                "stvd%d" % vb)
        P.barrier()

    def build_ebc(l, ar):
        EBC = ar.tile([4 * NCT, 128], BF16)
        sub = Arena(ar.off, SB_BYTES)
        nbf = sub.tile([60], F32)
        nbhi = sub.tile([60], BF16)
        nbhf = sub.tile([60], F32)
        nblo = sub.tile([60], BF16)
        ohc = sub.tile([4096], BF16)
        mcs = sub.tile([4096], F32)
        mcolS = sub.tile([60, 64], BF16)
        mk2 = sub.tile([64], BF16)
        dma("sp", nbf[0:31], nab_in[l], [], ["nbf"], "nbf")
        dma("pool", ohc[0:31], c_ohc, [], ["ohc"], "ohc")
        dma("pool", mk2, c_maskc, [], ["mk2"], "mk2")
        P.op("dve", lambda e: e.tensor_copy(nbhi[0:31], nbf[0:31]), reads=["nbf"], writes=["nbhi"])
        P.op("dve", lambda e: e.tensor_copy(nbhf[0:31], nbhi[0:31]), reads=["nbhi"], writes=["nbhf"])
        P.op("dve", lambda e: e.tensor_tensor(nblo[0:31], nbf[0:31], nbhf[0:31], ALU.subtract),
             reads=["nbf", "nbhf"], writes=["nblo"])
        for k in range(8):
            ps, pk = psum()
            P.op("pe", lambda e, ps=ps, k=k: e.matmul(ps[0:60, :], nbhi[0:31, :], ohc[0:31, k * 512:(k + 1) * 512],
                                                      start=True, stop=False), reads=["nbhi", "ohc"], writes=[pk])
            P.op("pe", lambda e, ps=ps, k=k: e.matmul(ps[0:60, :], nblo[0:31, :], ohc[0:31, k * 512:(k + 1) * 512],
                                                      start=False, stop=True), reads=["nblo", "ohc"], writes=[pk])
            P.op("act", lambda e, ps=ps, k=k: e.activation(out=mcs[0:60, k * 512:(k + 1) * 512], in_=ps[0:60, :],
                                                           func=AF.Exp), reads=[pk], writes=["mcs"], partial=True)
        dma("sp", mcol_d.rearrange("m a b -> m (a b)"), mcs[0:60], ["mcs"], [], "stmcs")
        P.barrier()
        src = mcol_d.rearrange("m kc qc -> kc m qc")
        dma("pool", mcolS[0:64], src, [], ["mcolS"], "mcolS", partial=True)
        dma("pool", mcolS[64:128], src, [], ["mcolS"], "mcolS", partial=True)
        P.op("dve", lambda e: e.tensor_tensor(mcolS, mcolS, mk2.unsqueeze(1).to_broadcast([128, 60, 64]), ALU.mult),
             reads=["mcolS", "mk2"], writes=["mcolS"])
        P.op("pool", lambda e: e.memset(EBC, 0.0), writes=["EBC"])
        rr = 0
        for h in range(4):
            for ti, sg in enumerate(C_SIGS):
                k = 0
                for krl in range(2):
                    for qrl in range(2):
                        dr = sg[k]
                        k += 1
                        if dr is None:
                            continue
                        dst = EBC[krl * 64:(krl + 1) * 64, h * NCT + ti, qrl * 64:(qrl + 1) * 64]
                        srcv = mcolS[krl * 64:(krl + 1) * 64, h * 15 + dr, :]
                        eng = ("dve", "pool")[rr % 2]
                        rr += 1
                        P.op(eng, lambda e, dst=dst, srcv=srcv: e.tensor_copy(dst, srcv), reads=["mcolS", "EBC"],
                             writes=["EBC"], partial=True)
        P.barrier()
        return EBC

    def phase_b(l):
        psn_pool[0] = 6
        ar0 = Arena(PBASE, SB_BYTES)
        EBC = build_ebc(l, ar0)
        base = ar0.off
        NPE = 12
        pexp = [ar0.tile([512], BF16) for _ in range(NPE)]
        pt = [ar0.tile([512], BF16) for _ in range(NPE)]
        rec = [ar0.tile([512], F32) for _ in range(2)]
        yst = [ar0.tile([2, 512], BF16) for _ in range(2)]
        wbase = ar0.off
        cnt = {"pe": 0, "rec": 0, "mul": 0}

        def score_slot(qbs, eb, scale, ebkey, zero=()):
            ps, pk = psum()
            for q, ent in enumerate(qbs):
                if ent is None:
                    continue
                kT, qT, lo, hi, rds = ent
                if lo == 0 and hi == 128:
                    P.op("pe", lambda e, kT=kT, qT=qT, q=q: e.matmul(ps[:, q * 128:(q + 1) * 128], kT, qT,
                                                                     start=True, stop=True), reads=rds, writes=[pk])
                else:
                    pb = kT.base_partition()
                    P.op("pe", lambda e, kT=kT, qT=qT, q=q, lo=lo, hi=hi, pb=pb: e.matmul(
                        ps[lo:hi, q * 128:(q + 1) * 128], kT, qT, start=True, stop=True, tile_position=(pb, lo)),
                        reads=rds, writes=[pk])
            i = cnt["pe"] % NPE
            cnt["pe"] += 1
            if eb is None:
                P.op("act", lambda e: e.activation(out=pt[i], in_=ps, func=AF.Exp, scale=scale), reads=[pk],
                     writes=["pt%d" % i])
                return pt[i], "pt%d" % i
            P.op("act", lambda e: e.activation(out=pexp[i], in_=ps, func=AF.Exp, scale=scale), reads=[pk],
                 writes=["pexp%d" % i])
            eng = ("dve", "pool")[cnt["mul"] % 2]
            cnt["mul"] += 1
            if isinstance(eb, list):
                for q, ebq in enumerate(eb):
                    if ebq is None or qbs[q] is None:
                        continue
                    P.op(eng, lambda e, q=q, ebq=ebq: e.tensor_tensor(pt[i][:, q * 128:(q + 1) * 128],
                                                                      pexp[i][:, q * 128:(q + 1) * 128], ebq, ALU.mult),
                         reads=["pexp%d" % i, ebkey], writes=["pt%d" % i], partial=(q > 0))
            else:
                P.op(eng, lambda e: e.tensor_tensor(pt[i].rearrange("p (a b) -> p a b", a=4),
                                                    pexp[i].rearrange("p (a b) -> p a b", a=4),
                                                    eb.unsqueeze(1).to_broadcast([128, 4, 128]), ALU.mult),
                     reads=["pexp%d" % i, ebkey], writes=["pt%d" % i])
                for (q, zlo, zhi) in zero:
                    P.op(eng, lambda e, q=q, zlo=zlo, zhi=zhi: e.memset(pt[i][zlo:zhi, q * 128:(q + 1) * 128], 0.0),
                         writes=["pt%d" % i], partial=True)
            return pt[i], "pt%d" % i

        def pv(ot, otk, pts, vents):
            for q in range(4):
                lst = [(sl, vents[sl][q]) for sl in range(len(pts)) if vents[sl][q] is not None]
                for n, (sl, (va, lo, hi, rds)) in enumerate(lst):
                    ptile, ptk = pts[sl]
                    P.op("pe", lambda e, va=va, lo=lo, hi=hi, ptile=ptile, q=q, n=n, last=len(lst) - 1: e.matmul(
                        ot[:, q * 128:(q + 1) * 128], va, ptile[lo:hi, q * 128:(q + 1) * 128],
                        start=(n == 0), stop=(n == last)), reads=rds + [ptk], writes=[otk])

        def normalize(ot, otk, dst, dstkey, addcol=None, partial=True):
            i = cnt["rec"] % 2
            cnt["rec"] += 1
            rk = "rec%d" % i
            if addcol is not None:
                P.op("act", lambda e: e.activation(out=rec[i][0:64], in_=ot[64:128, :], func=AF.Ln, bias=addcol),
                     reads=[otk], writes=[rk])
            else:
                P.op("act", lambda e: e.activation(out=rec[i][0:64], in_=ot[64:128, :], func=AF.Ln),
                     reads=[otk], writes=[rk])
            P.op("act", lambda e: e.activation(out=rec[i][0:64], in_=rec[i][0:64], func=AF.Exp, scale=-1.0),
                 reads=[rk], writes=[rk])
            P.op("dve", lambda e: e.tensor_tensor(dst, ot[0:64, :], rec[i][0:64], ALU.mult), reads=[otk, rk],
                 writes=[dstkey], partial=partial)

        def store_y(m, s, c2, ch, ysti, yk):
            dma("pool", yT_d[m, c2 * 128:(c2 + 1) * 128, s * S + ch * 512:s * S + (ch + 1) * 512], yst[ysti][:, 0, :],
                [yk], [], "st" + yk)

        ycnt = [0]

        def run_units(units, lag=1):
            for ui in range(len(units) + lag):
                if ui < len(units):
                    units[ui][0]()
                if ui >= lag:
                    units[ui - lag][1]()

        for s in range(NS):
            if "d" in mixers:
                ar = Arena(wbase, SB_BYTES)
                QT = [ar.tile([S], BF16) for _ in range(2)]
                KT = [ar.tile([S], BF16) for _ in range(2)]
                VG = ar.tile([32, 2, 128], BF16)
                for hp in range(2):
                    dma("sp", QT[hp], QD_d[hp][:, s * S:(s + 1) * S], [], ["QT%d" % hp], "QT%d" % hp)
                    for hb in range(2):
                        dma("sp", KT[hp][hb * 64:(hb + 1) * 64], KD_d[hp * 64:(hp + 1) * 64, s * S:(s + 1) * S], [],
                            ["KT%d" % hp], "KT%d" % hp, partial=True)
                P.op("pool", lambda e: e.memset(VG[:, :, :, 64:128], 1.0), writes=["VG"])
                for g_ in range(2):
                    dma("sp", VG[:, :, g_, 0:64],
                        VD_d[s * S:(s + 1) * S, g_ * 64:(g_ + 1) * 64].rearrange("(kb p) x -> p kb x", p=128),
                        [], ["VG"], "VG", partial=True)
                units = []
                for hp in range(2):
                    for ch in range(8):
                        yi = ycnt[0] % 2
                        ycnt[0] += 1
                        yk = "yst%d" % yi
                        for hb in range(2):
                            h = hp * 2 + hb
                            kvh = h // 2
                            box = {}

                            def st1(hp=hp, ch=ch, hb=hb, h=h, kvh=kvh, box=box):
                                pts = []
                                vents = []
                                for d in (-1, 0, 1):
                                    qbs = []
                                    vv = []
                                    for q in range(4):
                                        i = ch * 4 + q
                                        j = i + d
                                        if j < 0 or j > 31:
                                            qbs.append(None)
                                            vv.append(None)
                                            continue
                                        qbs.append((KT[hp][hb * 64:(hb + 1) * 64, j * 128:(j + 1) * 128],
                                                    QT[hp][hb * 64:(hb + 1) * 64, i * 128:(i + 1) * 128], 0, 128,
                                                    ["KT%d" % hp, "QT%d" % hp]))
                                        vv.append((VG[:, j, kvh, :], 0, 128, ["VG"]))
                                    pts.append(score_slot(qbs, EB_D[:, h * 3 + d + 1, :], 0.125, "EB"))
                                    vents.append(vv)
                                box["pts"] = pts
                                box["vents"] = vents

                            def st2(hp=hp, ch=ch, hb=hb, h=h, box=box, yi=yi, yk=yk):
                                ot, otk = psum_acc()
                                pv(ot, otk, box["pts"], box["vents"])
                                normalize(ot, otk, yst[yi][hb * 64:(hb + 1) * 64, 0, :], yk,
                                          addcol=snk[64:128, l * 4 + h:l * 4 + h + 1], partial=(hb == 1))
                                if hb == 1:
                                    store_y(3, s, hp, ch, yi, yk)

                            units.append((st1, st2))
                run_units(units)
                P.barrier()
            if "c" in mixers:
                ar = Arena(wbase, SB_BYTES)
                QT = [ar.tile([S], BF16) for _ in range(2)]
                KT = [ar.tile([S], BF16) for _ in range(2)]
                VG = ar.tile([32, 4, 128], BF16)
                for hp in range(2):
                    dma("sp", QT[hp], QC_d[hp][:, s * S:(s + 1) * S], [], ["QT%d" % hp], "QT%d" % hp)
                    dma("sp", KT[hp], KC_d[hp][:, s * S:(s + 1) * S], [], ["KT%d" % hp], "KT%d" % hp)
                P.op("pool", lambda e: e.memset(VG[:, :, :, 64:128], 1.0), writes=["VG"])
                for g_ in range(4):
                    dma("sp", VG[:, :, g_, 0:64],
                        VC_d[s * S:(s + 1) * S, g_ * 64:(g_ + 1) * 64].rearrange("(kb p) x -> p kb x", p=128),
                        [], ["VG"], "VG", partial=True)
                units = []
                for hp in range(2):
                    for ch in range(8):
                        yi = ycnt[0] % 2
                        ycnt[0] += 1
                        yk = "yst%d" % yi
                        dlist = sorted({j - (ch * 4 + q) for q in range(4) for (j, _) in C_TABLE[ch * 4 + q]})
                        for hb in range(2):
                            h = hp * 2 + hb
                            box = {}

                            def st1(hp=hp, ch=ch, hb=hb, h=h, box=box, dlist=dlist):
                                pts = []
                                vents = []
                                for d in dlist:
                                    qbs = []
                                    vv = []
                                    ebl = []
                                    for q in range(4):
                                        i = ch * 4 + q
                                        j = i + d
                                        ti = dict(C_TABLE[i]).get(j)
                                        if ti is None:
                                            qbs.append(None)
                                            vv.append(None)
                                            ebl.append(None)
                                            continue
                                        qbs.append((KT[hp][hb * 64:(hb + 1) * 64, j * 128:(j + 1) * 128],
                                                    QT[hp][hb * 64:(hb + 1) * 64, i * 128:(i + 1) * 128], 0, 128,
                                                    ["KT%d" % hp, "QT%d" % hp]))
                                        vv.append((VG[:, j, h, :], 0, 128, ["VG"]))
                                        ebl.append(EBC[:, h * NCT + ti, :])
                                    tis = {dict(C_TABLE[ch * 4 + q]).get(ch * 4 + q + d) for q in range(4)}
                                    if len(tis) == 1 and None not in tis:
                                        eb = ebl[0]
                                    else:
                                        eb = ebl
                                    pts.append(score_slot(qbs, eb, 0.125, "EBC"))
                                    vents.append(vv)
                                box["pts"] = pts
                                box["vents"] = vents

                            def st2(hp=hp, ch=ch, hb=hb, box=box, yi=yi, yk=yk):
                                ot, otk = psum_acc()
                                pv(ot, otk, box["pts"], box["vents"])
                                normalize(ot, otk, yst[yi][hb * 64:(hb + 1) * 64, 0, :], yk, partial=(hb == 1))
                                if hb == 1:
                                    store_y(2, s, hp, ch, yi, yk)

                            units.append((st1, st2))
                run_units(units)
                P.barrier()
            if "a" in mixers:
                for hp in range(2):
                    ar = Arena(wbase, SB_BYTES)
                    QT = ar.tile([S], BF16)
                    KT = ar.tile([S + 128], BF16)
                    VG = ar.tile([48, 2, 128], BF16)
                    OA = [ar.tile([S], F32) for _ in range(2)]
                    P.op("pool", lambda e: e.memset(VG, 0.0), writes=["VG"])
                    P.op("pool", lambda e: e.memset(VG[:, :, :, 64:128], 1.0), writes=["VG"])
                    P.op("pool", lambda e: e.memset(KT[:, 0:64], 0.0), writes=["KTpad"])
                    P.op("pool", lambda e: e.memset(KT[:, S + 64:S + 128], 0.0), writes=["KTpad"])
                    for g in range(3):
                        dil = DIL[g]
                        Lg = S // dil
                        nb = Lg // 128
                        dma("sp", QT, QA_d[g, hp][:, s * S:(s + 1) * S], [], ["QT"], "QT")
                        dma("sp", KT[:, 64:64 + S], KA_d[g, hp][:, s * S:(s + 1) * S], [], ["KT"], "KT")
                        vsrc = VA_d[s * S:(s + 1) * S, :].rearrange("(j r) (g x) -> r j g x", r=dil, g=3)
                        for r in range(dil):
                            for m in range(nb + 1):
                                lo = 64 if m == 0 else 0
                                hi = 64 if m == nb else 128
                                j0 = 128 * m - 64 + lo
                                src = vsrc[r, j0:j0 + (hi - lo), g, :].rearrange("j (h x) -> j h x", h=4)[
                                    :, hp * 2:hp * 2 + 2, :]
                                dma("sp", VG[lo:hi, r * (nb + 1) + m, :, 0:64], src, [], ["VG"], "VG", partial=True)
                        nqb = S // 128
                        units = []
                        for ch in range(nqb // 4):
                            for hb in range(2):
                                h = hp * 2 + hb
                                box = {}

                                def st1(ch=ch, hb=hb, h=h, box=box, g=g, Lg=Lg, nb=nb):
                                    pts = []
                                    vents = []
                                    for ab in range(2):
                                        qbs = []
                                        vv = []
                                        zer = []
                                        for q in range(4):
                                            gi = ch * 4 + q
                                            r, i = divmod(gi, nb)
                                            m = i + ab
                                            k0 = 64 + r * Lg + 128 * m - 64
                                            qbs.append((KT[hb * 64:(hb + 1) * 64, k0:k0 + 128],
                                                        QT[hb * 64:(hb + 1) * 64, gi * 128:(gi + 1) * 128], 0, 128,
                                                        ["KT", "KTpad", "QT"]))
                                            vv.append((VG[:, r * (nb + 1) + m, hb, :], 0, 128, ["VG"]))
                                            if m == 0:
                                                zer.append((q, 0, 64))
                                            elif m == nb:
                                                zer.append((q, 64, 128))
                                        pts.append(score_slot(qbs, EB_A[:, (g * 4 + h) * 2 + ab, :], 0.125, "EB",
                                                              zero=zer))
                                        vents.append(vv)
                                    box["pts"] = pts
                                    box["vents"] = vents

                                def st2(ch=ch, hb=hb, box=box, dil=dil, Lg=Lg):
                                    ot, otk = psum_acc()
                                    pv(ot, otk, box["pts"], box["vents"])
                                    if dil == 1:
                                        dst = OA[hb][:, ch * 512:(ch + 1) * 512]
                                        P.op("act", lambda e, dst=dst, ot=ot: e.copy(dst, ot), reads=[otk],
                                             writes=["OA%d" % hb], partial=True)
                                    else:
                                        oav = OA[hb].rearrange("p (j r) -> p r j", r=dil)
                                        pos0 = ch * 512
                                        done = 0
                                        while done < 512:
                                            r, j = divmod(pos0 + done, Lg)
                                            n = min(512 - done, Lg - j)
                                            dst = oav[:, r, j:j + n]
                                            srcv = ot[:, done:done + n]
                                            P.op("dve", lambda e, dst=dst, srcv=srcv: e.tensor_tensor(dst, dst, srcv,
                                                                                                      ALU.add),
                                                 reads=[otk, "OA%d" % hb], writes=["OA%d" % hb], partial=True)
                                            done += n

                                units.append((st1, st2))
                        run_units(units)
                    for ch in range(8):
                        yi = ycnt[0] % 2
                        ycnt[0] += 1
                        yk = "yst%d" % yi
                        for hb in range(2):
                            i = cnt["rec"] % 2
                            cnt["rec"] += 1
                            rk = "rec%d" % i
                            oa = OA[hb][:, ch * 512:(ch + 1) * 512]
                            P.op("act", lambda e, oa=oa, i=i: e.activation(out=rec[i][0:64], in_=oa[64:128], func=AF.Ln),
                                 reads=["OA%d" % hb], writes=[rk])
                            P.op("act", lambda e, i=i: e.activation(out=rec[i][0:64], in_=rec[i][0:64], func=AF.Exp,
                                                                    scale=-1.0), reads=[rk], writes=[rk])
                            P.op("dve", lambda e, oa=oa, i=i, hb=hb, yi=yi: e.tensor_tensor(
                                yst[yi][hb * 64:(hb + 1) * 64, 0, :], oa[0:64], rec[i][0:64], ALU.mult),
                                reads=["OA%d" % hb, rk], writes=[yk], partial=(hb == 1))
                        store_y(0, s, hp, ch, yi, yk)
                    P.barrier()
            if "b" in mixers:
                ar = Arena(wbase, SB_BYTES)
                QT = [ar.tile([S], BF16) for _ in range(2)]
                KT = [ar.tile([S], BF16) for _ in range(2)]
                VG = [ar.tile([32, 128], BF16) for _ in range(2)]
                for bb in range(2):
                    P.op("pool", lambda e, bb=bb: e.memset(VG[bb][:, :, 64:128], 1.0), writes=["VG%d" % bb])

                def load_b(h):
                    bb = h % 2
                    dma("sp", QT[bb][0:96], QB_d[h][:, s * S:(s + 1) * S], [], ["QT%d" % bb], "QT%d" % bb)
                    dma("sp", KT[bb][0:96], KB_d[h][:, s * S:(s + 1) * S], [], ["KT%d" % bb], "KT%d" % bb)
                    dma("sp", VG[bb][:, :, 0:64],
                        VB_d[s * S:(s + 1) * S, h * 64:(h + 1) * 64].rearrange("(kb p) x -> p kb x", p=128),
                        [], ["VG%d" % bb], "VG%d" % bb, partial=True)

                LAG = 3
                work = [(h, ch, kb) for h in range(4) for ch in range(8) for kb in range(32)]
                pend = []
                ots = {}
                load_b(0)
                for wi in range(len(work) + LAG):
                    if wi < len(work):
                        h, ch, kb = work[wi]
                        bb = h % 2
                        if ch == 0 and kb == 0 and h + 1 < 4:
                            pass
                        ps, pk = psum()
                        P.op("pe", lambda e, ps=ps, kb=kb, bb=bb, ch=ch: e.matmul(
                            ps, KT[bb][0:96, kb * 128:(kb + 1) * 128], QT[bb][0:96, ch * 512:(ch + 1) * 512],
                            start=True, stop=True), reads=["KT%d" % bb, "QT%d" % bb], writes=[pk])
                        i = cnt["pe"] % NPE
                        cnt["pe"] += 1
                        P.op("act", lambda e, ps=ps, i=i: e.activation(out=pt[i], in_=ps, func=AF.Exp,
                                                                       scale=96.0 ** -0.5),
                             reads=[pk], writes=["pt%d" % i])
                        pend.append((h, ch, kb, i))
                    if wi >= LAG:
                        h, ch, kb, i = pend.pop(0)
                        bb = h % 2
                        if kb == 0:
                            ots[(h, ch)] = psum_acc()
                        ot, otk = ots[(h, ch)]
                        P.op("pe", lambda e, kb=kb, bb=bb, i=i, ot=ot: e.matmul(
                            ot, VG[bb][:, kb, :], pt[i], start=(kb == 0), stop=(kb == 31)),
                            reads=["VG%d" % bb, "pt%d" % i], writes=[otk])
                        if kb == 31:
                            i2 = cnt["rec"] % 2
                            cnt["rec"] += 1
                            rk = "rec%d" % i2
                            P.op("act", lambda e, ot=ot, i2=i2: e.activation(out=rec[i2][0:64], in_=ot[64:128, :],
                                                                            func=AF.Ln), reads=[otk], writes=[rk])
                            P.op("act", lambda e, i2=i2: e.activation(out=rec[i2][0:64], in_=rec[i2][0:64], func=AF.Exp,
                                                                     scale=-1.0), reads=[rk], writes=[rk])
                            ysi = ycnt[0] % 2
                            ycnt[0] += 1
                            yk = "yst%d" % ysi
                            P.op("dve", lambda e, ot=ot, i2=i2, ysi=ysi: e.tensor_tensor(
                                yst[ysi][0:64, 0, :], ot[0:64, :], rec[i2][0:64], ALU.mult),
                                reads=[otk, rk], writes=[yk])
                            dma("pool", yT_d[1, h * 64:(h + 1) * 64, s * S + ch * 512:s * S + (ch + 1) * 512],
                                yst[ysi][0:64, 0, :], [yk], [], "st" + yk)
                            if ch == 0 and h + 1 < 4:
                                load_b(h + 1)
                P.barrier()

    def phase_c(l, x_src, x_dst):
        psn_pool[0] = 8
        ar = Arena(PBASE, SB_BYTES)
        wg = ar.tile([32, D], BF16)
        wb = ar.tile([8, D], BF16)
        wo = ar.tile([8, D], BF16)
        hT = [ar.tile([8, 512], BF16) for _ in range(2)]
        yT = [ar.tile([8, 512], BF16) for _ in range(2)]
        xt = ar.tile([8, 512], F32)
        sg = [ar.tile([512], F32) for _ in range(2)]
        tm = [ar.tile([512], F32) for _ in range(2)]
        acc = ar.tile([512], F32)
        mg = ar.tile([8, 512], BF16)
        for i in range(4):
            for c in range(8):
                k = "wg%d" % (i * 8 + c)
                dma("pool", wg[:, i * 8 + c, :], w_gate_in[l, i, c * 128:(c + 1) * 128, :], [], [k], k)
        dma("pool", wb, w_br_in[l].rearrange("i (c p) n -> p (i c) n", p=128), [], ["wb"], "wb")
        dma("pool", wo, w_out_in[l].rearrange("(c p) n -> p c n", p=128), [], ["wo"], "wo")
        tiles = [(s, T) for s in range(NS) for T in range(8)]

        def load(idx):
            s, T = tiles[idx]
            b = idx % 2
            t0 = s * S + T * 512
            dma("sp", hT[b], hT_d.rearrange("(c p) n -> p c n", p=128)[:, :, t0:t0 + 512], [], ["hT%d" % b], "hT%d" % b)
            dma("sp", yT[b], yT_d.rearrange("m (c p) n -> p (m c) n", p=128)[:, :, t0:t0 + 512], [], ["yT%d" % b],
                "yT%d" % b)

        load(0)
        k2 = 0
        for idx, (s, T) in enumerate(tiles):
            b = idx % 2
            t0 = s * S + T * 512
            if idx + 1 < len(tiles):
                load(idx + 1)
            dma("sp", xt, x_src.rearrange("(c p) n -> p c n", p=128)[:, :, t0:t0 + 512], [], ["xt"], "xt")
            for oc in range(8):
                for i in range(4):
                    psg, pkg = psum()
                    for c in range(8):
                        P.op("pe", lambda e, c=c, i=i, oc=oc, psg=psg: e.matmul(
                            psg, wg[:, i * 8 + c, oc * 128:(oc + 1) * 128], hT[b][:, c, :],
                            start=(c == 0), stop=(c == 7)), reads=["wg%d" % (i * 8 + c), "hT%d" % b], writes=[pkg])
                    psb_, pkb = psum()
                    for c2 in range(2):
                        P.op("pe", lambda e, c2=c2, i=i, oc=oc, psb_=psb_: e.matmul(
                            psb_, wb[:, i * 2 + c2, oc * 128:(oc + 1) * 128], yT[b][:, i * 2 + c2, :],
                            start=(c2 == 0), stop=(c2 == 1)), reads=["wb", "yT%d" % b], writes=[pkb])
                    j = k2 % 2
                    k2 += 1
                    P.op("act", lambda e, psg=psg, j=j: e.activation(out=sg[j], in_=psg, func=AF.Sigmoid), reads=[pkg],
                         writes=["sg%d" % j])
                    if i == 0:
                        P.op("dve", lambda e, psb_=psb_, j=j: e.tensor_tensor(acc, sg[j], psb_, ALU.mult),
                             reads=["sg%d" % j, pkb], writes=["acc"])
                    else:
                        P.op("dve", lambda e, psb_=psb_, j=j: e.tensor_tensor(tm[j], sg[j], psb_, ALU.mult),
                             reads=["sg%d" % j, pkb], writes=["tm%d" % j])
                        if i < 3:
                            P.op("pool", lambda e, j=j: e.tensor_tensor(acc, acc, tm[j], ALU.add),
                                 reads=["tm%d" % j, "acc"], writes=["acc"])
                        else:
                            P.op("pool", lambda e, j=j, oc=oc: e.tensor_tensor(mg[:, oc, :], acc, tm[j], ALU.add),
                                 reads=["tm%d" % j, "acc"], writes=["mg"], partial=True)
            for oc in range(8):
                ps, pk = psum()
                for c in range(8):
                    P.op("pe", lambda e, c=c, oc=oc, ps=ps: e.matmul(ps, wo[:, c, oc * 128:(oc + 1) * 128], mg[:, c, :],
                                                                     start=(c == 0), stop=(c == 7)),
                         reads=["wo", "mg"], writes=[pk])
                P.op("dve", lambda e, oc=oc, ps=ps: e.tensor_tensor(xt[:, oc, :], xt[:, oc, :], ps, ALU.add),
                     reads=[pk, "xt"], writes=["xt"], partial=True)
            dma("pool", x_dst.rearrange("(c p) n -> p c n", p=128)[:, :, t0:t0 + 512], xt, ["xt"], [], "stxt")
        P.barrier()

    def phase_d(l, x_src, x_dst, final):
        psn_pool[0] = 8
        ar = Arena(PBASE, SB_BYTES)
        wg = ar.tile([8, DFF], BF16)
        wu = ar.tile([8, DFF], BF16)
        wd = ar.tile([NFF, D], BF16)
        TW = 256
        xt = [ar.tile([8, TW + 2], F32) for _ in range(2)]
        sq = ar.tile([8, TW + 2], BF16)
        rt = ar.tile([TW + 2], F32)
        rstd = ar.tile([TW + 2], F32)
        h2s = [ar.tile([8, TW + 2], BF16) for _ in range(2)]
        cv = [ar.tile([TW], F32) for _ in range(2)]
        ge = [ar.tile([TW], F32) for _ in range(2)]
        uT = ar.tile([NFF, TW], BF16)
        sqfin, rtf, rstdf = sq, rt, rstd
        for c in range(8):
            dma("pool", wg[:, c, :], w_fg_in[l, c * 128:(c + 1) * 128, :], [], ["fg%d" % c], "fg%d" % c)
            dma("pool", wu[:, c, :], w_fu_in[l, c * 128:(c + 1) * 128, :], [], ["fu%d" % c], "fu%d" % c)
        for f in range(NFF):
            dma("pool", wd[:, f, :], w_fd_in[l, f * 128:(f + 1) * 128, :], [], ["fd%d" % (f % 4)], "fd%d" % (f % 4),
                partial=True)
        fdk = ["fd%d" % i for i in range(4)]
        NTI = S // TW
        tiles = [(s, T) for s in range(NS) for T in range(NTI)]
        xs = x_src.rearrange("(c p) n -> p c n", p=128)

        def load(idx):
            s, T = tiles[idx]
            b = idx % 2
            t0 = s * S + T * TW
            lo = 1 if T == 0 else 0
            hi = TW + 1 if T == NTI - 1 else TW + 2
            k = "xt%d" % b
            if lo == 1:
                P.op("pool", lambda e: e.memset(xt[b][:, :, 0:1], 0.0), writes=[k])
            if hi == TW + 1:
                P.op("pool", lambda e: e.memset(xt[b][:, :, TW + 1:TW + 2], 0.0), writes=[k])
            dma("sp", xt[b][:, :, lo:hi], xs[:, :, t0 - 1 + lo:t0 - 1 + hi], [], [k], k)

        def norm_d(idx):
            b_ = idx % 2
            return norm_stages(xt[b_], "xt%d" % b_, 8, TW + 2, lambda c: gffn[:, l * 8 + c:l * 8 + c + 1], sq, "sq",
                               rt, "rt", rstd, "rstd", h2s[b_], "h2%d" % b_, float(D))

        load(0)
        for st_ in norm_d(0):
            st_()
        k2 = 0
        pend_tail = [None]
        for idx, (s, T) in enumerate(tiles):
            b = idx % 2
            t0 = s * S + T * TW
            nst = None
            if idx + 1 < len(tiles):
                load(idx + 1)
                nst = norm_d(idx + 1)
            xk = "xt%d" % b
            h2 = h2s[b]
            h2k = "h2%d" % b
            for f in range(NFF):
                if nst is not None and f == 2:
                    nst[0]()
                if nst is not None and f == 8:
                    nst[1]()
                    nst[2]()
                psg, pkg = psum()
                for c in range(8):
                    P.op("pe", lambda e, c=c, f=f, psg=psg: e.matmul(psg[:, 0:TW + 2], wg[:, c, f * 128:(f + 1) * 128],
                                                                     h2[:, c, :], start=(c == 0), stop=(c == 7)),
                         reads=["fg%d" % c, h2k], writes=[pkg])
                psu, pku = psum()
                for c in range(8):
                    P.op("pe", lambda e, c=c, f=f, psu=psu: e.matmul(psu[:, 0:TW], wu[:, c, f * 128:(f + 1) * 128],
                                                                     h2[:, c, 1:TW + 1], start=(c == 0), stop=(c == 7)),
                         reads=["fu%d" % c, h2k], writes=[pku])
                j = k2 % 2
                k2 += 1
                cwb = (l * NFF + f) * 3
                P.op("act", lambda e, psg=psg, j=j, cwb=cwb, f=f: e.activation(
                    out=cv[j], in_=psg[:, 1:TW + 1], func=AF.Identity, scale=cw[:, cwb + 1:cwb + 2],
                    bias=cb[:, l * NFF + f:l * NFF + f + 1]), reads=[pkg], writes=["cv%d" % j])
                P.op("dve", lambda e, psg=psg, j=j, cwb=cwb: e.scalar_tensor_tensor(
                    cv[j], psg[:, 0:TW], cw[:, cwb:cwb + 1], cv[j], ALU.mult, ALU.add), reads=[pkg, "cv%d" % j],
                    writes=["cv%d" % j])
                P.op("dve", lambda e, psg=psg, j=j, cwb=cwb: e.scalar_tensor_tensor(
                    cv[j], psg[:, 2:TW + 2], cw[:, cwb + 2:cwb + 3], cv[j], ALU.mult, ALU.add), reads=[pkg, "cv%d" % j],
                    writes=["cv%d" % j])
                def tail(j=j, f=f, psu=psu, pku=pku):
                    P.op("act", lambda e: e.activation(out=ge[j], in_=cv[j], func=AF.Gelu_apprx_tanh),
                         reads=["cv%d" % j], writes=["ge%d" % j])
                    P.op("dve", lambda e: e.tensor_tensor(uT[:, f, :], ge[j], psu[:, 0:TW], ALU.mult),
                         reads=["ge%d" % j, pku], writes=["uT"], partial=True)

                if pend_tail[0] is not None:
                    pend_tail[0]()
                pend_tail[0] = tail
            pend_tail[0]()
            pend_tail[0] = None
            xo = xt[b][:, :, 1:TW + 1]
            for oc in range(8):
                ps, pk = psum()
                for f in range(NFF):
                    P.op("pe", lambda e, f=f, oc=oc, ps=ps: e.matmul(ps[:, 0:TW], wd[:, f, oc * 128:(oc + 1) * 128],
                                                                     uT[:, f, :], start=(f == 0), stop=(f == NFF - 1)),
                         reads=fdk + ["uT"], writes=[pk])
                P.op("dve", lambda e, oc=oc, ps=ps: e.tensor_tensor(xt[b][:, oc, 1:TW + 1], xt[b][:, oc, 1:TW + 1],
                                                                    ps[:, 0:TW], ALU.add),
                     reads=[pk, xk], writes=[xk], partial=True)
            if not final:
                dma("pool", x_dst.rearrange("(c p) n -> p c n", p=128)[:, :, t0:t0 + TW], xo, [xk], [], "st" + xk)
            else:
                sqf = sqfin[:, :, 0:TW]
                P.op("act", lambda e: e.activation(out=sqf, in_=xo, func=AF.Square), reads=[xk], writes=["sq"])
                ps, pk = psum()
                for c in range(8):
                    P.op("pe", lambda e, c=c, ps=ps: e.matmul(ps[:, 0:TW], ones_bf, sqfin[:, c, 0:TW], start=(c == 0),
                                                              stop=(c == 7)), reads=["sq", "ones"], writes=[pk])
                P.op("act", lambda e, ps=ps: e.activation(out=rtf[:, 0:TW], in_=ps[:, 0:TW], func=AF.Ln,
                                                          scale=1.0 / D, bias=epsc), reads=[pk, "epsc"], writes=["rt"])
                P.op("act", lambda e: e.activation(out=rstdf[:, 0:TW], in_=rtf[:, 0:TW], func=AF.Exp, scale=-0.5),
                     reads=["rt"], writes=["rstd"])
                for c in range(8):
                    P.op("dve", lambda e, c=c: e.scalar_tensor_tensor(xt[b][:, c, 1:TW + 1], xt[b][:, c, 1:TW + 1],
                                                                      gfin[:, c:c + 1], rstdf[:, 0:TW], ALU.mult,
                                                                      ALU.mult),
                         reads=[xk, "rstd"], writes=[xk], partial=True)
                dma("pool", outT.rearrange("(c p) n -> p c n", p=128)[:, :, t0:t0 + TW], xo, [xk], [], "st" + xk)
        P.barrier()

    setup()
    x_cur = xT_in
    for l in range(L):
        if "A" in phases:
            phase_a(l, x_cur)
        if "B" in phases:
            phase_b(l)
        if "C" in phases:
            x_c = xa_d if l == 0 else x_cur
            phase_c(l, x_cur, x_c)
        else:
            x_c = x_cur
        if "D" in phases:
            x_n = xb_d if x_c is xa_d else xa_d
            phase_d(l, x_c, x_n, final=(l == L - 1))
            x_cur = x_n
    P.barrier()
    P.emit(st)
    st.close()
    return nc


_NC_CACHE = {}


def _host_inputs(inp, core, NS, L):
    f = np.float32
    x = np.asarray(inp["x"], f)
    xs = x[core * NS:(core + 1) * NS]
    xT = np.ascontiguousarray(xs.reshape(NS * S, D).T)

    def cols(v, n):
        v = np.asarray(v, f).reshape(-1, n, 128)
        return np.ascontiguousarray(v.transpose(2, 0, 1).reshape(128, -1))

    m = dict(
        xT=xT,
        t5=np.asarray(inp["t5_table"], f),
        gmix=cols(inp["norm_mix_g"][:L], 8),
        gffn=cols(inp["norm_ffn_g"][:L], 8),
        gfin=cols(np.asarray(inp["final_g"])[None], 8),
        gq=cols(inp["q_norm_g"][:L], 2),
        gkv=cols(inp["kv_norm_g"][:L], 1),
        cb=cols(inp["conv_b"][:L], NFF),
        snk=np.ascontiguousarray(np.broadcast_to(np.asarray(inp["sink_logit"][:L], f).reshape(1, -1), (128, L * 4))),
        nab=np.ascontiguousarray(np.asarray(inp["na_bias"][:L], f).transpose(0, 3, 1, 2).reshape(L, 31, 60)),
        w_in=np.asarray(inp["w_in"][:L], f), w_uq=np.asarray(inp["w_uq"][:L], f),
        w_ukv=np.asarray(inp["w_ukv"][:L], f), w_gate=np.asarray(inp["w_gate"][:L], f),
        w_branch=np.asarray(inp["w_branch"][:L], f), w_out=np.asarray(inp["w_out"][:L], f),
        w_ffn_gate=np.asarray(inp["w_ffn_gate"][:L], f), w_ffn_up=np.asarray(inp["w_ffn_up"][:L], f),
        w_ffn_down=np.asarray(inp["w_ffn_down"][:L], f),
    )
    cwv = np.asarray(inp["conv_w"][:L], f)
    cwv = cwv.reshape(L, 3, NFF, 128).transpose(3, 0, 2, 1)
    m["cw"] = np.ascontiguousarray(cwv.reshape(128, -1))
    m.update(_consts())
    return m


def kernel(**inputs):
    NS, L = 2, DEPTH
    key = (NS, L)
    if key not in _NC_CACHE:
        _NC_CACHE[key] = build_nc(NS, L)
    nc = _NC_CACHE[key]
    n = 8
    in_maps = [_host_inputs(inputs, c, NS, L) for c in range(n)]
    res = run_bass_kernel_spmd(nc, in_maps, core_ids=list(range(n)))
    outs = []
    for c in range(n):
        oT = np.asarray(res.results[c]["outT"], np.float32)
        outs.append(oT.T.reshape(NS, S, D))
    return np.ascontiguousarray(np.concatenate(outs, 0))
```

```python
import math
from contextlib import ExitStack

import numpy as np
import concourse.bass as bass
import concourse.mybir as mybir
from concourse.bass_utils import run_bass_kernel_spmd

F32 = mybir.dt.float32
BF16 = mybir.dt.bfloat16
U8 = mybir.dt.uint8
AF = mybir.ActivationFunctionType
ALU = mybir.AluOpType

S = 4096
D = 1024
NCH = 8
DEPTH = 2
INC = 4000
DFF = 2816
NFF = 22
EPS = 1e-6
SEM_EPOCH = 24000
SB_BYTES = 189 * 1024

RT_D = 511
RT_A = 383
RT = RT_D + 3 * RT_A
DIL = (1, 4, 16)


class _Op:
    __slots__ = ("idx", "eng", "fn", "lane", "deps", "inc", "count", "dma_key")

    def __init__(self, idx, eng, fn, lane, dma_key):
        self.idx = idx
        self.eng = eng
        self.fn = fn
        self.lane = lane
        self.deps = {}
        self.inc = False
        self.count = None
        self.dma_key = dma_key


class _Rec:
    def __init__(self):
        self.call = None

    def __getattr__(self, name):
        def f(*a, **k):
            self.call = (name, a, k)
            return None
        return f


class Prog:
    ENGS = ("pe", "act", "dve", "pool", "sp")

    def __init__(self, nc):
        self.nc = nc
        self.ops = []
        self.by_eng = {e: [] for e in self.ENGS}
        self.res = {}
        self.dma_count = {}
        self.pool_map = {}
        self.last = {}

    def _state(self, key):
        st = self.res.get(key)
        if st is None:
            st = [{}, {}]
            self.res[key] = st
        return st

    def op(self, eng, fn, reads=(), writes=(), sem=None, partial=False):
        dma = sem is not None
        if dma:
            key = self.pool_map.get(sem)
            if key is None:
                key = len(self.pool_map)
                self.pool_map[sem] = key
            lane = ("dma", key)
        else:
            key = None
            lane = eng
        if fn is not None:
            rec = _Rec()
            fn(rec)
            fn = rec.call
        o = _Op(len(self.ops), eng, fn, lane, key)
        if dma:
            self.dma_count[key] = self.dma_count.get(key, 0) + 1
            o.count = self.dma_count[key]

        def add_dep(d, raw=False):
            if d.lane == o.lane and not dma:
                if not raw or eng == "pe":
                    return
            cur = o.deps.get(d.lane)
            if cur is None or d.idx > cur.idx:
                o.deps[d.lane] = d

        for r in reads:
            w, rd = self._state(r)
            for d in w.values():
                add_dep(d, raw=True)
        for wkey in writes:
            w, rd = self._state(wkey)
            if not (partial and not rd):
                for d in w.values():
                    add_dep(d)
            for d in rd.values():
                add_dep(d)
        for r in reads:
            w, rd = self._state(r)
            rd[o.lane] = o
        for wkey in writes:
            st = self._state(wkey)
            if partial and not st[1]:
                st[0][o.lane] = o
            else:
                st[0] = {o.lane: o}
                st[1] = {}
        for d in o.deps.values():
            if d.dma_key is None:
                d.inc = True
        self.ops.append(o)
        self.by_eng[eng].append(o)
        self.last[lane] = o
        return o

    def barrier(self, engs=None):
        lasts = list(self.last.values())
        for e in (engs or self.ENGS):
            o = _Op(len(self.ops), e, None, e, None)
            for d in lasts:
                if d.lane == e:
                    continue
                o.deps[d.lane] = d
                if d.dma_key is None:
                    d.inc = True
            self.ops.append(o)
            self.by_eng[e].append(o)
        self.res = {}
        self.pool_map = {}
        self.last = {}

    def emit(self, stack):
        nc = self.nc
        sems = {}

        def get_sem(name):
            s = sems.get(name)
            if s is None:
                s = stack.enter_context(nc.semaphore("s%d" % len(sems)))
                sems[name] = s
            return s

        for e in self.ENGS:
            c = 0
            for o in self.by_eng[e]:
                if o.dma_key is None and o.inc:
                    c += 1
                    o.count = c

        def sem_for(o):
            if o.dma_key is None:
                ep, v = divmod(o.count - 1, SEM_EPOCH)
                return get_sem(("c", o.lane, ep)), v + 1
            per = SEM_EPOCH // 16
            ep, v = divmod(o.count - 1, per)
            return get_sem(("d", o.dma_key, ep)), (v + 1) * 16

        for o in self.ops:
            if o.dma_key is not None or o.inc:
                sem_for(o)
        block = stack.enter_context(nc.Block())
        deco = {"pe": block.tensor, "act": block.scalar, "dve": block.vector,
                "pool": block.gpsimd, "sp": block.sync}
        for e in self.ENGS:
            ops = self.by_eng[e]
            if not ops:
                continue

            def body(engh, ops=ops):
                waited = {}
                for o in ops:
                    for d in o.deps.values():
                        s, v = sem_for(d)
                        k = id(s)
                        if waited.get(k, 0) >= v:
                            continue
                        waited[k] = v
                        engh.wait_ge(s, v)
                    if o.fn is None:
                        continue
                    name, a, k = o.fn
                    ins = getattr(engh, name)(*a, **k)
                    if o.dma_key is not None:
                        s, v = sem_for(o)
                        ins.then_inc(s, 16)
                    elif o.inc:
                        s, v = sem_for(o)
                        ins.then_inc(s, 1)

            deco[e](body)
        self.n_sems = len(sems)


def _t5_bucket(rel):
    half = 16
    exact = 8
    n = np.abs(rel)
    large = exact + (np.log(np.maximum(n, 1) / exact) / math.log(1024 / exact)
                     * (half - exact)).astype(np.int32)
    large = np.minimum(large, half - 1)
    return (np.where(rel > 0, half, 0) + np.where(n < exact, n, large)).astype(np.int32)


def _c_rs(qr):
    return min(max(qr - 4, 0), 56)


def _c_sig(i, j):
    sig = []
    for krl in range(2):
        for qrl in range(2):
            kr = 2 * j + krl
            qr = 2 * i + qrl
            rs = _c_rs(qr)
            if rs <= kr < rs + 8:
                sig.append(kr - qr + 7)
            else:
                sig.append(None)
    return tuple(sig)


def _c_tiles():
    sigs = []
    table = []
    for i in range(32):
        row = []
        for j in range(32):
            sg = _c_sig(i, j)
            if all(v is None for v in sg):
                continue
            if sg not in sigs:
                sigs.append(sg)
            row.append((j, sigs.index(sg)))
        table.append(row)
    return sigs, table


C_SIGS, C_TABLE = _c_tiles()
NCT = len(C_SIGS)


def _consts():
    ohv = np.zeros((32, RT), np.float32)
    msk = np.zeros((16, RT), np.float32)
    u = np.arange(RT_D)
    rel = u - 255
    val = np.abs(rel) <= 128
    b = _t5_bucket(rel)
    ohv[b[val], u[val]] = 1.0
    msk[:, u[val]] = 1.0
    for g, dil in enumerate(DIL):
        off = RT_D + g * RT_A
        u = np.arange(RT_A)
        j = u - 191
        val = np.abs(j) <= 64
        b = _t5_bucket(j * dil)
        ohv[b[val], off + u[val]] = 1.0
        msk[:, off + u[val]] = 1.0
    jf = np.zeros((128, 128), np.float32)
    jf[np.arange(128), 127 - np.arange(128)] = 1.0
    ohc = np.zeros((31, 64, 64), np.float32)
    mc = np.zeros((64, 64), np.float32)
    for qc in range(64):
        cs = min(max(qc - 8, 0), 48)
        for kc in range(cs, cs + 16):
            ohc[kc - qc + 15, kc, qc] = 1.0
            mc[kc, qc] = 1.0
    maskc2 = np.concatenate([mc, mc], 0)
    inv = 10000.0 ** (-np.arange(16, dtype=np.float32) / 16)
    ang = np.arange(S, dtype=np.float32)[None, :] * inv[:, None]
    cos2 = np.concatenate([np.cos(ang), np.cos(ang)], 0).astype(np.float32)
    sin2 = np.concatenate([np.sin(ang), np.sin(ang)], 0).astype(np.float32)
    return dict(c_ohv=ohv, c_msk=msk, c_jf=jf, c_ohc=ohc.reshape(31, 4096),
                c_maskc=maskc2, c_cos=cos2, c_sin=sin2)


def build_nc(NS=2, L=DEPTH, dbg=False, phases="ABCD", mixers="abcd"):
    nc = bass.Bass("TRN2", target_bir_lowering=False)
    NT = NS * S

    def din(name, shape, dt=F32):
        return nc.dram_tensor(name, list(shape), dt, kind="ExternalInput").ap()

    def dscr(name, shape, dt=BF16):
        kind = "ExternalOutput" if dbg else "Internal"
        return nc.dram_tensor(name, list(shape), dt, kind=kind).ap()

    xT_in = din("xT", [D, NT])
    t5_in = din("t5", [32, 16])
    gmix_in = din("gmix", [128, L * 8])
    gffn_in = din("gffn", [128, L * 8])
    gfin_in = din("gfin", [128, 8])
    gq_in = din("gq", [128, L * 2])
    gkv_in = din("gkv", [128, L])
    cw_in = din("cw", [128, L * NFF * 3])
    cb_in = din("cb", [128, L * NFF])
    snk_in = din("snk", [128, L * 4])
    nab_in = din("nab", [L, 31, 60])
    w_in_in = din("w_in", [L, D, INC])
    w_uq_in = din("w_uq", [L, 256, 384])
    w_ukv_in = din("w_ukv", [L, 128, 512])
    w_gate_in = din("w_gate", [L, 4, D, D])
    w_br_in = din("w_branch", [L, 4, 256, D])
    w_out_in = din("w_out", [L, D, D])
    w_fg_in = din("w_ffn_gate", [L, D, DFF])
    w_fu_in = din("w_ffn_up", [L, D, DFF])
    w_fd_in = din("w_ffn_down", [L, DFF, D])
    c_ohv = din("c_ohv", [32, RT])
    c_msk = din("c_msk", [16, RT])
    c_jf = din("c_jf", [128, 128])
    c_ohc = din("c_ohc", [31, 4096])
    c_maskc = din("c_maskc", [128, 64])
    c_cos = din("c_cos", [32, S])
    c_sin = din("c_sin", [32, S])

    outT = nc.dram_tensor("outT", [D, NT], F32, kind="ExternalOutput").ap()

    evec_d = dscr("evec_d", [16, RT], F32)
    mcol_d = dscr("mcol_d", [60, 64, 64], F32)
    hT_d = dscr("hT_d", [D, NT])
    QA_d = dscr("QA_d", [3, 2, 128, NT])
    KA_d = dscr("KA_d", [3, 2, 128, NT])
    VA_d = dscr("VA_d", [NT, 768])
    QB_d = dscr("QB_d", [4, 96, NT])
    KB_d = dscr("KB_d", [4, 96, NT])
    VB_d = dscr("VB_d", [NT, 256])
    QC_d = dscr("QC_d", [2, 128, NT])
    KC_d = dscr("KC_d", [2, 128, NT])
    VC_d = dscr("VC_d", [NT, 256])
    QD_d = dscr("QD_d", [2, 128, NT])
    KD_d = dscr("KD_d", [128, NT])
    VD_d = dscr("VD_d", [NT, 128])
    yT_d = dscr("yT_d", [4, 256, NT])
    xa_d = dscr("xa_d", [D, NT], F32)
    xb_d = dscr("xb_d", [D, NT], F32)

    st = ExitStack()
    P = Prog(nc)
    big = st.enter_context(nc.sbuf_tensor("big", [128, SB_BYTES], U8))
    psb = [st.enter_context(nc.psum_tensor("ps%d" % i, [128, 512], F32))[:, :] for i in range(8)]

    class Arena:
        def __init__(self, base, limit):
            self.off = base
            self.limit = limit

        def tile(self, shape, dt):
            n = 1
            for v in shape:
                n *= v
            nb = n * (4 if dt == F32 else 2)
            nb_al = (nb + 63) // 64 * 64
            assert self.off + nb_al <= self.limit, ("SBUF arena overflow", self.off, nb_al, self.limit)
            ap = big[:, self.off:self.off + nb].bitcast(dt)
            self.off += nb_al
            if len(shape) == 2:
                ap = ap.rearrange("p (a b) -> p a b", a=shape[0])
            elif len(shape) == 3:
                ap = ap.rearrange("p (a b c) -> p a b c", a=shape[0], b=shape[1])
            return ap

    pa = Arena(0, 16 * 1024)
    gmix = pa.tile([L * 8], F32)
    gffn = pa.tile([L * 8], F32)
    gfin = pa.tile([8], F32)
    gq = pa.tile([L * 2], F32)
    gkv = pa.tile([L], F32)
    cw = pa.tile([L * NFF * 3], F32)
    cb = pa.tile([L * NFF], F32)
    snk = pa.tile([L * 4], F32)
    epsc = pa.tile([1], F32)
    ones_bf = pa.tile([128], BF16)
    jf_bf = pa.tile([128], BF16)
    EB_A = pa.tile([24, 128], BF16)
    EB_D = pa.tile([12, 128], BF16)
    PBASE = pa.off

    pscnt = [0, 0]

    psn_pool = [6]

    def psum():
        i = pscnt[0] % psn_pool[0]
        pscnt[0] += 1
        return psb[i], "ps%d" % i

    def psum_acc():
        i = 6 + pscnt[1] % 2
        pscnt[1] += 1
        return psb[i], "ps%d" % i

    def dma(eng, out, in_, reads, writes, sem, partial=False):
        P.op(eng, lambda e: e.dma_start(out=out, in_=in_), reads=reads, writes=writes,
             sem=sem, partial=partial)

    def setup():
        ar = Arena(PBASE, SB_BYTES)
        for i, (t, src) in enumerate([(gmix, gmix_in), (gffn, gffn_in), (gfin, gfin_in), (gq, gq_in),
                                      (gkv, gkv_in), (cw, cw_in), (cb, cb_in), (snk, snk_in)]):
            dma("sp", t, src, [], ["sv%d" % i], "sv%d" % i)
        dma("pool", jf_bf, c_jf, [], ["jf"], "jf")
        P.op("dve", lambda e: e.memset(ones_bf, 1.0), writes=["ones"])
        P.op("dve", lambda e: e.memset(epsc, EPS), writes=["epsc"])
        P.op("act", lambda e: e.activation(out=snk, in_=snk, func=AF.Exp), reads=["sv7"], writes=["sv7"])
        t5f = ar.tile([16], F32)
        t5hi = ar.tile([16], BF16)
        t5hf = ar.tile([16], F32)
        t5lo = ar.tile([16], BF16)
        ohv = ar.tile([RT], BF16)
        mskt = ar.tile([RT], F32)
        evec = ar.tile([RT], F32)
        hk = ar.tile([36, 128], BF16)
        dma("sp", t5f[0:32], t5_in, [], ["t5f"], "t5f")
        dma("pool", ohv[0:32], c_ohv, [], ["ohv"], "ohv")
        dma("sp", mskt[0:16], c_msk, [], ["mskt"], "mskt")
        P.op("dve", lambda e: e.tensor_copy(t5hi[0:32], t5f[0:32]), reads=["t5f"], writes=["t5hi"])
        P.op("dve", lambda e: e.tensor_copy(t5hf[0:32], t5hi[0:32]), reads=["t5hi"], writes=["t5hf"])
        P.op("dve", lambda e: e.tensor_tensor(t5lo[0:32], t5f[0:32], t5hf[0:32], ALU.subtract),
             reads=["t5f", "t5hf"], writes=["t5lo"])
        ncol = 415
        for k in range(4):
            ps, pk = psum()
            c0 = k * ncol
            P.op("pe", lambda e, ps=ps, c0=c0: e.matmul(ps[0:16, 0:ncol], t5hi[0:32, :], ohv[0:32, c0:c0 + ncol],
                                                        start=True, stop=False),
                 reads=["t5hi", "ohv"], writes=[pk])
            P.op("pe", lambda e, ps=ps, c0=c0: e.matmul(ps[0:16, 0:ncol], t5lo[0:32, :], ohv[0:32, c0:c0 + ncol],
                                                        start=False, stop=True),
                 reads=["t5lo", "ohv"], writes=[pk])
            P.op("act", lambda e, ps=ps, c0=c0: e.activation(out=evec[0:16, c0:c0 + ncol], in_=ps[0:16, 0:ncol],
                                                             func=AF.Exp),
                 reads=[pk], writes=["evec"], partial=True)
        P.op("dve", lambda e: e.tensor_tensor(evec[0:16], evec[0:16], mskt[0:16], ALU.mult),
             reads=["evec", "mskt"], writes=["evec"])
        dma("sp", evec_d, evec[0:16], ["evec"], [], "evst")
        P.barrier()
        tiles = []
        for h in range(4):
            for d in (-1, 0, 1):
                tiles.append((12 + h, 0 + d * 128 + 128))
        for g in range(3):
            for h in range(4):
                for ab in range(2):
                    tiles.append((4 * g + h, RT_D + g * RT_A + (0 if ab == 0 else 128)))
        for t, (row, u0) in enumerate(tiles):
            src = bass.AP(evec_d.tensor, row * RT + u0, [[1, 128], [1, 128]])
            dma("pool", hk[:, t, :], src, [], ["hk%d" % t], "hk%d" % t)
        for b in range(9):
            ps, pk = psum()
            for q in range(4):
                t = b * 4 + q
                P.op("pe", lambda e, ps=ps, t=t, q=q: e.matmul(ps[:, q * 128:(q + 1) * 128], hk[:, t, :], jf_bf,
                                                               start=True, stop=True),
                     reads=["hk%d" % t, "jf"], writes=[pk])
            if b < 3:
                dst = EB_D[:, b * 4:(b + 1) * 4, :]
            else:
                dst = EB_A[:, (b - 3) * 4:(b - 2) * 4, :]
            P.op("dve", lambda e, ps=ps, dst=dst: e.tensor_copy(dst, ps.rearrange("p (a b) -> p a b", a=4)),
                 reads=[pk], writes=["EB"], partial=True)
        P.barrier()

    def norm_stages(xt, xkey, nch, n, gcol, sq, sqkey, rt, rtkey, rstd, rstdkey, hT, hkey, feat):
        box = {}

        def s1():
            P.op("act", lambda e: e.activation(out=sq, in_=xt, func=AF.Square), reads=[xkey], writes=[sqkey])

        def s2():
            ps, pk = psum()
            box["ps"] = (ps, pk)
            for c in range(nch):
                P.op("pe", lambda e, c=c: e.matmul(ps[:, 0:n], ones_bf, sq[:, c, :], start=(c == 0), stop=(c == nch - 1)),
                     reads=[sqkey, "ones"], writes=[pk])

        def s3():
            ps, pk = box["ps"]
            P.op("act", lambda e: e.activation(out=rt, in_=ps[:, 0:n], func=AF.Ln, scale=1.0 / feat, bias=epsc),
                 reads=[pk, "epsc"], writes=[rtkey])
            P.op("act", lambda e: e.activation(out=rstd, in_=rt, func=AF.Exp, scale=-0.5), reads=[rtkey],
                 writes=[rstdkey])
            for c in range(nch):
                P.op("dve", lambda e, c=c: e.scalar_tensor_tensor(hT[:, c, :], xt[:, c, :], gcol(c), rstd,
                                                                  ALU.mult, ALU.mult),
                     reads=[xkey, rstdkey], writes=[hkey], partial=True)

        return s1, s2, s3

    def rms_norm(*a):
        for st_ in norm_stages(*a):
            st_()

    def phase_a(l, x_src):
        psn_pool[0] = 8
        ar = Arena(PBASE, SB_BYTES)
        w_in = ar.tile([8, INC], BF16)
        wkrot = ar.tile([8, 96], BF16)
        wuq = ar.tile([2, 384], BF16)
        wuqr = ar.tile([2, 4, 96], BF16)
        wukv = ar.tile([512], BF16)
        xt = [ar.tile([8, 512], F32) for _ in range(2)]
        sq = ar.tile([8, 512], BF16)
        rt = ar.tile([512], F32)
        rstd = ar.tile([512], F32)
        hT = [ar.tile([8, 512], BF16) for _ in range(2)]
        cs = [ar.tile([2, 512], F32) for _ in range(1)]
        NSTG = 4
        stg = [ar.tile([512], BF16) for _ in range(NSTG)]
        vst = [ar.tile([4, 1152], BF16) for _ in range(1)]
        cq = ar.tile([2, 512], F32)
        cqsq = ar.tile([2, 512], BF16)
        cqn = ar.tile([2, 512], BF16)
        ckv = ar.tile([1, 512], F32)
        ckvsq = ar.tile([1, 512], BF16)
        ckvn = ar.tile([1, 512], BF16)
        rt2 = ar.tile([512], F32)
        rs2 = ar.tile([512], F32)
        rt3 = ar.tile([512], F32)
        rs3 = ar.tile([512], F32)
        kr = ar.tile([512], F32)
        t1 = ar.tile([512], F32)
        t2 = ar.tile([512], F32)
        qst = [ar.tile([512], BF16) for _ in range(2)]
        kst = [ar.tile([512], BF16) for _ in range(2)]
        vbst = [ar.tile([4, 256], BF16) for _ in range(1)]

        for c in range(8):
            dma("pool", w_in[:, c, :], w_in_in[l, c * 128:(c + 1) * 128, :], [], ["w_in%d" % c], "w_in%d" % c)
        dma("pool", wuq, w_uq_in[l].rearrange("(c p) n -> p c n", p=128), [], ["wuq"], "wuq")
        dma("pool", wukv, w_ukv_in[l], [], ["wukv"], "wukv")
        allw = ["w_in%d" % c for c in range(8)]
        P.op("dve", lambda e: e.memset(wkrot, 0.0), writes=["wkrot"])
        P.op("dve", lambda e: e.tensor_scalar(wkrot[:, :, 64:80], w_in[:, :, 2704:2720], -1.0, 0.0, ALU.mult, ALU.add),
             reads=allw, writes=["wkrot"])
        P.op("dve", lambda e: e.tensor_copy(wkrot[:, :, 80:96], w_in[:, :, 2688:2704]), reads=allw, writes=["wkrot"])
        P.op("dve", lambda e: e.memset(wuqr, 0.0), writes=["wuqr"])
        for h in range(4):
            P.op("dve", lambda e, h=h: e.tensor_scalar(wuqr[:, :, h, 64:80], wuq[:, :, h * 96 + 80:h * 96 + 96],
                                                       -1.0, 0.0, ALU.mult, ALU.add), reads=["wuq"], writes=["wuqr"])
            P.op("dve", lambda e, h=h: e.tensor_copy(wuqr[:, :, h, 80:96], wuq[:, :, h * 96 + 64:h * 96 + 80]),
                 reads=["wuq"], writes=["wuqr"])

        tiles = [(s, T) for s in range(NS) for T in range(8)]

        def load(idx):
            s, T = tiles[idx]
            b = idx % 2
            t0 = s * S + T * 512
            dma("sp", xt[b], x_src.rearrange("(c p) n -> p c n", p=128)[:, :, t0:t0 + 512], [], ["xt%d" % b],
                "xt%d" % b)

        evac_rr = [0]

        def evac(dst, src, reads, writes, partial=False):
            evac_rr[0] += 1
            if evac_rr[0] % 2:
                P.op("act", lambda e: e.copy(dst, src), reads=reads, writes=writes, partial=partial)
            else:
                P.op("dve", lambda e: e.tensor_copy(dst, src), reads=reads, writes=writes, partial=partial)

        stg_rr = [0]

        def norm_a(idx):
            s_, T_ = tiles[idx]
            b_ = idx % 2
            t0_ = s_ * S + T_ * 512
            rms_norm(xt[b_], "xt%d" % b_, 8, 512, lambda c: gmix[:, l * 8 + c:l * 8 + c + 1], sq, "sq", rt, "rt",
                     rstd, "rstd", hT[b_], "hT%d" % b_, float(D))
            dma("pool", hT_d.rearrange("(c p) n -> p c n", p=128)[:, :, t0_:t0_ + 512], hT[b_], ["hT%d" % b_], [],
                "sthT%d" % b_)

        load(0)
        if len(tiles) > 1:
            load(1)
        norm_a(0)
        for idx, (s, T) in enumerate(tiles):
            b = idx % 2
            t0 = s * S + T * 512
            if idx + 1 < len(tiles):
                norm_a(idx + 1)
            if idx + 2 < len(tiles):
                load(idx + 2)
            dma("sp", cs[0][64:96, 0, :], c_cos[:, T * 512:(T + 1) * 512], [], ["cs0"], "cs0", partial=True)
            dma("sp", cs[0][64:96, 1, :], c_sin[:, T * 512:(T + 1) * 512], [], ["cs0"], "cs0", partial=True)
            xk, hk_ = "xt%d" % b, "hT%d" % b

            def fm_chunk(col, M, w=None, wkey=None):
                ps, pk = psum()
                for c in range(8):
                    if w is None:
                        lhsT = w_in[:, c, col:col + M]
                        rk = "w_in%d" % c
                    else:
                        lhsT = w[:, c, col:col + M]
                        rk = wkey
                    P.op("pe", lambda e, c=c, lhsT=lhsT: e.matmul(ps[0:M, :], lhsT, hT[b][:, c, :],
                                                                  start=(c == 0), stop=(c == 7)),
                         reads=[rk, hk_], writes=[pk])
                return ps, pk

            def out_chunk(ps, pk, dst_dram, dil=1):
                i = stg_rr[0] % NSTG
                stg_rr[0] += 1
                sk = "stg%d" % i
                if dil == 1:
                    evac(stg[i], ps, [pk], [sk])
                    dma("pool", dst_dram[:, t0:t0 + 512], stg[i], [sk], [], "st" + sk)
                else:
                    J = 512 // dil
                    evac(stg[i].rearrange("p (r j) -> p j r", r=dil), ps.rearrange("p (j r) -> p j r", r=dil),
                         [pk], [sk])
                    Lg = S // dil
                    dst = dst_dram[:, s * S:(s + 1) * S].rearrange("p (r j) -> p r j", r=dil)[:, :, T * J:(T + 1) * J]
                    dma("pool", dst, stg[i].rearrange("p (r j) -> p r j", r=dil), [sk], [], "st" + sk)

            for c2 in range(2):
                ps, pk = fm_chunk(2304 + c2 * 128, 128)
                evac(cq[:, c2, :], ps, [pk], ["cq"], partial=True)
            ps, pk = fm_chunk(2560, 128)
            evac(ckv[:, 0, :], ps, [pk], ["ckv"])
            nq = norm_stages(cq, "cq", 2, 512, lambda c: gq[:, l * 2 + c:l * 2 + c + 1], cqsq, "cqsq", rt2, "rt2",
                             rs2, "rs2", cqn, "cqn", 256.0)
            nkv = norm_stages(ckv, "ckv", 1, 512, lambda c: gkv[:, l:l + 1], ckvsq, "ckvsq", rt3, "rt3", rs3, "rs3",
                              ckvn, "ckvn", 128.0)
            nq[0]()
            nkv[0]()
            psk, pkk = fm_chunk(2624, 96)
            psr, pkr = fm_chunk(0, 96, w=wkrot, wkey="wkrot")
            csk = "cs0"
            P.op("dve", lambda e: e.tensor_tensor(t1[64:96], psk[64:96, :], cs[0][64:96, 0, :], ALU.mult),
                 reads=[pkk, csk], writes=["t1"])
            P.op("dve", lambda e: e.tensor_tensor(t2[64:96], psr[64:96, :], cs[0][64:96, 1, :], ALU.mult),
                 reads=[pkr, csk], writes=["t2"])
            P.op("dve", lambda e: e.tensor_tensor(kr[64:96], t1[64:96], t2[64:96], ALU.add),
                 reads=["t1", "t2"], writes=["kr"])
            for g in range(3):
                for hp in range(1):
                    ps, pk = fm_chunk((g * 4 + hp * 2) * 64, 128)
                    out_chunk(ps, pk, QA_d[g, hp], DIL[g])
                    ps, pk = fm_chunk(768 + (g * 4 + hp * 2) * 64, 128)
                    out_chunk(ps, pk, KA_d[g, hp], DIL[g])
            nq[1]()
            nq[2]()
            nkv[1]()
            nkv[2]()
            for g in range(3):
                for hp in range(1, 2):
                    ps, pk = fm_chunk((g * 4 + hp * 2) * 64, 128)
                    out_chunk(ps, pk, QA_d[g, hp], DIL[g])
                    ps, pk = fm_chunk(768 + (g * 4 + hp * 2) * 64, 128)
                    out_chunk(ps, pk, KA_d[g, hp], DIL[g])
            for h in range(4):
                hb = h % 2
                psq, pkq = psum()
                psq2, pkq2 = psum()
                for c2 in range(2):
                    P.op("pe", lambda e, c2=c2: e.matmul(psq[0:96, :], wuq[:, c2, h * 96:(h + 1) * 96], cqn[:, c2, :],
                                                         start=(c2 == 0), stop=(c2 == 1)),
                         reads=["wuq", "cqn"], writes=[pkq])
                for c2 in range(2):
                    P.op("pe", lambda e, c2=c2: e.matmul(psq2[0:96, :], wuqr[:, c2, h, :], cqn[:, c2, :],
                                                         start=(c2 == 0), stop=(c2 == 1)),
                         reads=["wuqr", "cqn"], writes=[pkq2])
                qk = "qst%d" % hb
                P.op("dve", lambda e: e.tensor_copy(qst[hb][0:64], psq[0:64, :]), reads=[pkq], writes=[qk])
                P.op("dve", lambda e: e.tensor_tensor(t1[64:96], psq[64:96, :], cs[0][64:96, 0, :], ALU.mult),
                     reads=[pkq, csk], writes=["t1"])
                P.op("dve", lambda e: e.tensor_tensor(t2[64:96], psq2[64:96, :], cs[0][64:96, 1, :], ALU.mult),
                     reads=[pkq2, csk], writes=["t2"])
                P.op("dve", lambda e: e.tensor_tensor(qst[hb][64:96], t1[64:96], t2[64:96], ALU.add),
                     reads=["t1", "t2", qk], writes=[qk], partial=True)
                dma("pool", QB_d[h][:, t0:t0 + 512], qst[hb][0:96], [qk], [], "st" + qk)
                psn, pkn = psum()
                P.op("pe", lambda e: e.matmul(psn[0:64, :], wukv[:, h * 128:h * 128 + 64], ckvn[:, 0, :],
                                              start=True, stop=True), reads=["wukv", "ckvn"], writes=[pkn])
                kk = "kst%d" % hb
                evac(kst[hb][0:64], psn[0:64, :], [pkn], [kk])
                P.op("pool", lambda e: e.tensor_copy(kst[hb][64:96], kr[64:96]), reads=["kr", kk], writes=[kk],
                     partial=True)
                dma("pool", KB_d[h][:, t0:t0 + 512], kst[hb][0:96], [kk], [], "st" + kk)
            vb = 0
            vbk = "vbst%d" % vb
            wv = wukv.rearrange("p (h x) -> p h x", h=4)[:, :, 64:128]
            for tb in range(4):
                ps, pk = psum()
                P.op("pe", lambda e, tb=tb, ps=ps: e.matmul(ps[:, 0:256].rearrange("p (h x) -> p h x", h=4),
                                                            ckvn[:, 0, tb * 128:(tb + 1) * 128], wv,
                                                            start=True, stop=True),
                     reads=["wukv", "ckvn"], writes=[pk])
                evac(vbst[vb][:, tb, :], ps[:, 0:256], [pk], [vbk], partial=True)
            dma("pool", VB_d[t0:t0 + 512, :].rearrange("(tb p) f -> p tb f", p=128), vbst[vb], [vbk], [], "st" + vbk)

            for hp in range(2):
                ps, pk = fm_chunk(2720 + hp * 128, 128)
                out_chunk(ps, pk, QC_d[hp])
                ps, pk = fm_chunk(2976 + hp * 128, 128)
                out_chunk(ps, pk, KC_d[hp])
                ps, pk = fm_chunk(3488 + hp * 128, 128)
                out_chunk(ps, pk, QD_d[hp])
            ps, pk = fm_chunk(3744, 128)
            out_chunk(ps, pk, KD_d)

            vk = "vst%d" % vb
            groups = [(1536, 512, 0), (2048, 256, 512), (3232, 256, 768), (3872, 128, 1024)]
            for tb in range(4):
                for (col, n, so) in groups:
                    ps, pk = psum()
                    for c in range(8):
                        P.op("pe", lambda e, c=c, ps=ps, col=col, n=n: e.matmul(
                            ps[:, 0:n], hT[b][:, c, tb * 128:(tb + 1) * 128], w_in[:, c, col:col + n],
                            start=(c == 0), stop=(c == 7)), reads=["w_in%d" % c, hk_], writes=[pk])
                    evac(vst[vb][:, tb, so:so + n], ps[:, 0:n], [pk], [vk], partial=True)
            dma("pool", VA_d[t0:t0 + 512, :].rearrange("(tb p) f -> p tb f", p=128), vst[vb][:, :, 0:768], [vk], [],
                "stva%d" % vb)
            dma("pool", VC_d[t0:t0 + 512, :].rearrange("(tb p) f -> p tb f", p=128), vst[vb][:, :, 768:1024], [vk], [],
                "stvc%d" % vb)
            dma("pool", VD_d[t0:t0 + 512, :].rearrange("(tb p) f -> p tb f", p=128), vst[vb][:, :, 1024:1152], [vk], [],
                "stvd%d" % vb)
        P.barrier()

    def build_ebc(l, ar):
        EBC = ar.tile([4 * NCT, 128], BF16)
        sub = Arena(ar.off, SB_BYTES)
        nbf = sub.tile([60], F32)
        nbhi = sub.tile([60], BF16)
        nbhf = sub.tile([60], F32)
        nblo = sub.tile([60], BF16)
        ohc = sub.tile([4096], BF16)
        mcs = sub.tile([4096], F32)
        mcolS = sub.tile([60, 64], BF16)
        mk2 = sub.tile([64], BF16)
        dma("sp", nbf[0:31], nab_in[l], [], ["nbf"], "nbf")
        dma("pool", ohc[0:31], c_ohc, [], ["ohc"], "ohc")
        dma("pool", mk2, c_maskc, [], ["mk2"], "mk2")
        P.op("dve", lambda e: e.tensor_copy(nbhi[0:31], nbf[0:31]), reads=["nbf"], writes=["nbhi"])
        P.op("dve", lambda e: e.tensor_copy(nbhf[0:31], nbhi[0:31]), reads=["nbhi"], writes=["nbhf"])
        P.op("dve", lambda e: e.tensor_tensor(nblo[0:31], nbf[0:31], nbhf[0:31], ALU.subtract),
             reads=["nbf", "nbhf"], writes=["nblo"])
        for k in range(8):
            ps, pk = psum()
            P.op("pe", lambda e, ps=ps, k=k: e.matmul(ps[0:60, :], nbhi[0:31, :], ohc[0:31, k * 512:(k + 1) * 512],
                                                      start=True, stop=False), reads=["nbhi", "ohc"], writes=[pk])
            P.op("pe", lambda e, ps=ps, k=k: e.matmul(ps[0:60, :], nblo[0:31, :], ohc[0:31, k * 512:(k + 1) * 512],
                                                      start=False, stop=True), reads=["nblo", "ohc"], writes=[pk])
            P.op("act", lambda e, ps=ps, k=k: e.activation(out=mcs[0:60, k * 512:(k + 1) * 512], in_=ps[0:60, :],
                                                           func=AF.Exp), reads=[pk], writes=["mcs"], partial=True)
        dma("sp", mcol_d.rearrange("m a b -> m (a b)"), mcs[0:60], ["mcs"], [], "stmcs")
        P.barrier()
        src = mcol_d.rearrange("m kc qc -> kc m qc")
        dma("pool", mcolS[0:64], src, [], ["mcolS"], "mcolS", partial=True)
        dma("pool", mcolS[64:128], src, [], ["mcolS"], "mcolS", partial=True)
        P.op("dve", lambda e: e.tensor_tensor(mcolS, mcolS, mk2.unsqueeze(1).to_broadcast([128, 60, 64]), ALU.mult),
             reads=["mcolS", "mk2"], writes=["mcolS"])
        P.op("pool", lambda e: e.memset(EBC, 0.0), writes=["EBC"])
        rr = 0
        for h in range(4):
            for ti, sg in enumerate(C_SIGS):
                k = 0
                for krl in range(2):
                    for qrl in range(2):
                        dr = sg[k]
                        k += 1
                        if dr is None:
                            continue
                        dst = EBC[krl * 64:(krl + 1) * 64, h * NCT + ti, qrl * 64:(qrl + 1) * 64]
                        srcv = mcolS[krl * 64:(krl + 1) * 64, h * 15 + dr, :]
                        eng = ("dve", "pool")[rr % 2]
                        rr += 1
                        P.op(eng, lambda e, dst=dst, srcv=srcv: e.tensor_copy(dst, srcv), reads=["mcolS", "EBC"],
                             writes=["EBC"], partial=True)
        P.barrier()
        return EBC

    def phase_b(l):
        psn_pool[0] = 6
        ar0 = Arena(PBASE, SB_BYTES)
        EBC = build_ebc(l, ar0)
        base = ar0.off
        NPE = 12
        pexp = [ar0.tile([512], BF16) for _ in range(NPE)]
        pt = [ar0.tile([512], BF16) for _ in range(NPE)]
        rec = [ar0.tile([512], F32) for _ in range(2)]
        yst = [ar0.tile([2, 512], BF16) for _ in range(2)]
        wbase = ar0.off
        cnt = {"pe": 0, "rec": 0, "mul": 0}

        def score_slot(qbs, eb, scale, ebkey, zero=()):
            ps, pk = psum()
            for q, ent in enumerate(qbs):
                if ent is None:
                    continue
                kT, qT, lo, hi, rds = ent
                if lo == 0 and hi == 128:
                    P.op("pe", lambda e, kT=kT, qT=qT, q=q: e.matmul(ps[:, q * 128:(q + 1) * 128], kT, qT,
                                                                     start=True, stop=True), reads=rds, writes=[pk])
                else:
                    pb = kT.base_partition()
                    P.op("pe", lambda e, kT=kT, qT=qT, q=q, lo=lo, hi=hi, pb=pb: e.matmul(
                        ps[lo:hi, q * 128:(q + 1) * 128], kT, qT, start=True, stop=True, tile_position=(pb, lo)),
                        reads=rds, writes=[pk])
            i = cnt["pe"] % NPE
            cnt["pe"] += 1
            if eb is None:
                P.op("act", lambda e: e.activation(out=pt[i], in_=ps, func=AF.Exp, scale=scale), reads=[pk],
                     writes=["pt%d" % i])
                return pt[i], "pt%d" % i
            P.op("act", lambda e: e.activation(out=pexp[i], in_=ps, func=AF.Exp, scale=scale), reads=[pk],
                 writes=["pexp%d" % i])
            eng = ("dve", "pool")[cnt["mul"] % 2]
            cnt["mul"] += 1
            if isinstance(eb, list):
                for q, ebq in enumerate(eb):
                    if ebq is None or qbs[q] is None:
                        continue
                    P.op(eng, lambda e, q=q, ebq=ebq: e.tensor_tensor(pt[i][:, q * 128:(q + 1) * 128],
                                                                      pexp[i][:, q * 128:(q + 1) * 128], ebq, ALU.mult),
                         reads=["pexp%d" % i, ebkey], writes=["pt%d" % i], partial=(q > 0))
            else:
                P.op(eng, lambda e: e.tensor_tensor(pt[i].rearrange("p (a b) -> p a b", a=4),
                                                    pexp[i].rearrange("p (a b) -> p a b", a=4),
                                                    eb.unsqueeze(1).to_broadcast([128, 4, 128]), ALU.mult),
                     reads=["pexp%d" % i, ebkey], writes=["pt%d" % i])
                for (q, zlo, zhi) in zero:
                    P.op(eng, lambda e, q=q, zlo=zlo, zhi=zhi: e.memset(pt[i][zlo:zhi, q * 128:(q + 1) * 128], 0.0),
                         writes=["pt%d" % i], partial=True)
            return pt[i], "pt%d" % i

        def pv(ot, otk, pts, vents):
            for q in range(4):
                lst = [(sl, vents[sl][q]) for sl in range(len(pts)) if vents[sl][q] is not None]
                for n, (sl, (va, lo, hi, rds)) in enumerate(lst):
                    ptile, ptk = pts[sl]
                    P.op("pe", lambda e, va=va, lo=lo, hi=hi, ptile=ptile, q=q, n=n, last=len(lst) - 1: e.matmul(
                        ot[:, q * 128:(q + 1) * 128], va, ptile[lo:hi, q * 128:(q + 1) * 128],
                        start=(n == 0), stop=(n == last)), reads=rds + [ptk], writes=[otk])

        def normalize(ot, otk, dst, dstkey, addcol=None, partial=True):
            i = cnt["rec"] % 2
            cnt["rec"] += 1
            rk = "rec%d" % i
            if addcol is not None:
                P.op("act", lambda e: e.activation(out=rec[i][0:64], in_=ot[64:128, :], func=AF.Ln, bias=addcol),
                     reads=[otk], writes=[rk])
            else:
                P.op("act", lambda e: e.activation(out=rec[i][0:64], in_=ot[64:128, :], func=AF.Ln),
                     reads=[otk], writes=[rk])
            P.op("act", lambda e: e.activation(out=rec[i][0:64], in_=rec[i][0:64], func=AF.Exp, scale=-1.0),
                 reads=[rk], writes=[rk])
            P.op("dve", lambda e: e.tensor_tensor(dst, ot[0:64, :], rec[i][0:64], ALU.mult), reads=[otk, rk],
                 writes=[dstkey], partial=partial)

        def store_y(m, s, c2, ch, ysti, yk):
            dma("pool", yT_d[m, c2 * 128:(c2 + 1) * 128, s * S + ch * 512:s * S + (ch + 1) * 512], yst[ysti][:, 0, :],
                [yk], [], "st" + yk)

        ycnt = [0]

        def run_units(units, lag=1):
            for ui in range(len(units) + lag):
                if ui < len(units):
                    units[ui][0]()
                if ui >= lag:
                    units[ui - lag][1]()

        for s in range(NS):
            if "d" in mixers:
                ar = Arena(wbase, SB_BYTES)
                QT = [ar.tile([S], BF16) for _ in range(2)]
                KT = [ar.tile([S], BF16) for _ in range(2)]
                VG = ar.tile([32, 2, 128], BF16)
                for hp in range(2):
                    dma("sp", QT[hp], QD_d[hp][:, s * S:(s + 1) * S], [], ["QT%d" % hp], "QT%d" % hp)
                    for hb in range(2):
                        dma("sp", KT[hp][hb * 64:(hb + 1) * 64], KD_d[hp * 64:(hp + 1) * 64, s * S:(s + 1) * S], [],
                            ["KT%d" % hp], "KT%d" % hp, partial=True)
                P.op("pool", lambda e: e.memset(VG[:, :, :, 64:128], 1.0), writes=["VG"])
                for g_ in range(2):
                    dma("sp", VG[:, :, g_, 0:64],
                        VD_d[s * S:(s + 1) * S, g_ * 64:(g_ + 1) * 64].rearrange("(kb p) x -> p kb x", p=128),
                        [], ["VG"], "VG", partial=True)
                units = []
                for hp in range(2):
                    for ch in range(8):
                        yi = ycnt[0] % 2
                        ycnt[0] += 1
                        yk = "yst%d" % yi
                        for hb in range(2):
                            h = hp * 2 + hb
                            kvh = h // 2
                            box = {}

                            def st1(hp=hp, ch=ch, hb=hb, h=h, kvh=kvh, box=box):
                                pts = []
                                vents = []
                                for d in (-1, 0, 1):
                                    qbs = []
                                    vv = []
                                    for q in range(4):
                                        i = ch * 4 + q
                                        j = i + d
                                        if j < 0 or j > 31:
                                            qbs.append(None)
                                            vv.append(None)
                                            continue
                                        qbs.append((KT[hp][hb * 64:(hb + 1) * 64, j * 128:(j + 1) * 128],
                                                    QT[hp][hb * 64:(hb + 1) * 64, i * 128:(i + 1) * 128], 0, 128,
                                                    ["KT%d" % hp, "QT%d" % hp]))
                                        vv.append((VG[:, j, kvh, :], 0, 128, ["VG"]))
                                    pts.append(score_slot(qbs, EB_D[:, h * 3 + d + 1, :], 0.125, "EB"))
                                    vents.append(vv)
                                box["pts"] = pts
                                box["vents"] = vents

                            def st2(hp=hp, ch=ch, hb=hb, h=h, box=box, yi=yi, yk=yk):
                                ot, otk = psum_acc()
                                pv(ot, otk, box["pts"], box["vents"])
                                normalize(ot, otk, yst[yi][hb * 64:(hb + 1) * 64, 0, :], yk,
                                          addcol=snk[64:128, l * 4 + h:l * 4 + h + 1], partial=(hb == 1))
                                if hb == 1:
                                    store_y(3, s, hp, ch, yi, yk)

                            units.append((st1, st2))
                run_units(units)
                P.barrier()
            if "c" in mixers:
                ar = Arena(wbase, SB_BYTES)
                QT = [ar.tile([S], BF16) for _ in range(2)]
                KT = [ar.tile([S], BF16) for _ in range(2)]
                VG = ar.tile([32, 4, 128], BF16)
                for hp in range(2):
                    dma("sp", QT[hp], QC_d[hp][:, s * S:(s + 1) * S], [], ["QT%d" % hp], "QT%d" % hp)
                    dma("sp", KT[hp], KC_d[hp][:, s * S:(s + 1) * S], [], ["KT%d" % hp], "KT%d" % hp)
                P.op("pool", lambda e: e.memset(VG[:, :, :, 64:128], 1.0), writes=["VG"])
                for g_ in range(4):
                    dma("sp", VG[:, :, g_, 0:64],
                        VC_d[s * S:(s + 1) * S, g_ * 64:(g_ + 1) * 64].rearrange("(kb p) x -> p kb x", p=128),
                        [], ["VG"], "VG", partial=True)
                units = []
                for hp in range(2):
                    for ch in range(8):
                        yi = ycnt[0] % 2
                        ycnt[0] += 1
                        yk = "yst%d" % yi
                        dlist = sorted({j - (ch * 4 + q) for q in range(4) for (j, _) in C_TABLE[ch * 4 + q]})
                        for hb in range(2):
                            h = hp * 2 + hb
                            box = {}

                            def st1(hp=hp, ch=ch, hb=hb, h=h, box=box, dlist=dlist):
                                pts = []
                                vents = []
                                for d in dlist:
                                    qbs = []
                                    vv = []
                                    ebl = []
                                    for q in range(4):
                                        i = ch * 4 + q
                                        j = i + d
                                        ti = dict(C_TABLE[i]).get(j)
                                        if ti is None:
                                            qbs.append(None)
                                            vv.append(None)
                                            ebl.append(None)
                                            continue
                                        qbs.append((KT[hp][hb * 64:(hb + 1) * 64, j * 128:(j + 1) * 128],
                                                    QT[hp][hb * 64:(hb + 1) * 64, i * 128:(i + 1) * 128], 0, 128,
                                                    ["KT%d" % hp, "QT%d" % hp]))
                                        vv.append((VG[:, j, h, :], 0, 128, ["VG"]))
                                        ebl.append(EBC[:, h * NCT + ti, :])
                                    tis = {dict(C_TABLE[ch * 4 + q]).get(ch * 4 + q + d) for q in range(4)}
                                    if len(tis) == 1 and None not in tis:
                                        eb = ebl[0]
                                    else:
                                        eb = ebl
                                    pts.append(score_slot(qbs, eb, 0.125, "EBC"))
                                    vents.append(vv)
                                box["pts"] = pts
                                box["vents"] = vents

                            def st2(hp=hp, ch=ch, hb=hb, box=box, yi=yi, yk=yk):
                                ot, otk = psum_acc()
                                pv(ot, otk, box["pts"], box["vents"])
                                normalize(ot, otk, yst[yi][hb * 64:(hb + 1) * 64, 0, :], yk, partial=(hb == 1))
                                if hb == 1:
                                    store_y(2, s, hp, ch, yi, yk)

                            units.append((st1, st2))
                run_units(units)
                P.barrier()
            if "a" in mixers:
                ar = Arena(wbase, SB_BYTES)
                QTs = [ar.tile([S], BF16) for _ in range(2)]
                KTs = [ar.tile([S + 128], BF16) for _ in range(2)]
                VGs = [ar.tile([48, 2, 128], BF16) for _ in range(2)]
                OA = [ar.tile([S], F32) for _ in range(2)]
                for z in range(2):
                    P.op("pool", lambda e, z=z: e.memset(VGs[z], 0.0), writes=["VG%d" % z])
                    P.op("pool", lambda e, z=z: e.memset(VGs[z][:, :, :, 64:128], 1.0), writes=["VG%d" % z])
                    P.op("pool", lambda e, z=z: e.memset(KTs[z][:, 0:64], 0.0), writes=["KTpad%d" % z])
                    P.op("pool", lambda e, z=z: e.memset(KTs[z][:, S + 64:S + 128], 0.0), writes=["KTpad%d" % z])
                groups = [(hp, g) for hp in range(2) for g in range(3)]

                def load_a(gidx):
                    hp, g = groups[gidx]
                    z = gidx % 2
                    dil = DIL[g]
                    Lg = S // dil
                    nb = Lg // 128
                    dma("sp", QTs[z], QA_d[g, hp][:, s * S:(s + 1) * S], [], ["QT%d" % z], "QT%d" % z)
                    dma("sp", KTs[z][:, 64:64 + S], KA_d[g, hp][:, s * S:(s + 1) * S], [], ["KT%d" % z], "KT%d" % z)
                    vsrc = VA_d[s * S:(s + 1) * S, :].rearrange("(j r) (g x) -> r j g x", r=dil, g=3)
                    for r in range(dil):
                        for m in range(nb + 1):
                            lo = 64 if m == 0 else 0
                            hi = 64 if m == nb else 128
                            j0 = 128 * m - 64 + lo
                            src = vsrc[r, j0:j0 + (hi - lo), g, :].rearrange("j (h x) -> j h x", h=4)[
                                :, hp * 2:hp * 2 + 2, :]
                            dma("sp", VGs[z][lo:hi, r * (nb + 1) + m, :, 0:64], src, [], ["VG%d" % z], "VG%d" % z,
                                partial=True)

                load_a(0)
                for gidx, (hp, g) in enumerate(groups):
                    z = gidx % 2
                    QT, KT, VG = QTs[z], KTs[z], VGs[z]
                    qk_, kk_, vk_, kp_ = "QT%d" % z, "KT%d" % z, "VG%d" % z, "KTpad%d" % z
                    if gidx + 1 < len(groups):
                        load_a(gidx + 1)
                    dil = DIL[g]
                    Lg = S // dil
                    nb = Lg // 128
                    nqb = S // 128
                    units = []
                    for ch in range(nqb // 4):
                        for hb in range(2):
                            h = hp * 2 + hb
                            box = {}

                            def st1(ch=ch, hb=hb, h=h, box=box, g=g, Lg=Lg, nb=nb, QT=QT, KT=KT, VG=VG, qk_=qk_,
                                    kk_=kk_, vk_=vk_, kp_=kp_):
                                pts = []
                                vents = []
                                for ab in range(2):
                                    qbs = []
                                    vv = []
                                    zer = []
                                    for q in range(4):
                                        gi = ch * 4 + q
                                        r, i = divmod(gi, nb)
                                        m = i + ab
                                        k0 = 64 + r * Lg + 128 * m - 64
                                        qbs.append((KT[hb * 64:(hb + 1) * 64, k0:k0 + 128],
                                                    QT[hb * 64:(hb + 1) * 64, gi * 128:(gi + 1) * 128], 0, 128,
                                                    [kk_, kp_, qk_]))
                                        vv.append((VG[:, r * (nb + 1) + m, hb, :], 0, 128, [vk_]))
                                        if m == 0:
                                            zer.append((q, 0, 64))
                                        elif m == nb:
                                            zer.append((q, 64, 128))
                                    pts.append(score_slot(qbs, EB_A[:, (g * 4 + h) * 2 + ab, :], 0.125, "EB",
                                                          zero=zer))
                                    vents.append(vv)
                                box["pts"] = pts
                                box["vents"] = vents

                            def st2(ch=ch, hb=hb, box=box, dil=dil, Lg=Lg):
                                ot, otk = psum_acc()
                                pv(ot, otk, box["pts"], box["vents"])
                                if dil == 1:
                                    dst = OA[hb][:, ch * 512:(ch + 1) * 512]
                                    P.op("act", lambda e, dst=dst, ot=ot: e.copy(dst, ot), reads=[otk],
                                         writes=["OA%d" % hb], partial=True)
                                else:
                                    oav = OA[hb].rearrange("p (j r) -> p r j", r=dil)
                                    pos0 = ch * 512
                                    done = 0
                                    while done < 512:
                                        r, j = divmod(pos0 + done, Lg)
                                        n = min(512 - done, Lg - j)
                                        dst = oav[:, r, j:j + n]
                                        srcv = ot[:, done:done + n]
                                        P.op("dve", lambda e, dst=dst, srcv=srcv: e.tensor_tensor(dst, dst, srcv,
                                                                                                  ALU.add),
                                             reads=[otk, "OA%d" % hb], writes=["OA%d" % hb], partial=True)
                                        done += n

                            units.append((st1, st2))
                    run_units(units)
                    if g == 2:
                        for ch in range(8):
                            yi = ycnt[0] % 2
                            ycnt[0] += 1
                            yk = "yst%d" % yi
                            for hb in range(2):
                                i = cnt["rec"] % 2
                                cnt["rec"] += 1
                                rk = "rec%d" % i
                                oa = OA[hb][:, ch * 512:(ch + 1) * 512]
                                P.op("act", lambda e, oa=oa, i=i: e.activation(out=rec[i][0:64], in_=oa[64:128],
                                                                               func=AF.Ln),
                                     reads=["OA%d" % hb], writes=[rk])
                                P.op("act", lambda e, i=i: e.activation(out=rec[i][0:64], in_=rec[i][0:64], func=AF.Exp,
                                                                        scale=-1.0), reads=[rk], writes=[rk])
                                P.op("dve", lambda e, oa=oa, i=i, hb=hb, yi=yi: e.tensor_tensor(
                                    yst[yi][hb * 64:(hb + 1) * 64, 0, :], oa[0:64], rec[i][0:64], ALU.mult),
                                    reads=["OA%d" % hb, rk], writes=[yk], partial=(hb == 1))
                            store_y(0, s, hp, ch, yi, yk)
                P.barrier()
            if "b" in mixers:
                ar = Arena(wbase, SB_BYTES)
                QT = [ar.tile([S], BF16) for _ in range(2)]
                KT = [ar.tile([S], BF16) for _ in range(2)]
                VG = [ar.tile([32, 128], BF16) for _ in range(2)]
                for bb in range(2):
                    P.op("pool", lambda e, bb=bb: e.memset(VG[bb][:, :, 64:128], 1.0), writes=["VG%d" % bb])

                def load_b(h):
                    bb = h % 2
                    dma("sp", QT[bb][0:96], QB_d[h][:, s * S:(s + 1) * S], [], ["QT%d" % bb], "QT%d" % bb)
                    dma("sp", KT[bb][0:96], KB_d[h][:, s * S:(s + 1) * S], [], ["KT%d" % bb], "KT%d" % bb)
                    dma("sp", VG[bb][:, :, 0:64],
                        VB_d[s * S:(s + 1) * S, h * 64:(h + 1) * 64].rearrange("(kb p) x -> p kb x", p=128),
                        [], ["VG%d" % bb], "VG%d" % bb, partial=True)

                LAG = 3
                work = [(h, ch, kb) for h in range(4) for ch in range(8) for kb in range(32)]
                pend = []
                ots = {}
                load_b(0)
                for wi in range(len(work) + LAG):
                    if wi < len(work):
                        h, ch, kb = work[wi]
                        bb = h % 2
                        if ch == 0 and kb == 0 and h + 1 < 4:
                            pass
                        ps, pk = psum()
                        P.op("pe", lambda e, ps=ps, kb=kb, bb=bb, ch=ch: e.matmul(
                            ps, KT[bb][0:96, kb * 128:(kb + 1) * 128], QT[bb][0:96, ch * 512:(ch + 1) * 512],
                            start=True, stop=True), reads=["KT%d" % bb, "QT%d" % bb], writes=[pk])
                        i = cnt["pe"] % NPE
                        cnt["pe"] += 1
                        P.op("act", lambda e, ps=ps, i=i: e.activation(out=pt[i], in_=ps, func=AF.Exp,
                                                                       scale=96.0 ** -0.5),
                             reads=[pk], writes=["pt%d" % i])
                        pend.append((h, ch, kb, i))
                    if wi >= LAG:
                        h, ch, kb, i = pend.pop(0)
                        bb = h % 2
                        if kb == 0:
                            ots[(h, ch)] = psum_acc()
                        ot, otk = ots[(h, ch)]
                        P.op("pe", lambda e, kb=kb, bb=bb, i=i, ot=ot: e.matmul(
                            ot, VG[bb][:, kb, :], pt[i], start=(kb == 0), stop=(kb == 31)),
                            reads=["VG%d" % bb, "pt%d" % i], writes=[otk])
                        if kb == 31:
                            i2 = cnt["rec"] % 2
                            cnt["rec"] += 1
                            rk = "rec%d" % i2
                            P.op("act", lambda e, ot=ot, i2=i2: e.activation(out=rec[i2][0:64], in_=ot[64:128, :],
                                                                            func=AF.Ln), reads=[otk], writes=[rk])
                            P.op("act", lambda e, i2=i2: e.activation(out=rec[i2][0:64], in_=rec[i2][0:64], func=AF.Exp,
                                                                     scale=-1.0), reads=[rk], writes=[rk])
                            ysi = ycnt[0] % 2
                            ycnt[0] += 1
                            yk = "yst%d" % ysi
                            P.op("dve", lambda e, ot=ot, i2=i2, ysi=ysi: e.tensor_tensor(
                                yst[ysi][0:64, 0, :], ot[0:64, :], rec[i2][0:64], ALU.mult),
                                reads=[otk, rk], writes=[yk])
                            dma("pool", yT_d[1, h * 64:(h + 1) * 64, s * S + ch * 512:s * S + (ch + 1) * 512],
                                yst[ysi][0:64, 0, :], [yk], [], "st" + yk)
                            if ch == 0 and h + 1 < 4:
                                load_b(h + 1)
                P.barrier()

    def phase_c(l, x_src, x_dst):
        psn_pool[0] = 8
        ar = Arena(PBASE, SB_BYTES)
        wg = ar.tile([32, D], BF16)
        wb = ar.tile([8, D], BF16)
        wo = ar.tile([8, D], BF16)
        hT = [ar.tile([8, 512], BF16) for _ in range(2)]
        yT = [ar.tile([8, 512], BF16) for _ in range(2)]
        xt = ar.tile([8, 512], F32)
        sg = [ar.tile([512], F32) for _ in range(2)]
        tm = [ar.tile([512], F32) for _ in range(2)]
        acc = ar.tile([512], F32)
        mg = ar.tile([8, 512], BF16)
        for i in range(4):
            for c in range(8):
                k = "wg%d" % (i * 8 + c)
                dma("pool", wg[:, i * 8 + c, :], w_gate_in[l, i, c * 128:(c + 1) * 128, :], [], [k], k)
        dma("pool", wb, w_br_in[l].rearrange("i (c p) n -> p (i c) n", p=128), [], ["wb"], "wb")
        dma("pool", wo, w_out_in[l].rearrange("(c p) n -> p c n", p=128), [], ["wo"], "wo")
        tiles = [(s, T) for s in range(NS) for T in range(8)]

        def load(idx):
            s, T = tiles[idx]
            b = idx % 2
            t0 = s * S + T * 512
            dma("sp", hT[b], hT_d.rearrange("(c p) n -> p c n", p=128)[:, :, t0:t0 + 512], [], ["hT%d" % b], "hT%d" % b)
            dma("sp", yT[b], yT_d.rearrange("m (c p) n -> p (m c) n", p=128)[:, :, t0:t0 + 512], [], ["yT%d" % b],
                "yT%d" % b)

        load(0)
        k2 = 0
        for idx, (s, T) in enumerate(tiles):
            b = idx % 2
            t0 = s * S + T * 512
            if idx + 1 < len(tiles):
                load(idx + 1)
            dma("sp", xt, x_src.rearrange("(c p) n -> p c n", p=128)[:, :, t0:t0 + 512], [], ["xt"], "xt")
            for oc in range(8):
                for i in range(4):
                    psg, pkg = psum()
                    for c in range(8):
                        P.op("pe", lambda e, c=c, i=i, oc=oc, psg=psg: e.matmul(
                            psg, wg[:, i * 8 + c, oc * 128:(oc + 1) * 128], hT[b][:, c, :],
                            start=(c == 0), stop=(c == 7)), reads=["wg%d" % (i * 8 + c), "hT%d" % b], writes=[pkg])
                    psb_, pkb = psum()
                    for c2 in range(2):
                        P.op("pe", lambda e, c2=c2, i=i, oc=oc, psb_=psb_: e.matmul(
                            psb_, wb[:, i * 2 + c2, oc * 128:(oc + 1) * 128], yT[b][:, i * 2 + c2, :],
                            start=(c2 == 0), stop=(c2 == 1)), reads=["wb", "yT%d" % b], writes=[pkb])
                    j = k2 % 2
                    k2 += 1
                    P.op("act", lambda e, psg=psg, j=j: e.activation(out=sg[j], in_=psg, func=AF.Sigmoid), reads=[pkg],
                         writes=["sg%d" % j])
                    if i == 0:
                        P.op("dve", lambda e, psb_=psb_, j=j: e.tensor_tensor(acc, sg[j], psb_, ALU.mult),
                             reads=["sg%d" % j, pkb], writes=["acc"])
                    else:
                        P.op("dve", lambda e, psb_=psb_, j=j: e.tensor_tensor(tm[j], sg[j], psb_, ALU.mult),
                             reads=["sg%d" % j, pkb], writes=["tm%d" % j])
                        if i < 3:
                            P.op("pool", lambda e, j=j: e.tensor_tensor(acc, acc, tm[j], ALU.add),
                                 reads=["tm%d" % j, "acc"], writes=["acc"])
                        else:
                            P.op("pool", lambda e, j=j, oc=oc: e.tensor_tensor(mg[:, oc, :], acc, tm[j], ALU.add),
                                 reads=["tm%d" % j, "acc"], writes=["mg"], partial=True)
            for oc in range(8):
                ps, pk = psum()
                for c in range(8):
                    P.op("pe", lambda e, c=c, oc=oc, ps=ps: e.matmul(ps, wo[:, c, oc * 128:(oc + 1) * 128], mg[:, c, :],
                                                                     start=(c == 0), stop=(c == 7)),
                         reads=["wo", "mg"], writes=[pk])
                P.op("dve", lambda e, oc=oc, ps=ps: e.tensor_tensor(xt[:, oc, :], xt[:, oc, :], ps, ALU.add),
                     reads=[pk, "xt"], writes=["xt"], partial=True)
            dma("pool", x_dst.rearrange("(c p) n -> p c n", p=128)[:, :, t0:t0 + 512], xt, ["xt"], [], "stxt")
        P.barrier()

    def phase_d(l, x_src, x_dst, final):
        psn_pool[0] = 8
        ar = Arena(PBASE, SB_BYTES)
        wg = ar.tile([8, DFF], BF16)
        wu = ar.tile([8, DFF], BF16)
        wd = ar.tile([NFF, D], BF16)
        TW = 256
        xt = [ar.tile([8, TW + 2], F32) for _ in range(2)]
        sq = ar.tile([8, TW + 2], BF16)
        rt = ar.tile([TW + 2], F32)
        rstd = ar.tile([TW + 2], F32)
        h2s = [ar.tile([8, TW + 2], BF16) for _ in range(2)]
        cv = [ar.tile([TW], F32) for _ in range(2)]
        ge = [ar.tile([TW], F32) for _ in range(2)]
        uT = ar.tile([NFF, TW], BF16)
        sqfin, rtf, rstdf = sq, rt, rstd
        for c in range(8):
            dma("pool", wg[:, c, :], w_fg_in[l, c * 128:(c + 1) * 128, :], [], ["fg%d" % c], "fg%d" % c)
            dma("pool", wu[:, c, :], w_fu_in[l, c * 128:(c + 1) * 128, :], [], ["fu%d" % c], "fu%d" % c)
        for f in range(NFF):
            dma("pool", wd[:, f, :], w_fd_in[l, f * 128:(f + 1) * 128, :], [], ["fd%d" % (f % 4)], "fd%d" % (f % 4),
                partial=True)
        fdk = ["fd%d" % i for i in range(4)]
        NTI = S // TW
        tiles = [(s, T) for s in range(NS) for T in range(NTI)]
        xs = x_src.rearrange("(c p) n -> p c n", p=128)

        def load(idx):
            s, T = tiles[idx]
            b = idx % 2
            t0 = s * S + T * TW
            lo = 1 if T == 0 else 0
            hi = TW + 1 if T == NTI - 1 else TW + 2
            k = "xt%d" % b
            if lo == 1:
                P.op("pool", lambda e: e.memset(xt[b][:, :, 0:1], 0.0), writes=[k])
            if hi == TW + 1:
                P.op("pool", lambda e: e.memset(xt[b][:, :, TW + 1:TW + 2], 0.0), writes=[k])
            dma("sp", xt[b][:, :, lo:hi], xs[:, :, t0 - 1 + lo:t0 - 1 + hi], [], [k], k)

        def norm_d(idx):
            b_ = idx % 2
            return norm_stages(xt[b_], "xt%d" % b_, 8, TW + 2, lambda c: gffn[:, l * 8 + c:l * 8 + c + 1], sq, "sq",
                               rt, "rt", rstd, "rstd", h2s[b_], "h2%d" % b_, float(D))

        load(0)
        for st_ in norm_d(0):
            st_()
        k2 = 0
        pend_tail = [None]
        for idx, (s, T) in enumerate(tiles):
            b = idx % 2
            t0 = s * S + T * TW
            nst = None
            if idx + 1 < len(tiles):
                load(idx + 1)
                nst = norm_d(idx + 1)
            xk = "xt%d" % b
            h2 = h2s[b]
            h2k = "h2%d" % b
            for f in range(NFF):
                if nst is not None and f == 10:
                    nst[0]()
                if nst is not None and f == 15:
                    nst[1]()
                    nst[2]()
                psg, pkg = psum()
                for c in range(8):
                    P.op("pe", lambda e, c=c, f=f, psg=psg: e.matmul(psg[:, 0:TW + 2], wg[:, c, f * 128:(f + 1) * 128],
                                                                     h2[:, c, :], start=(c == 0), stop=(c == 7)),
                         reads=["fg%d" % c, h2k], writes=[pkg])
                psu, pku = psum()
                for c in range(8):
                    P.op("pe", lambda e, c=c, f=f, psu=psu: e.matmul(psu[:, 0:TW], wu[:, c, f * 128:(f + 1) * 128],
                                                                     h2[:, c, 1:TW + 1], start=(c == 0), stop=(c == 7)),
                         reads=["fu%d" % c, h2k], writes=[pku])
                j = k2 % 2
                k2 += 1
                cwb = (l * NFF + f) * 3
                P.op("act", lambda e, psg=psg, j=j, cwb=cwb, f=f: e.activation(
                    out=cv[j], in_=psg[:, 1:TW + 1], func=AF.Identity, scale=cw[:, cwb + 1:cwb + 2],
                    bias=cb[:, l * NFF + f:l * NFF + f + 1]), reads=[pkg], writes=["cv%d" % j])
                P.op("dve", lambda e, psg=psg, j=j, cwb=cwb: e.scalar_tensor_tensor(
                    cv[j], psg[:, 0:TW], cw[:, cwb:cwb + 1], cv[j], ALU.mult, ALU.add), reads=[pkg, "cv%d" % j],
                    writes=["cv%d" % j])
                P.op("dve", lambda e, psg=psg, j=j, cwb=cwb: e.scalar_tensor_tensor(
                    cv[j], psg[:, 2:TW + 2], cw[:, cwb + 2:cwb + 3], cv[j], ALU.mult, ALU.add), reads=[pkg, "cv%d" % j],
                    writes=["cv%d" % j])
                def tail(j=j, f=f, psu=psu, pku=pku):
                    P.op("act", lambda e: e.activation(out=ge[j], in_=cv[j], func=AF.Gelu_apprx_tanh),
                         reads=["cv%d" % j], writes=["ge%d" % j])
                    P.op("dve", lambda e: e.tensor_tensor(uT[:, f, :], ge[j], psu[:, 0:TW], ALU.mult),
                         reads=["ge%d" % j, pku], writes=["uT"], partial=True)

                if pend_tail[0] is not None:
                    pend_tail[0]()
                pend_tail[0] = tail
            pend_tail[0]()
            pend_tail[0] = None
            xo = xt[b][:, :, 1:TW + 1]
            for oc in range(8):
                ps, pk = psum()
                for f in range(NFF):
                    P.op("pe", lambda e, f=f, oc=oc, ps=ps: e.matmul(ps[:, 0:TW], wd[:, f, oc * 128:(oc + 1) * 128],
                                                                     uT[:, f, :], start=(f == 0), stop=(f == NFF - 1)),
                         reads=fdk + ["uT"], writes=[pk])
                P.op("dve", lambda e, oc=oc, ps=ps: e.tensor_tensor(xt[b][:, oc, 1:TW + 1], xt[b][:, oc, 1:TW + 1],
                                                                    ps[:, 0:TW], ALU.add),
                     reads=[pk, xk], writes=[xk], partial=True)
            if not final:
                dma("pool", x_dst.rearrange("(c p) n -> p c n", p=128)[:, :, t0:t0 + TW], xo, [xk], [], "st" + xk)
            else:
                sqf = sqfin[:, :, 0:TW]
                P.op("act", lambda e: e.activation(out=sqf, in_=xo, func=AF.Square), reads=[xk], writes=["sq"])
                ps, pk = psum()
                for c in range(8):
                    P.op("pe", lambda e, c=c, ps=ps: e.matmul(ps[:, 0:TW], ones_bf, sqfin[:, c, 0:TW], start=(c == 0),
                                                              stop=(c == 7)), reads=["sq", "ones"], writes=[pk])
                P.op("act", lambda e, ps=ps: e.activation(out=rtf[:, 0:TW], in_=ps[:, 0:TW], func=AF.Ln,
                                                          scale=1.0 / D, bias=epsc), reads=[pk, "epsc"], writes=["rt"])
                P.op("act", lambda e: e.activation(out=rstdf[:, 0:TW], in_=rtf[:, 0:TW], func=AF.Exp, scale=-0.5),
                     reads=["rt"], writes=["rstd"])
                for c in range(8):
                    P.op("dve", lambda e, c=c: e.scalar_tensor_tensor(xt[b][:, c, 1:TW + 1], xt[b][:, c, 1:TW + 1],
                                                                      gfin[:, c:c + 1], rstdf[:, 0:TW], ALU.mult,
                                                                      ALU.mult),
                         reads=[xk, "rstd"], writes=[xk], partial=True)
                dma("pool", outT.rearrange("(c p) n -> p c n", p=128)[:, :, t0:t0 + TW], xo, [xk], [], "st" + xk)
        P.barrier()

    setup()
    x_cur = xT_in
    for l in range(L):
        if "A" in phases:
            phase_a(l, x_cur)
        if "B" in phases:
            phase_b(l)
        if "C" in phases:
            x_c = xa_d if l == 0 else x_cur
            phase_c(l, x_cur, x_c)
        else:
            x_c = x_cur
        if "D" in phases:
            x_n = xb_d if x_c is xa_d else xa_d
            phase_d(l, x_c, x_n, final=(l == L - 1))
            x_cur = x_n
    P.barrier()
    P.emit(st)
    st.close()
    return nc


_NC_CACHE = {}


def _host_inputs(inp, core, NS, L):
    f = np.float32
    x = np.asarray(inp["x"], f)
    xs = x[core * NS:(core + 1) * NS]
    xT = np.ascontiguousarray(xs.reshape(NS * S, D).T)

    def cols(v, n):
        v = np.asarray(v, f).reshape(-1, n, 128)
        return np.ascontiguousarray(v.transpose(2, 0, 1).reshape(128, -1))

    m = dict(
        xT=xT,
        t5=np.asarray(inp["t5_table"], f),
        gmix=cols(inp["norm_mix_g"][:L], 8),
        gffn=cols(inp["norm_ffn_g"][:L], 8),
        gfin=cols(np.asarray(inp["final_g"])[None], 8),
        gq=cols(inp["q_norm_g"][:L], 2),
        gkv=cols(inp["kv_norm_g"][:L], 1),
        cb=cols(inp["conv_b"][:L], NFF),
        snk=np.ascontiguousarray(np.broadcast_to(np.asarray(inp["sink_logit"][:L], f).reshape(1, -1), (128, L * 4))),
        nab=np.ascontiguousarray(np.asarray(inp["na_bias"][:L], f).transpose(0, 3, 1, 2).reshape(L, 31, 60)),
        w_in=np.asarray(inp["w_in"][:L], f), w_uq=np.asarray(inp["w_uq"][:L], f),
        w_ukv=np.asarray(inp["w_ukv"][:L], f), w_gate=np.asarray(inp["w_gate"][:L], f),
        w_branch=np.asarray(inp["w_branch"][:L], f), w_out=np.asarray(inp["w_out"][:L], f),
        w_ffn_gate=np.asarray(inp["w_ffn_gate"][:L], f), w_ffn_up=np.asarray(inp["w_ffn_up"][:L], f),
        w_ffn_down=np.asarray(inp["w_ffn_down"][:L], f),
    )
    cwv = np.asarray(inp["conv_w"][:L], f)
    cwv = cwv.reshape(L, 3, NFF, 128).transpose(3, 0, 2, 1)
    m["cw"] = np.ascontiguousarray(cwv.reshape(128, -1))
    m.update(_consts())
    return m


def kernel(**inputs):
    NS, L = 2, DEPTH
    key = (NS, L)
    if key not in _NC_CACHE:
        _NC_CACHE[key] = build_nc(NS, L)
    nc = _NC_CACHE[key]
    n = 8
    in_maps = [_host_inputs(inputs, c, NS, L) for c in range(n)]
    res = run_bass_kernel_spmd(nc, in_maps, core_ids=list(range(n)))
    outs = []
    for c in range(n):
        oT = np.asarray(res.results[c]["outT"], np.float32)
        outs.append(oT.T.reshape(NS, S, D))
    return np.ascontiguousarray(np.concatenate(outs, 0))
```

```python
import math
from contextlib import ExitStack

import numpy as np
import concourse.bass as bass
import concourse.mybir as mybir
from concourse.bass_utils import run_bass_kernel_spmd

F32 = mybir.dt.float32
BF16 = mybir.dt.bfloat16
U8 = mybir.dt.uint8
AF = mybir.ActivationFunctionType
ALU = mybir.AluOpType

S = 4096
D = 1024
NCH = 8
DEPTH = 2
INC = 4000
DFF = 2816
NFF = 22
EPS = 1e-6
SEM_EPOCH = 24000
SB_BYTES = 189 * 1024

RT_D = 511
RT_A = 383
RT = RT_D + 3 * RT_A
DIL = (1, 4, 16)


class _Op:
    __slots__ = ("idx", "eng", "fn", "lane", "deps", "inc", "count", "dma_key")

    def __init__(self, idx, eng, fn, lane, dma_key):
        self.idx = idx
        self.eng = eng
        self.fn = fn
        self.lane = lane
        self.deps = {}
        self.inc = False
        self.count = None
        self.dma_key = dma_key


class _Rec:
    def __init__(self):
        self.call = None

    def __getattr__(self, name):
        def f(*a, **k):
            self.call = (name, a, k)
            return None
        return f


class Prog:
    ENGS = ("pe", "act", "dve", "pool", "sp")

    def __init__(self, nc):
        self.nc = nc
        self.ops = []
        self.by_eng = {e: [] for e in self.ENGS}
        self.res = {}
        self.dma_count = {}
        self.pool_map = {}
        self.last = {}

    def _state(self, key):
        st = self.res.get(key)
        if st is None:
            st = [{}, {}]
            self.res[key] = st
        return st

    def op(self, eng, fn, reads=(), writes=(), sem=None, partial=False):
        dma = sem is not None
        if dma:
            key = self.pool_map.get(sem)
            if key is None:
                key = len(self.pool_map)
                self.pool_map[sem] = key
            lane = ("dma", key)
        else:
            key = None
            lane = eng
        if fn is not None:
            rec = _Rec()
            fn(rec)
            fn = rec.call
        o = _Op(len(self.ops), eng, fn, lane, key)
        if dma:
            self.dma_count[key] = self.dma_count.get(key, 0) + 1
            o.count = self.dma_count[key]

        def add_dep(d, raw=False):
            if d.lane == o.lane and not dma:
                if not raw or eng == "pe":
                    return
            cur = o.deps.get(d.lane)
            if cur is None or d.idx > cur.idx:
                o.deps[d.lane] = d

        for r in reads:
            w, rd = self._state(r)
            for d in w.values():
                add_dep(d, raw=True)
        for wkey in writes:
            w, rd = self._state(wkey)
            if not (partial and not rd):
                for d in w.values():
                    add_dep(d)
            for d in rd.values():
                add_dep(d)
        for r in reads:
            w, rd = self._state(r)
            rd[o.lane] = o
        for wkey in writes:
            st = self._state(wkey)
            if partial and not st[1]:
                st[0][o.lane] = o
            else:
                st[0] = {o.lane: o}
                st[1] = {}
        for d in o.deps.values():
            if d.dma_key is None:
                d.inc = True
        self.ops.append(o)
        self.by_eng[eng].append(o)
        self.last[lane] = o
        return o

    def barrier(self, engs=None):
        lasts = list(self.last.values())
        for e in (engs or self.ENGS):
            o = _Op(len(self.ops), e, None, e, None)
            for d in lasts:
                if d.lane == e:
                    continue
                o.deps[d.lane] = d
                if d.dma_key is None:
                    d.inc = True
            self.ops.append(o)
            self.by_eng[e].append(o)
        self.res = {}
        self.pool_map = {}
        self.last = {}

    def emit(self, stack):
        nc = self.nc
        sems = {}

        def get_sem(name):
            s = sems.get(name)
            if s is None:
                s = stack.enter_context(nc.semaphore("s%d" % len(sems)))
                sems[name] = s
            return s

        for e in self.ENGS:
            c = 0
            for o in self.by_eng[e]:
                if o.dma_key is None and o.inc:
                    c += 1
                    o.count = c

        def sem_for(o):
            if o.dma_key is None:
                ep, v = divmod(o.count - 1, SEM_EPOCH)
                return get_sem(("c", o.lane, ep)), v + 1
            per = SEM_EPOCH // 16
            ep, v = divmod(o.count - 1, per)
            return get_sem(("d", o.dma_key, ep)), (v + 1) * 16

        for o in self.ops:
            if o.dma_key is not None or o.inc:
                sem_for(o)
        block = stack.enter_context(nc.Block())
        deco = {"pe": block.tensor, "act": block.scalar, "dve": block.vector,
                "pool": block.gpsimd, "sp": block.sync}
        for e in self.ENGS:
            ops = self.by_eng[e]
            if not ops:
                continue

            def body(engh, ops=ops):
                waited = {}
                for o in ops:
                    for d in o.deps.values():
                        s, v = sem_for(d)
                        k = id(s)
                        if waited.get(k, 0) >= v:
                            continue
                        waited[k] = v
                        engh.wait_ge(s, v)
                    if o.fn is None:
                        continue
                    name, a, k = o.fn
                    ins = getattr(engh, name)(*a, **k)
                    if o.dma_key is not None:
                        s, v = sem_for(o)
                        ins.then_inc(s, 16)
                    elif o.inc:
                        s, v = sem_for(o)
                        ins.then_inc(s, 1)

            deco[e](body)
        self.n_sems = len(sems)


def _t5_bucket(rel):
    half = 16
    exact = 8
    n = np.abs(rel)
    large = exact + (np.log(np.maximum(n, 1) / exact) / math.log(1024 / exact)
                     * (half - exact)).astype(np.int32)
    large = np.minimum(large, half - 1)
    return (np.where(rel > 0, half, 0) + np.where(n < exact, n, large)).astype(np.int32)


def _c_rs(qr):
    return min(max(qr - 4, 0), 56)


def _c_sig(i, j):
    sig = []
    for krl in range(2):
        for qrl in range(2):
            kr = 2 * j + krl
            qr = 2 * i + qrl
            rs = _c_rs(qr)
            if rs <= kr < rs + 8:
                sig.append(kr - qr + 7)
            else:
                sig.append(None)
    return tuple(sig)


def _c_tiles():
    sigs = []
    table = []
    for i in range(32):
        row = []
        for j in range(32):
            sg = _c_sig(i, j)
            if all(v is None for v in sg):
                continue
            if sg not in sigs:
                sigs.append(sg)
            row.append((j, sigs.index(sg)))
        table.append(row)
    return sigs, table


C_SIGS, C_TABLE = _c_tiles()
NCT = len(C_SIGS)


def _consts():
    ohv = np.zeros((32, RT), np.float32)
    msk = np.zeros((16, RT), np.float32)
    u = np.arange(RT_D)
    rel = u - 255
    val = np.abs(rel) <= 128
    b = _t5_bucket(rel)
    ohv[b[val], u[val]] = 1.0
    msk[:, u[val]] = 1.0
    for g, dil in enumerate(DIL):
        off = RT_D + g * RT_A
        u = np.arange(RT_A)
        j = u - 191
        val = np.abs(j) <= 64
        b = _t5_bucket(j * dil)
        ohv[b[val], off + u[val]] = 1.0
        msk[:, off + u[val]] = 1.0
    jf = np.zeros((128, 128), np.float32)
    jf[np.arange(128), 127 - np.arange(128)] = 1.0
    ohc = np.zeros((31, 64, 64), np.float32)
    mc = np.zeros((64, 64), np.float32)
    for qc in range(64):
        cs = min(max(qc - 8, 0), 48)
        for kc in range(cs, cs + 16):
            ohc[kc - qc + 15, kc, qc] = 1.0
            mc[kc, qc] = 1.0
    maskc2 = np.concatenate([mc, mc], 0)
    inv = 10000.0 ** (-np.arange(16, dtype=np.float32) / 16)
    ang = np.arange(S, dtype=np.float32)[None, :] * inv[:, None]
    cos2 = np.concatenate([np.cos(ang), np.cos(ang)], 0).astype(np.float32)
    sin2 = np.concatenate([np.sin(ang), np.sin(ang)], 0).astype(np.float32)
    return dict(c_ohv=ohv, c_msk=msk, c_jf=jf, c_ohc=ohc.reshape(31, 4096),
                c_maskc=maskc2, c_cos=cos2, c_sin=sin2)


def build_nc(NS=2, L=DEPTH, dbg=False, phases="ABCD", mixers="abcd"):
    nc = bass.Bass("TRN2", target_bir_lowering=False)
    NT = NS * S

    def din(name, shape, dt=F32):
        return nc.dram_tensor(name, list(shape), dt, kind="ExternalInput").ap()

    def dscr(name, shape, dt=BF16):
        kind = "ExternalOutput" if dbg else "Internal"
        return nc.dram_tensor(name, list(shape), dt, kind=kind).ap()

    xT_in = din("xT", [D, NT])
    t5_in = din("t5", [32, 16])
    gmix_in = din("gmix", [128, L * 8])
    gffn_in = din("gffn", [128, L * 8])
    gfin_in = din("gfin", [128, 8])
    gq_in = din("gq", [128, L * 2])
    gkv_in = din("gkv", [128, L])
    cw_in = din("cw", [128, L * NFF * 3])
    cb_in = din("cb", [128, L * NFF])
    snk_in = din("snk", [128, L * 4])
    nab_in = din("nab", [L, 31, 60])
    w_in_in = din("w_in", [L, D, INC])
    w_uq_in = din("w_uq", [L, 256, 384])
    w_ukv_in = din("w_ukv", [L, 128, 512])
    w_gate_in = din("w_gate", [L, 4, D, D])
    w_br_in = din("w_branch", [L, 4, 256, D])
    w_out_in = din("w_out", [L, D, D])
    w_fg_in = din("w_ffn_gate", [L, D, DFF])
    w_fu_in = din("w_ffn_up", [L, D, DFF])
    w_fd_in = din("w_ffn_down", [L, DFF, D])
    c_ohv = din("c_ohv", [32, RT])
    c_msk = din("c_msk", [16, RT])
    c_jf = din("c_jf", [128, 128])
    c_ohc = din("c_ohc", [31, 4096])
    c_maskc = din("c_maskc", [128, 64])
    c_cos = din("c_cos", [32, S])
    c_sin = din("c_sin", [32, S])

    outT = nc.dram_tensor("outT", [D, NT], F32, kind="ExternalOutput").ap()

    evec_d = dscr("evec_d", [16, RT], F32)
    mcol_d = dscr("mcol_d", [60, 64, 64], F32)
    hT_d = dscr("hT_d", [D, NT])
    QA_d = dscr("QA_d", [3, 2, 128, NT])
    KA_d = dscr("KA_d", [3, 2, 128, NT])
    VA_d = dscr("VA_d", [NT, 768])
    QB_d = dscr("QB_d", [4, 96, NT])
    KB_d = dscr("KB_d", [4, 96, NT])
    VB_d = dscr("VB_d", [NT, 256])
    QC_d = dscr("QC_d", [2, 128, NT])
    KC_d = dscr("KC_d", [2, 128, NT])
    VC_d = dscr("VC_d", [NT, 256])
    QD_d = dscr("QD_d", [2, 128, NT])
    KD_d = dscr("KD_d", [128, NT])
    VD_d = dscr("VD_d", [NT, 128])
    yT_d = dscr("yT_d", [4, 256, NT])
    xa_d = dscr("xa_d", [D, NT], F32)
    xb_d = dscr("xb_d", [D, NT], F32)

    st = ExitStack()
    P = Prog(nc)
    big = st.enter_context(nc.sbuf_tensor("big", [128, SB_BYTES], U8))
    psb = [st.enter_context(nc.psum_tensor("ps%d" % i, [128, 512], F32))[:, :] for i in range(8)]

    class Arena:
        def __init__(self, base, limit):
            self.off = base
            self.limit = limit

        def tile(self, shape, dt):
            n = 1
            for v in shape:
                n *= v
            nb = n * (4 if dt == F32 else 2)
            nb_al = (nb + 63) // 64 * 64
            assert self.off + nb_al <= self.limit, ("SBUF arena overflow", self.off, nb_al, self.limit)
            ap = big[:, self.off:self.off + nb].bitcast(dt)
            self.off += nb_al
            if len(shape) == 2:
                ap = ap.rearrange("p (a b) -> p a b", a=shape[0])
            elif len(shape) == 3:
                ap = ap.rearrange("p (a b c) -> p a b c", a=shape[0], b=shape[1])
            return ap

    pa = Arena(0, 16 * 1024)
    gmix = pa.tile([L * 8], F32)
    gffn = pa.tile([L * 8], F32)
    gfin = pa.tile([8], F32)
    gq = pa.tile([L * 2], F32)
    gkv = pa.tile([L], F32)
    cw = pa.tile([L * NFF * 3], F32)
    cb = pa.tile([L * NFF], F32)
    snk = pa.tile([L * 4], F32)
    epsc = pa.tile([1], F32)
    ones_bf = pa.tile([128], BF16)
    jf_bf = pa.tile([128], BF16)
    EB_A = pa.tile([24, 128], BF16)
    EB_D = pa.tile([12, 128], BF16)
    PBASE = pa.off

    pscnt = [0, 0]

    psn_pool = [6]

    def psum():
        i = pscnt[0] % psn_pool[0]
        pscnt[0] += 1
        return psb[i], "ps%d" % i

    def psum_acc():
        i = 4 + pscnt[1] % 2
        pscnt[1] += 1
        return psb[i], "ps%d" % i

    def dma(eng, out, in_, reads, writes, sem, partial=False):
        P.op(eng, lambda e: e.dma_start(out=out, in_=in_), reads=reads, writes=writes,
             sem=sem, partial=partial)

    def setup():
        ar = Arena(PBASE, SB_BYTES)
        for i, (t, src) in enumerate([(gmix, gmix_in), (gffn, gffn_in), (gfin, gfin_in), (gq, gq_in),
                                      (gkv, gkv_in), (cw, cw_in), (cb, cb_in), (snk, snk_in)]):
            dma("sp", t, src, [], ["sv%d" % i], "sv%d" % i)
        dma("pool", jf_bf, c_jf, [], ["jf"], "jf")
        P.op("dve", lambda e: e.memset(ones_bf, 1.0), writes=["ones"])
        P.op("dve", lambda e: e.memset(epsc, EPS), writes=["epsc"])
        P.op("act", lambda e: e.activation(out=snk, in_=snk, func=AF.Exp), reads=["sv7"], writes=["sv7"])
        t5f = ar.tile([16], F32)
        t5hi = ar.tile([16], BF16)
        t5hf = ar.tile([16], F32)
        t5lo = ar.tile([16], BF16)
        ohv = ar.tile([RT], BF16)
        mskt = ar.tile([RT], F32)
        evec = ar.tile([RT], F32)
        hk = ar.tile([36, 128], BF16)
        dma("sp", t5f[0:32], t5_in, [], ["t5f"], "t5f")
        dma("pool", ohv[0:32], c_ohv, [], ["ohv"], "ohv")
        dma("sp", mskt[0:16], c_msk, [], ["mskt"], "mskt")
        P.op("dve", lambda e: e.tensor_copy(t5hi[0:32], t5f[0:32]), reads=["t5f"], writes=["t5hi"])
        P.op("dve", lambda e: e.tensor_copy(t5hf[0:32], t5hi[0:32]), reads=["t5hi"], writes=["t5hf"])
        P.op("dve", lambda e: e.tensor_tensor(t5lo[0:32], t5f[0:32], t5hf[0:32], ALU.subtract),
             reads=["t5f", "t5hf"], writes=["t5lo"])
        ncol = 415
        for k in range(4):
            ps, pk = psum()
            c0 = k * ncol
            P.op("pe", lambda e, ps=ps, c0=c0: e.matmul(ps[0:16, 0:ncol], t5hi[0:32, :], ohv[0:32, c0:c0 + ncol],
                                                        start=True, stop=False),
                 reads=["t5hi", "ohv"], writes=[pk])
            P.op("pe", lambda e, ps=ps, c0=c0: e.matmul(ps[0:16, 0:ncol], t5lo[0:32, :], ohv[0:32, c0:c0 + ncol],
                                                        start=False, stop=True),
                 reads=["t5lo", "ohv"], writes=[pk])
            P.op("act", lambda e, ps=ps, c0=c0: e.activation(out=evec[0:16, c0:c0 + ncol], in_=ps[0:16, 0:ncol],
                                                             func=AF.Exp),
                 reads=[pk], writes=["evec"], partial=True)
        P.op("dve", lambda e: e.tensor_tensor(evec[0:16], evec[0:16], mskt[0:16], ALU.mult),
             reads=["evec", "mskt"], writes=["evec"])
        dma("sp", evec_d, evec[0:16], ["evec"], [], "evst")
        P.barrier()
        tiles = []
        for h in range(4):
            for d in (-1, 0, 1):
                tiles.append((12 + h, 0 + d * 128 + 128))
        for g in range(3):
            for h in range(4):
                for ab in range(2):
                    tiles.append((4 * g + h, RT_D + g * RT_A + (0 if ab == 0 else 128)))
        for t, (row, u0) in enumerate(tiles):
            src = bass.AP(evec_d.tensor, row * RT + u0, [[1, 128], [1, 128]])
            dma("pool", hk[:, t, :], src, [], ["hk%d" % t], "hk%d" % t)
        for b in range(9):
            ps, pk = psum()
            for q in range(4):
                t = b * 4 + q
                P.op("pe", lambda e, ps=ps, t=t, q=q: e.matmul(ps[:, q * 128:(q + 1) * 128], hk[:, t, :], jf_bf,
                                                               start=True, stop=True),
                     reads=["hk%d" % t, "jf"], writes=[pk])
            if b < 3:
                dst = EB_D[:, b * 4:(b + 1) * 4, :]
            else:
                dst = EB_A[:, (b - 3) * 4:(b - 2) * 4, :]
            P.op("dve", lambda e, ps=ps, dst=dst: e.tensor_copy(dst, ps.rearrange("p (a b) -> p a b", a=4)),
                 reads=[pk], writes=["EB"], partial=True)
        P.barrier()

    def norm_stages(xt, xkey, nch, n, gcol, sq, sqkey, rt, rtkey, rstd, rstdkey, hT, hkey, feat):
        box = {}

        def s1():
            P.op("act", lambda e: e.activation(out=sq, in_=xt, func=AF.Square), reads=[xkey], writes=[sqkey])

        def s2():
            ps, pk = psum()
            box["ps"] = (ps, pk)
            for c in range(nch):
                P.op("pe", lambda e, c=c: e.matmul(ps[:, 0:n], ones_bf, sq[:, c, :], start=(c == 0), stop=(c == nch - 1)),
                     reads=[sqkey, "ones"], writes=[pk])

        def s3():
            ps, pk = box["ps"]
            P.op("act", lambda e: e.activation(out=rt, in_=ps[:, 0:n], func=AF.Ln, scale=1.0 / feat, bias=epsc),
                 reads=[pk, "epsc"], writes=[rtkey])
            P.op("act", lambda e: e.activation(out=rstd, in_=rt, func=AF.Exp, scale=-0.5), reads=[rtkey],
                 writes=[rstdkey])
            for c in range(nch):
                P.op("dve", lambda e, c=c: e.scalar_tensor_tensor(hT[:, c, :], xt[:, c, :], gcol(c), rstd,
                                                                  ALU.mult, ALU.mult),
                     reads=[xkey, rstdkey], writes=[hkey], partial=True)

        return s1, s2, s3

    def rms_norm(*a):
        for st_ in norm_stages(*a):
            st_()

    def phase_a(l, x_src):
        psn_pool[0] = 8
        ar = Arena(PBASE, SB_BYTES)
        w_in = ar.tile([8, INC], BF16)
        wkrot = ar.tile([8, 96], BF16)
        wuq = ar.tile([2, 384], BF16)
        wuqr = ar.tile([2, 4, 96], BF16)
        wukv = ar.tile([512], BF16)
        xt = [ar.tile([8, 512], F32) for _ in range(2)]
        sq = ar.tile([8, 512], BF16)
        rt = ar.tile([512], F32)
        rstd = ar.tile([512], F32)
        hT = [ar.tile([8, 512], BF16) for _ in range(2)]
        cs = [ar.tile([2, 512], F32) for _ in range(1)]
        NSTG = 4
        stg = [ar.tile([512], BF16) for _ in range(NSTG)]
        vst = [ar.tile([4, 1152], BF16) for _ in range(1)]
        cq = ar.tile([2, 512], F32)
        cqsq = ar.tile([2, 512], BF16)
        cqn = ar.tile([2, 512], BF16)
        ckv = ar.tile([1, 512], F32)
        ckvsq = ar.tile([1, 512], BF16)
        ckvn = ar.tile([1, 512], BF16)
        rt2 = ar.tile([512], F32)
        rs2 = ar.tile([512], F32)
        rt3 = ar.tile([512], F32)
        rs3 = ar.tile([512], F32)
        kr = ar.tile([512], F32)
        t1 = ar.tile([512], F32)
        t2 = ar.tile([512], F32)
        qst = [ar.tile([512], BF16) for _ in range(2)]
        kst = [ar.tile([512], BF16) for _ in range(2)]
        vbst = [ar.tile([4, 256], BF16) for _ in range(1)]

        for c in range(8):
            dma("pool", w_in[:, c, :], w_in_in[l, c * 128:(c + 1) * 128, :], [], ["w_in%d" % c], "w_in%d" % c)
        dma("pool", wuq, w_uq_in[l].rearrange("(c p) n -> p c n", p=128), [], ["wuq"], "wuq")
        dma("pool", wukv, w_ukv_in[l], [], ["wukv"], "wukv")
        allw = ["w_in%d" % c for c in range(8)]
        P.op("dve", lambda e: e.memset(wkrot, 0.0), writes=["wkrot"])
        P.op("dve", lambda e: e.tensor_scalar(wkrot[:, :, 64:80], w_in[:, :, 2704:2720], -1.0, 0.0, ALU.mult, ALU.add),
             reads=allw, writes=["wkrot"])
        P.op("dve", lambda e: e.tensor_copy(wkrot[:, :, 80:96], w_in[:, :, 2688:2704]), reads=allw, writes=["wkrot"])
        P.op("dve", lambda e: e.memset(wuqr, 0.0), writes=["wuqr"])
        for h in range(4):
            P.op("dve", lambda e, h=h: e.tensor_scalar(wuqr[:, :, h, 64:80], wuq[:, :, h * 96 + 80:h * 96 + 96],
                                                       -1.0, 0.0, ALU.mult, ALU.add), reads=["wuq"], writes=["wuqr"])
            P.op("dve", lambda e, h=h: e.tensor_copy(wuqr[:, :, h, 80:96], wuq[:, :, h * 96 + 64:h * 96 + 80]),
                 reads=["wuq"], writes=["wuqr"])

        tiles = [(s, T) for s in range(NS) for T in range(8)]

        def load(idx):
            s, T = tiles[idx]
            b = idx % 2
            t0 = s * S + T * 512
            dma("sp", xt[b], x_src.rearrange("(c p) n -> p c n", p=128)[:, :, t0:t0 + 512], [], ["xt%d" % b],
                "xt%d" % b)

        evac_rr = [0]

        def evac(dst, src, reads, writes, partial=False):
            evac_rr[0] += 1
            if evac_rr[0] % 2:
                P.op("act", lambda e: e.copy(dst, src), reads=reads, writes=writes, partial=partial)
            else:
                P.op("dve", lambda e: e.tensor_copy(dst, src), reads=reads, writes=writes, partial=partial)

        stg_rr = [0]

        def norm_a(idx):
            s_, T_ = tiles[idx]
            b_ = idx % 2
            t0_ = s_ * S + T_ * 512
            rms_norm(xt[b_], "xt%d" % b_, 8, 512, lambda c: gmix[:, l * 8 + c:l * 8 + c + 1], sq, "sq", rt, "rt",
                     rstd, "rstd", hT[b_], "hT%d" % b_, float(D))
            dma("pool", hT_d.rearrange("(c p) n -> p c n", p=128)[:, :, t0_:t0_ + 512], hT[b_], ["hT%d" % b_], [],
                "sthT%d" % b_)

        load(0)
        if len(tiles) > 1:
            load(1)
        norm_a(0)
        for idx, (s, T) in enumerate(tiles):
            b = idx % 2
            t0 = s * S + T * 512
            if idx + 1 < len(tiles):
                norm_a(idx + 1)
            if idx + 2 < len(tiles):
                load(idx + 2)
            dma("sp", cs[0][64:96, 0, :], c_cos[:, T * 512:(T + 1) * 512], [], ["cs0"], "cs0", partial=True)
            dma("sp", cs[0][64:96, 1, :], c_sin[:, T * 512:(T + 1) * 512], [], ["cs0"], "cs0", partial=True)
            xk, hk_ = "xt%d" % b, "hT%d" % b

            def fm_chunk(col, M, w=None, wkey=None):
                ps, pk = psum()
                for c in range(8):
                    if w is None:
                        lhsT = w_in[:, c, col:col + M]
                        rk = "w_in%d" % c
                    else:
                        lhsT = w[:, c, col:col + M]
                        rk = wkey
                    P.op("pe", lambda e, c=c, lhsT=lhsT: e.matmul(ps[0:M, :], lhsT, hT[b][:, c, :],
                                                                  start=(c == 0), stop=(c == 7)),
                         reads=[rk, hk_], writes=[pk])
                return ps, pk

            def out_chunk(ps, pk, dst_dram, dil=1):
                i = stg_rr[0] % NSTG
                stg_rr[0] += 1
                sk = "stg%d" % i
                if dil == 1:
                    evac(stg[i], ps, [pk], [sk])
                    dma("pool", dst_dram[:, t0:t0 + 512], stg[i], [sk], [], "st" + sk)
                else:
                    J = 512 // dil
                    evac(stg[i].rearrange("p (r j) -> p j r", r=dil), ps.rearrange("p (j r) -> p j r", r=dil),
                         [pk], [sk])
                    Lg = S // dil
                    dst = dst_dram[:, s * S:(s + 1) * S].rearrange("p (r j) -> p r j", r=dil)[:, :, T * J:(T + 1) * J]
                    dma("pool", dst, stg[i].rearrange("p (r j) -> p r j", r=dil), [sk], [], "st" + sk)

            for c2 in range(2):
                ps, pk = fm_chunk(2304 + c2 * 128, 128)
                evac(cq[:, c2, :], ps, [pk], ["cq"], partial=True)
            ps, pk = fm_chunk(2560, 128)
            evac(ckv[:, 0, :], ps, [pk], ["ckv"])
            nq = norm_stages(cq, "cq", 2, 512, lambda c: gq[:, l * 2 + c:l * 2 + c + 1], cqsq, "cqsq", rt2, "rt2",
                             rs2, "rs2", cqn, "cqn", 256.0)
            nkv = norm_stages(ckv, "ckv", 1, 512, lambda c: gkv[:, l:l + 1], ckvsq, "ckvsq", rt3, "rt3", rs3, "rs3",
                              ckvn, "ckvn", 128.0)
            nq[0]()
            nkv[0]()
            psk, pkk = fm_chunk(2624, 96)
            psr, pkr = fm_chunk(0, 96, w=wkrot, wkey="wkrot")
            csk = "cs0"
            P.op("dve", lambda e: e.tensor_tensor(t1[64:96], psk[64:96, :], cs[0][64:96, 0, :], ALU.mult),
                 reads=[pkk, csk], writes=["t1"])
            P.op("dve", lambda e: e.tensor_tensor(t2[64:96], psr[64:96, :], cs[0][64:96, 1, :], ALU.mult),
                 reads=[pkr, csk], writes=["t2"])
            P.op("dve", lambda e: e.tensor_tensor(kr[64:96], t1[64:96], t2[64:96], ALU.add),
                 reads=["t1", "t2"], writes=["kr"])
            for g in range(3):
                for hp in range(1):
                    ps, pk = fm_chunk((g * 4 + hp * 2) * 64, 128)
                    out_chunk(ps, pk, QA_d[g, hp], DIL[g])
                    ps, pk = fm_chunk(768 + (g * 4 + hp * 2) * 64, 128)
                    out_chunk(ps, pk, KA_d[g, hp], DIL[g])
            nq[1]()
            nq[2]()
            nkv[1]()
            nkv[2]()
            for g in range(3):
                for hp in range(1, 2):
                    ps, pk = fm_chunk((g * 4 + hp * 2) * 64, 128)
                    out_chunk(ps, pk, QA_d[g, hp], DIL[g])
                    ps, pk = fm_chunk(768 + (g * 4 + hp * 2) * 64, 128)
                    out_chunk(ps, pk, KA_d[g, hp], DIL[g])
            for h in range(4):
                hb = h % 2
                psq, pkq = psum()
                psq2, pkq2 = psum()
                for c2 in range(2):
                    P.op("pe", lambda e, c2=c2: e.matmul(psq[0:96, :], wuq[:, c2, h * 96:(h + 1) * 96], cqn[:, c2, :],
                                                         start=(c2 == 0), stop=(c2 == 1)),
                         reads=["wuq", "cqn"], writes=[pkq])
                for c2 in range(2):
                    P.op("pe", lambda e, c2=c2: e.matmul(psq2[0:96, :], wuqr[:, c2, h, :], cqn[:, c2, :],
                                                         start=(c2 == 0), stop=(c2 == 1)),
                         reads=["wuqr", "cqn"], writes=[pkq2])
                qk = "qst%d" % hb
                P.op("dve", lambda e: e.tensor_copy(qst[hb][0:64], psq[0:64, :]), reads=[pkq], writes=[qk])
                P.op("dve", lambda e: e.tensor_tensor(t1[64:96], psq[64:96, :], cs[0][64:96, 0, :], ALU.mult),
                     reads=[pkq, csk], writes=["t1"])
                P.op("dve", lambda e: e.tensor_tensor(t2[64:96], psq2[64:96, :], cs[0][64:96, 1, :], ALU.mult),
                     reads=[pkq2, csk], writes=["t2"])
                P.op("dve", lambda e: e.tensor_tensor(qst[hb][64:96], t1[64:96], t2[64:96], ALU.add),
                     reads=["t1", "t2", qk], writes=[qk], partial=True)
                dma("pool", QB_d[h][:, t0:t0 + 512], qst[hb][0:96], [qk], [], "st" + qk)
                psn, pkn = psum()
                P.op("pe", lambda e: e.matmul(psn[0:64, :], wukv[:, h * 128:h * 128 + 64], ckvn[:, 0, :],
                                              start=True, stop=True), reads=["wukv", "ckvn"], writes=[pkn])
                kk = "kst%d" % hb
                evac(kst[hb][0:64], psn[0:64, :], [pkn], [kk])
                P.op("pool", lambda e: e.tensor_copy(kst[hb][64:96], kr[64:96]), reads=["kr", kk], writes=[kk],
                     partial=True)
                dma("pool", KB_d[h][:, t0:t0 + 512], kst[hb][0:96], [kk], [], "st" + kk)
            vb = 0
            vbk = "vbst%d" % vb
            wv = wukv.rearrange("p (h x) -> p h x", h=4)[:, :, 64:128]
            for tb in range(4):
                ps, pk = psum()
                P.op("pe", lambda e, tb=tb, ps=ps: e.matmul(ps[:, 0:256].rearrange("p (h x) -> p h x", h=4),
                                                            ckvn[:, 0, tb * 128:(tb + 1) * 128], wv,
                                                            start=True, stop=True),
                     reads=["wukv", "ckvn"], writes=[pk])
                evac(vbst[vb][:, tb, :], ps[:, 0:256], [pk], [vbk], partial=True)
            dma("pool", VB_d[t0:t0 + 512, :].rearrange("(tb p) f -> p tb f", p=128), vbst[vb], [vbk], [], "st" + vbk)

            for hp in range(2):
                ps, pk = fm_chunk(2720 + hp * 128, 128)
                out_chunk(ps, pk, QC_d[hp])
                ps, pk = fm_chunk(2976 + hp * 128, 128)
                out_chunk(ps, pk, KC_d[hp])
                ps, pk = fm_chunk(3488 + hp * 128, 128)
                out_chunk(ps, pk, QD_d[hp])
            ps, pk = fm_chunk(3744, 128)
            out_chunk(ps, pk, KD_d)

            vk = "vst%d" % vb
            groups = [(1536, 512, 0), (2048, 256, 512), (3232, 256, 768), (3872, 128, 1024)]
            for tb in range(4):
                for (col, n, so) in groups:
                    ps, pk = psum()
                    for c in range(8):
                        P.op("pe", lambda e, c=c, ps=ps, col=col, n=n: e.matmul(
                            ps[:, 0:n], hT[b][:, c, tb * 128:(tb + 1) * 128], w_in[:, c, col:col + n],
                            start=(c == 0), stop=(c == 7)), reads=["w_in%d" % c, hk_], writes=[pk])
                    evac(vst[vb][:, tb, so:so + n], ps[:, 0:n], [pk], [vk], partial=True)
            dma("pool", VA_d[t0:t0 + 512, :].rearrange("(tb p) f -> p tb f", p=128), vst[vb][:, :, 0:768], [vk], [],
                "stva%d" % vb)
            dma("pool", VC_d[t0:t0 + 512, :].rearrange("(tb p) f -> p tb f", p=128), vst[vb][:, :, 768:1024], [vk], [],
                "stvc%d" % vb)
            dma("pool", VD_d[t0:t0 + 512, :].rearrange("(tb p) f -> p tb f", p=128), vst[vb][:, :, 1024:1152], [vk], [],
                "stvd%d" % vb)
        P.barrier()

    def build_ebc(l, ar):
        EBC = ar.tile([4 * NCT, 128], BF16)
        sub = Arena(ar.off, SB_BYTES)
        nbf = sub.tile([60], F32)
        nbhi = sub.tile([60], BF16)
        nbhf = sub.tile([60], F32)
        nblo = sub.tile([60], BF16)
        ohc = sub.tile([4096], BF16)
        mcs = sub.tile([4096], F32)
        mcolS = sub.tile([60, 64], BF16)
        mk2 = sub.tile([64], BF16)
        dma("sp", nbf[0:31], nab_in[l], [], ["nbf"], "nbf")
        dma("pool", ohc[0:31], c_ohc, [], ["ohc"], "ohc")
        dma("pool", mk2, c_maskc, [], ["mk2"], "mk2")
        P.op("dve", lambda e: e.tensor_copy(nbhi[0:31], nbf[0:31]), reads=["nbf"], writes=["nbhi"])
        P.op("dve", lambda e: e.tensor_copy(nbhf[0:31], nbhi[0:31]), reads=["nbhi"], writes=["nbhf"])
        P.op("dve", lambda e: e.tensor_tensor(nblo[0:31], nbf[0:31], nbhf[0:31], ALU.subtract),
             reads=["nbf", "nbhf"], writes=["nblo"])
        for k in range(8):
            ps, pk = psum()
            P.op("pe", lambda e, ps=ps, k=k: e.matmul(ps[0:60, :], nbhi[0:31, :], ohc[0:31, k * 512:(k + 1) * 512],
                                                      start=True, stop=False), reads=["nbhi", "ohc"], writes=[pk])
            P.op("pe", lambda e, ps=ps, k=k: e.matmul(ps[0:60, :], nblo[0:31, :], ohc[0:31, k * 512:(k + 1) * 512],
                                                      start=False, stop=True), reads=["nblo", "ohc"], writes=[pk])
            P.op("act", lambda e, ps=ps, k=k: e.activation(out=mcs[0:60, k * 512:(k + 1) * 512], in_=ps[0:60, :],
                                                           func=AF.Exp), reads=[pk], writes=["mcs"], partial=True)
        dma("sp", mcol_d.rearrange("m a b -> m (a b)"), mcs[0:60], ["mcs"], [], "stmcs")
        P.barrier()
        src = mcol_d.rearrange("m kc qc -> kc m qc")
        dma("pool", mcolS[0:64], src, [], ["mcolS"], "mcolS", partial=True)
        dma("pool", mcolS[64:128], src, [], ["mcolS"], "mcolS", partial=True)
        P.op("dve", lambda e: e.tensor_tensor(mcolS, mcolS, mk2.unsqueeze(1).to_broadcast([128, 60, 64]), ALU.mult),
             reads=["mcolS", "mk2"], writes=["mcolS"])
        P.op("pool", lambda e: e.memset(EBC, 0.0), writes=["EBC"])
        rr = 0
        for h in range(4):
            for ti, sg in enumerate(C_SIGS):
                k = 0
                for krl in range(2):
                    for qrl in range(2):
                        dr = sg[k]
                        k += 1
                        if dr is None:
                            continue
                        dst = EBC[krl * 64:(krl + 1) * 64, h * NCT + ti, qrl * 64:(qrl + 1) * 64]
                        srcv = mcolS[krl * 64:(krl + 1) * 64, h * 15 + dr, :]
                        eng = ("dve", "pool")[rr % 2]
                        rr += 1
                        P.op(eng, lambda e, dst=dst, srcv=srcv: e.tensor_copy(dst, srcv), reads=["mcolS", "EBC"],
                             writes=["EBC"], partial=True)
        P.barrier()
        return EBC

    def phase_b(l):
        psn_pool[0] = 4
        ar0 = Arena(PBASE, SB_BYTES)
        EBC = build_ebc(l, ar0)
        base = ar0.off
        NPE = 12
        pexp = [ar0.tile([512], BF16) for _ in range(NPE)]
        pt = [ar0.tile([512], BF16) for _ in range(NPE)]
        rec = [ar0.tile([512], F32) for _ in range(2)]
        yst = [ar0.tile([1, 512], BF16) for _ in range(2)]
        wbase = ar0.off
        cnt = {"pe": 0, "rec": 0, "mul": 0}

        def score_slot(qbs, eb, scale, ebkey, zero=()):
            ps, pk = psum()
            for q, ent in enumerate(qbs):
                if ent is None:
                    continue
                kT, qT, lo, hi, rds = ent
                if lo == 0 and hi == 128:
                    P.op("pe", lambda e, kT=kT, qT=qT, q=q: e.matmul(ps[:, q * 128:(q + 1) * 128], kT, qT,
                                                                     start=True, stop=True), reads=rds, writes=[pk])
                else:
                    pb = kT.base_partition()
                    P.op("pe", lambda e, kT=kT, qT=qT, q=q, lo=lo, hi=hi, pb=pb: e.matmul(
                        ps[lo:hi, q * 128:(q + 1) * 128], kT, qT, start=True, stop=True, tile_position=(pb, lo)),
                        reads=rds, writes=[pk])
            i = cnt["pe"] % NPE
            cnt["pe"] += 1
            if eb is None:
                P.op("act", lambda e: e.activation(out=pt[i], in_=ps, func=AF.Exp, scale=scale), reads=[pk],
                     writes=["pt%d" % i])
                return pt[i], "pt%d" % i
            P.op("act", lambda e: e.activation(out=pexp[i], in_=ps, func=AF.Exp, scale=scale), reads=[pk],
                 writes=["pexp%d" % i])
            eng = ("dve", "pool")[cnt["mul"] % 2]
            cnt["mul"] += 1
            if isinstance(eb, list):
                for q, ebq in enumerate(eb):
                    if ebq is None or qbs[q] is None:
                        continue
                    P.op(eng, lambda e, q=q, ebq=ebq: e.tensor_tensor(pt[i][:, q * 128:(q + 1) * 128],
                                                                      pexp[i][:, q * 128:(q + 1) * 128], ebq, ALU.mult),
                         reads=["pexp%d" % i, ebkey], writes=["pt%d" % i], partial=(q > 0))
            else:
                P.op(eng, lambda e: e.tensor_tensor(pt[i].rearrange("p (a b) -> p a b", a=4),
                                                    pexp[i].rearrange("p (a b) -> p a b", a=4),
                                                    eb.unsqueeze(1).to_broadcast([128, 4, 128]), ALU.mult),
                     reads=["pexp%d" % i, ebkey], writes=["pt%d" % i])
                for (q, zlo, zhi) in zero:
                    P.op(eng, lambda e, q=q, zlo=zlo, zhi=zhi: e.memset(pt[i][zlo:zhi, q * 128:(q + 1) * 128], 0.0),
                         writes=["pt%d" % i], partial=True)
            return pt[i], "pt%d" % i

        def pv(ot, otk, pts, vents):
            for q in range(4):
                lst = [(sl, vents[sl][q]) for sl in range(len(pts)) if vents[sl][q] is not None]
                for n, (sl, (va, lo, hi, rds)) in enumerate(lst):
                    ptile, ptk = pts[sl]
                    P.op("pe", lambda e, va=va, lo=lo, hi=hi, ptile=ptile, q=q, n=n, last=len(lst) - 1: e.matmul(
                        ot[:, q * 128:(q + 1) * 128], va, ptile[lo:hi, q * 128:(q + 1) * 128],
                        start=(n == 0), stop=(n == last)), reads=rds + [ptk], writes=[otk])

        def normalize(ot, otk, dst, dstkey, addcol=None, partial=True):
            i = cnt["rec"] % 2
            cnt["rec"] += 1
            rk = "rec%d" % i
            if addcol is not None:
                P.op("act", lambda e: e.activation(out=rec[i][0:64], in_=ot[64:128, :], func=AF.Ln, bias=addcol),
                     reads=[otk], writes=[rk])
            else:
                P.op("act", lambda e: e.activation(out=rec[i][0:64], in_=ot[64:128, :], func=AF.Ln),
                     reads=[otk], writes=[rk])
            P.op("act", lambda e: e.activation(out=rec[i][0:64], in_=rec[i][0:64], func=AF.Exp, scale=-1.0),
                 reads=[rk], writes=[rk])
            P.op("dve", lambda e: e.tensor_tensor(dst, ot[0:64, :], rec[i][0:64], ALU.mult), reads=[otk, rk],
                 writes=[dstkey], partial=partial)

        def store_y(m, s, c2, ch, ysti, yk):
            dma("pool", yT_d[m, c2 * 128:(c2 + 1) * 128, s * S + ch * 512:s * S + (ch + 1) * 512], yst[ysti][:, 0, :],
                [yk], [], "st" + yk)

        ycnt = [0]

        def gen_units(units, lag=1):
            for ui in range(len(units) + lag):
                if ui < len(units):
                    units[ui][0]()
                    yield
                if ui >= lag:
                    units[ui - lag][1]()
                    yield

        def mix_d(s):
            ar = Arena(wbase2, SB_BYTES)
            QT = [ar.tile([S], BF16) for _ in range(2)]
            KT = [ar.tile([S], BF16) for _ in range(2)]
            VG = ar.tile([32, 2, 128], BF16)
            for hp in range(2):
                dma("sp", QT[hp], QD_d[hp][:, s * S:(s + 1) * S], [], ["QT%d" % hp], "QT%d" % hp)
                for hb in range(2):
                    dma("sp", KT[hp][hb * 64:(hb + 1) * 64], KD_d[hp * 64:(hp + 1) * 64, s * S:(s + 1) * S], [],
                        ["KT%d" % hp], "KT%d" % hp, partial=True)
            P.op("pool", lambda e: e.memset(VG[:, :, :, 64:128], 1.0), writes=["VG"])
            for g_ in range(2):
                dma("sp", VG[:, :, g_, 0:64],
                    VD_d[s * S:(s + 1) * S, g_ * 64:(g_ + 1) * 64].rearrange("(kb p) x -> p kb x", p=128),
                    [], ["VG"], "VG", partial=True)
            units = []
            for hp in range(2):
                for ch in range(8):
                    yi = ycnt[0] % 2
                    ycnt[0] += 1
                    yk = "yst%d" % yi
                    for hb in range(2):
                        h = hp * 2 + hb
                        kvh = h // 2
                        box = {}

                        def st1(hp=hp, ch=ch, hb=hb, h=h, kvh=kvh, box=box):
                            pts = []
                            vents = []
                            for d in (-1, 0, 1):
                                qbs = []
                                vv = []
                                for q in range(4):
                                    i = ch * 4 + q
                                    j = i + d
                                    if j < 0 or j > 31:
                                        qbs.append(None)
                                        vv.append(None)
                                        continue
                                    qbs.append((KT[hp][hb * 64:(hb + 1) * 64, j * 128:(j + 1) * 128],
                                                QT[hp][hb * 64:(hb + 1) * 64, i * 128:(i + 1) * 128], 0, 128,
                                                ["KT%d" % hp, "QT%d" % hp]))
                                    vv.append((VG[:, j, kvh, :], 0, 128, ["VG"]))
                                pts.append(score_slot(qbs, EB_D[:, h * 3 + d + 1, :], 0.125, "EB"))
                                vents.append(vv)
                            box["pts"] = pts
                            box["vents"] = vents

                        def st2(hp=hp, ch=ch, hb=hb, h=h, box=box, yi=yi, yk=yk):
                            ot, otk = psum_acc()
                            pv(ot, otk, box["pts"], box["vents"])
                            normalize(ot, otk, yst[yi][hb * 64:(hb + 1) * 64, 0, :], yk,
                                      addcol=snk[64:128, l * 4 + h:l * 4 + h + 1], partial=(hb == 1))
                            if hb == 1:
                                store_y(3, s, hp, ch, yi, yk)

                        units.append((st1, st2))
            yield from gen_units(units)

            yield

        def mix_c(s):
            ar = Arena(wbase2, SB_BYTES)
            QT = [ar.tile([S], BF16) for _ in range(2)]
            KT = [ar.tile([S], BF16) for _ in range(2)]
            VG = ar.tile([32, 4, 128], BF16)
            for hp in range(2):
                dma("sp", QT[hp], QC_d[hp][:, s * S:(s + 1) * S], [], ["QT%d" % hp], "QT%d" % hp)
                dma("sp", KT[hp], KC_d[hp][:, s * S:(s + 1) * S], [], ["KT%d" % hp], "KT%d" % hp)
            P.op("pool", lambda e: e.memset(VG[:, :, :, 64:128], 1.0), writes=["VG"])
            for g_ in range(4):
                dma("sp", VG[:, :, g_, 0:64],
                    VC_d[s * S:(s + 1) * S, g_ * 64:(g_ + 1) * 64].rearrange("(kb p) x -> p kb x", p=128),
                    [], ["VG"], "VG", partial=True)
            units = []
            for hp in range(2):
                for ch in range(8):
                    yi = ycnt[0] % 2
                    ycnt[0] += 1
                    yk = "yst%d" % yi
                    dlist = sorted({j - (ch * 4 + q) for q in range(4) for (j, _) in C_TABLE[ch * 4 + q]})
                    for hb in range(2):
                        h = hp * 2 + hb
                        box = {}

                        def st1(hp=hp, ch=ch, hb=hb, h=h, box=box, dlist=dlist):
                            pts = []
                            vents = []
                            for d in dlist:
                                qbs = []
                                vv = []
                                ebl = []
                                for q in range(4):
                                    i = ch * 4 + q
                                    j = i + d
                                    ti = dict(C_TABLE[i]).get(j)
                                    if ti is None:
                                        qbs.append(None)
                                        vv.append(None)
                                        ebl.append(None)
                                        continue
                                    qbs.append((KT[hp][hb * 64:(hb + 1) * 64, j * 128:(j + 1) * 128],
                                                QT[hp][hb * 64:(hb + 1) * 64, i * 128:(i + 1) * 128], 0, 128,
                                                ["KT%d" % hp, "QT%d" % hp]))
                                    vv.append((VG[:, j, h, :], 0, 128, ["VG"]))
                                    ebl.append(EBC[:, h * NCT + ti, :])
                                tis = {dict(C_TABLE[ch * 4 + q]).get(ch * 4 + q + d) for q in range(4)}
                                if len(tis) == 1 and None not in tis:
                                    eb = ebl[0]
                                else:
                                    eb = ebl
                                pts.append(score_slot(qbs, eb, 0.125, "EBC"))
                                vents.append(vv)
                            box["pts"] = pts
                            box["vents"] = vents

                        def st2(hp=hp, ch=ch, hb=hb, box=box, yi=yi, yk=yk):
                            ot, otk = psum_acc()
                            pv(ot, otk, box["pts"], box["vents"])
                            normalize(ot, otk, yst[yi][hb * 64:(hb + 1) * 64, 0, :], yk, partial=(hb == 1))
                            if hb == 1:
                                store_y(2, s, hp, ch, yi, yk)

                        units.append((st1, st2))
            yield from gen_units(units)

            yield

        def mix_a(s):
            ar = Arena(wbase2, SB_BYTES)
            QTs = [ar.tile([S], BF16) for _ in range(1)]
            KTs = [ar.tile([S + 128], BF16) for _ in range(1)]
            VGs = [ar.tile([48, 2, 128], BF16) for _ in range(1)]
            OA = [ar.tile([S], F32) for _ in range(2)]
            for z in range(1):
                P.op("pool", lambda e, z=z: e.memset(VGs[z], 0.0), writes=["VG%d" % z])
                P.op("pool", lambda e, z=z: e.memset(VGs[z][:, :, :, 64:128], 1.0), writes=["VG%d" % z])
                P.op("pool", lambda e, z=z: e.memset(KTs[z][:, 0:64], 0.0), writes=["KTpad%d" % z])
                P.op("pool", lambda e, z=z: e.memset(KTs[z][:, S + 64:S + 128], 0.0), writes=["KTpad%d" % z])
            groups = [(hp, g) for hp in range(2) for g in range(3)]

            def load_a(gidx):
                hp, g = groups[gidx]
                z = 0
                dil = DIL[g]
                Lg = S // dil
                nb = Lg // 128
                dma("sp", QTs[z], QA_d[g, hp][:, s * S:(s + 1) * S], [], ["QT%d" % z], "QT%d" % z)
                dma("sp", KTs[z][:, 64:64 + S], KA_d[g, hp][:, s * S:(s + 1) * S], [], ["KT%d" % z], "KT%d" % z)
                vsrc = VA_d[s * S:(s + 1) * S, :].rearrange("(j r) (g x) -> r j g x", r=dil, g=3)
                for r in range(dil):
                    for m in range(nb + 1):
                        lo = 64 if m == 0 else 0
                        hi = 64 if m == nb else 128
                        j0 = 128 * m - 64 + lo
                        src = vsrc[r, j0:j0 + (hi - lo), g, :].rearrange("j (h x) -> j h x", h=4)[
                            :, hp * 2:hp * 2 + 2, :]
                        dma("sp", VGs[z][lo:hi, r * (nb + 1) + m, :, 0:64], src, [], ["VG%d" % z], "VG%d" % z,
                            partial=True)

            for gidx, (hp, g) in enumerate(groups):
                z = 0
                QT, KT, VG = QTs[z], KTs[z], VGs[z]
                qk_, kk_, vk_, kp_ = "QT%d" % z, "KT%d" % z, "VG%d" % z, "KTpad%d" % z
                load_a(gidx)
                dil = DIL[g]
                Lg = S // dil
                nb = Lg // 128
                nqb = S // 128
                units = []
                for ch in range(nqb // 4):
                    for hb in range(2):
                        h = hp * 2 + hb
                        box = {}

                        def st1(ch=ch, hb=hb, h=h, box=box, g=g, Lg=Lg, nb=nb, QT=QT, KT=KT, VG=VG, qk_=qk_,
                                kk_=kk_, vk_=vk_, kp_=kp_):
                            pts = []
                            vents = []
                            for ab in range(2):
                                qbs = []
                                vv = []
                                zer = []
                                for q in range(4):
                                    gi = ch * 4 + q
                                    r, i = divmod(gi, nb)
                                    m = i + ab
                                    k0 = 64 + r * Lg + 128 * m - 64
                                    qbs.append((KT[hb * 64:(hb + 1) * 64, k0:k0 + 128],
                                                QT[hb * 64:(hb + 1) * 64, gi * 128:(gi + 1) * 128], 0, 128,
                                                [kk_, kp_, qk_]))
                                    vv.append((VG[:, r * (nb + 1) + m, hb, :], 0, 128, [vk_]))
                                    if m == 0:
                                        zer.append((q, 0, 64))
                                    elif m == nb:
                                        zer.append((q, 64, 128))
                                pts.append(score_slot(qbs, EB_A[:, (g * 4 + h) * 2 + ab, :], 0.125, "EB",
                                                      zero=zer))
                                vents.append(vv)
                            box["pts"] = pts
                            box["vents"] = vents

                        def st2(ch=ch, hb=hb, box=box, dil=dil, Lg=Lg):
                            ot, otk = psum_acc()
                            pv(ot, otk, box["pts"], box["vents"])
                            if dil == 1:
                                dst = OA[hb][:, ch * 512:(ch + 1) * 512]
                                P.op("act", lambda e, dst=dst, ot=ot: e.copy(dst, ot), reads=[otk],
                                     writes=["OA%d" % hb], partial=True)
                            else:
                                oav = OA[hb].rearrange("p (j r) -> p r j", r=dil)
                                pos0 = ch * 512
                                done = 0
                                while done < 512:
                                    r, j = divmod(pos0 + done, Lg)
                                    n = min(512 - done, Lg - j)
                                    dst = oav[:, r, j:j + n]
                                    srcv = ot[:, done:done + n]
                                    P.op("dve", lambda e, dst=dst, srcv=srcv: e.tensor_tensor(dst, dst, srcv,
                                                                                              ALU.add),
                                         reads=[otk, "OA%d" % hb], writes=["OA%d" % hb], partial=True)
                                    done += n

                        units.append((st1, st2))
                yield from gen_units(units)
                if g == 2:
                    for ch in range(8):
                        yi = ycnt[0] % 2
                        ycnt[0] += 1
                        yk = "yst%d" % yi
                        for hb in range(2):
                            i = cnt["rec"] % 2
                            cnt["rec"] += 1
                            rk = "rec%d" % i
                            oa = OA[hb][:, ch * 512:(ch + 1) * 512]
                            P.op("act", lambda e, oa=oa, i=i: e.activation(out=rec[i][0:64], in_=oa[64:128],
                                                                           func=AF.Ln),
                                 reads=["OA%d" % hb], writes=[rk])
                            P.op("act", lambda e, i=i: e.activation(out=rec[i][0:64], in_=rec[i][0:64], func=AF.Exp,
                                                                    scale=-1.0), reads=[rk], writes=[rk])
                            P.op("dve", lambda e, oa=oa, i=i, hb=hb, yi=yi: e.tensor_tensor(
                                yst[yi][hb * 64:(hb + 1) * 64, 0, :], oa[0:64], rec[i][0:64], ALU.mult),
                                reads=["OA%d" % hb, rk], writes=[yk], partial=(hb == 1))
                        store_y(0, s, hp, ch, yi, yk)
                        yield

            yield

        arm = Arena(wbase, SB_BYTES)
        NPM = 6
        ptm = [arm.tile([512], BF16) for _ in range(NPM)]
        ystm = [arm.tile([512], BF16) for _ in range(2)]
        QTm = [arm.tile([S], BF16) for _ in range(2)]
        KTm = [arm.tile([S], BF16) for _ in range(2)]
        VGm = [arm.tile([32, 128], BF16) for _ in range(1)]
        wbase2 = arm.off
        mcnt = {"pt": 0, "acc": 0, "y": 0}

        def psum_accm():
            i = 6 + mcnt["acc"] % 2
            mcnt["acc"] += 1
            return psb[i], "ps%d" % i

        def mla_gen(s):
            P.op("pool", lambda e: e.memset(VGm[0][:, :, 64:128], 1.0), writes=["VGm0"])

            def load_b(h):
                bb = h % 2
                dma("sp", QTm[bb][0:96], QB_d[h][:, s * S:(s + 1) * S], [], ["QTm%d" % bb], "QTm%d" % bb)
                dma("sp", KTm[bb][0:96], KB_d[h][:, s * S:(s + 1) * S], [], ["KTm%d" % bb], "KTm%d" % bb)

            def load_v(h):
                dma("sp", VGm[0][:, :, 0:64],
                    VB_d[s * S:(s + 1) * S, h * 64:(h + 1) * 64].rearrange("(kb p) x -> p kb x", p=128),
                    [], ["VGm0"], "VGm0", partial=True)

            LAG = 3
            work = [(h, ch, kb) for h in range(4) for ch in range(8) for kb in range(32)]
            pend = []
            ots = {}
            load_b(0)
            load_v(0)
            for wi in range(len(work) + LAG):
                if wi < len(work):
                    h, ch, kb = work[wi]
                    bb = h % 2
                    ps, pk = psum()
                    P.op("pe", lambda e, ps=ps, kb=kb, bb=bb, ch=ch: e.matmul(
                        ps, KTm[bb][0:96, kb * 128:(kb + 1) * 128], QTm[bb][0:96, ch * 512:(ch + 1) * 512],
                        start=True, stop=True), reads=["KTm%d" % bb, "QTm%d" % bb], writes=[pk])
                    i = mcnt["pt"] % NPM
                    mcnt["pt"] += 1
                    P.op("act", lambda e, ps=ps, i=i: e.activation(out=ptm[i], in_=ps, func=AF.Exp,
                                                                   scale=96.0 ** -0.5),
                         reads=[pk], writes=["ptm%d" % i])
                    pend.append((h, ch, kb, i))
                if wi >= LAG:
                    h, ch, kb, i = pend.pop(0)
                    bb = h % 2
                    if kb == 0:
                        ots[(h, ch)] = psum_accm()
                    ot, otk = ots[(h, ch)]
                    P.op("pe", lambda e, kb=kb, i=i, ot=ot: e.matmul(
                        ot, VGm[0][:, kb, :], ptm[i], start=(kb == 0), stop=(kb == 31)),
                        reads=["VGm0", "ptm%d" % i], writes=[otk])
                    if kb == 31:
                        i2 = cnt["rec"] % 2
                        cnt["rec"] += 1
                        rk = "rec%d" % i2
                        P.op("act", lambda e, ot=ot, i2=i2: e.activation(out=rec[i2][0:64], in_=ot[64:128, :],
                                                                        func=AF.Ln), reads=[otk], writes=[rk])
                        P.op("act", lambda e, i2=i2: e.activation(out=rec[i2][0:64], in_=rec[i2][0:64], func=AF.Exp,
                                                                 scale=-1.0), reads=[rk], writes=[rk])
                        ysi = mcnt["y"] % 2
                        mcnt["y"] += 1
                        yk = "ystm%d" % ysi
                        P.op("dve", lambda e, ot=ot, i2=i2, ysi=ysi: e.tensor_tensor(
                            ystm[ysi][0:64, :], ot[0:64, :], rec[i2][0:64], ALU.mult),
                            reads=[otk, rk], writes=[yk])
                        dma("pool", yT_d[1, h * 64:(h + 1) * 64, s * S + ch * 512:s * S + (ch + 1) * 512],
                            ystm[ysi][0:64, :], [yk], [], "st" + yk)
                        if ch == 0 and h + 1 < 4:
                            load_b(h + 1)
                        if ch == 7 and h + 1 < 4:
                            load_v(h + 1)
                yield

        def drain(gen):
            for _ in gen:
                pass

        for s in range(NS):
            mla = mla_gen(s) if "b" in mixers else iter(())
            n_mla = 1024 + 3
            bands = []
            if "d" in mixers:
                bands.append((mix_d, 130))
            if "c" in mixers:
                bands.append((mix_c, 130))
            if "a" in mixers:
                bands.append((mix_a, 230))
            n_band = sum(n for _, n in bands) or 1
            ratio = n_mla / float(n_band)
            credit = 0.0
            mla_done = False
            for (mk, _) in bands:
                for _ in mk(s):
                    credit += ratio
                    while credit >= 1.0 and not mla_done:
                        credit -= 1.0
                        try:
                            next(mla)
                        except StopIteration:
                            mla_done = True
                P.barrier()
            if not mla_done:
                drain(mla)
            P.barrier()

    def phase_c(l, x_src, x_dst):
        psn_pool[0] = 8
        ar = Arena(PBASE, SB_BYTES)
        wg = ar.tile([32, D], BF16)
        wb = ar.tile([8, D], BF16)
        wo = ar.tile([8, D], BF16)
        hT = [ar.tile([8, 512], BF16) for _ in range(2)]
        yT = [ar.tile([8, 512], BF16) for _ in range(2)]
        xt = ar.tile([8, 512], F32)
        sg = [ar.tile([512], F32) for _ in range(2)]
        tm = [ar.tile([512], F32) for _ in range(2)]
        acc = ar.tile([512], F32)
        mg = ar.tile([8, 512], BF16)
        for i in range(4):
            for c in range(8):
                k = "wg%d" % (i * 8 + c)
                dma("pool", wg[:, i * 8 + c, :], w_gate_in[l, i, c * 128:(c + 1) * 128, :], [], [k], k)
        dma("pool", wb, w_br_in[l].rearrange("i (c p) n -> p (i c) n", p=128), [], ["wb"], "wb")
        dma("pool", wo, w_out_in[l].rearrange("(c p) n -> p c n", p=128), [], ["wo"], "wo")
        tiles = [(s, T) for s in range(NS) for T in range(8)]

        def load(idx):
            s, T = tiles[idx]
            b = idx % 2
            t0 = s * S + T * 512
            dma("sp", hT[b], hT_d.rearrange("(c p) n -> p c n", p=128)[:, :, t0:t0 + 512], [], ["hT%d" % b], "hT%d" % b)
            dma("sp", yT[b], yT_d.rearrange("m (c p) n -> p (m c) n", p=128)[:, :, t0:t0 + 512], [], ["yT%d" % b],
                "yT%d" % b)

        load(0)
        k2 = 0
        for idx, (s, T) in enumerate(tiles):
            b = idx % 2
            t0 = s * S + T * 512
            if idx + 1 < len(tiles):
                load(idx + 1)
            dma("sp", xt, x_src.rearrange("(c p) n -> p c n", p=128)[:, :, t0:t0 + 512], [], ["xt"], "xt")
            for oc in range(8):
                for i in range(4):
                    psg, pkg = psum()
                    for c in range(8):
                        P.op("pe", lambda e, c=c, i=i, oc=oc, psg=psg: e.matmul(
                            psg, wg[:, i * 8 + c, oc * 128:(oc + 1) * 128], hT[b][:, c, :],
                            start=(c == 0), stop=(c == 7)), reads=["wg%d" % (i * 8 + c), "hT%d" % b], writes=[pkg])
                    psb_, pkb = psum()
                    for c2 in range(2):
                        P.op("pe", lambda e, c2=c2, i=i, oc=oc, psb_=psb_: e.matmul(
                            psb_, wb[:, i * 2 + c2, oc * 128:(oc + 1) * 128], yT[b][:, i * 2 + c2, :],
                            start=(c2 == 0), stop=(c2 == 1)), reads=["wb", "yT%d" % b], writes=[pkb])
                    j = k2 % 2
                    k2 += 1
                    P.op("act", lambda e, psg=psg, j=j: e.activation(out=sg[j], in_=psg, func=AF.Sigmoid), reads=[pkg],
                         writes=["sg%d" % j])
                    if i == 0:
                        P.op("dve", lambda e, psb_=psb_, j=j: e.tensor_tensor(acc, sg[j], psb_, ALU.mult),
                             reads=["sg%d" % j, pkb], writes=["acc"])
                    else:
                        P.op("dve", lambda e, psb_=psb_, j=j: e.tensor_tensor(tm[j], sg[j], psb_, ALU.mult),
                             reads=["sg%d" % j, pkb], writes=["tm%d" % j])
                        if i < 3:
                            P.op("pool", lambda e, j=j: e.tensor_tensor(acc, acc, tm[j], ALU.add),
                                 reads=["tm%d" % j, "acc"], writes=["acc"])
                        else:
                            P.op("pool", lambda e, j=j, oc=oc: e.tensor_tensor(mg[:, oc, :], acc, tm[j], ALU.add),
                                 reads=["tm%d" % j, "acc"], writes=["mg"], partial=True)
            for oc in range(8):
                ps, pk = psum()
                for c in range(8):
                    P.op("pe", lambda e, c=c, oc=oc, ps=ps: e.matmul(ps, wo[:, c, oc * 128:(oc + 1) * 128], mg[:, c, :],
                                                                     start=(c == 0), stop=(c == 7)),
                         reads=["wo", "mg"], writes=[pk])
                P.op("dve", lambda e, oc=oc, ps=ps: e.tensor_tensor(xt[:, oc, :], xt[:, oc, :], ps, ALU.add),
                     reads=[pk, "xt"], writes=["xt"], partial=True)
            dma("pool", x_dst.rearrange("(c p) n -> p c n", p=128)[:, :, t0:t0 + 512], xt, ["xt"], [], "stxt")
        P.barrier()

    def phase_d(l, x_src, x_dst, final):
        psn_pool[0] = 8
        ar = Arena(PBASE, SB_BYTES)
        wg = ar.tile([8, DFF], BF16)
        wu = ar.tile([8, DFF], BF16)
        wd = ar.tile([NFF, D], BF16)
        TW = 256
        xt = [ar.tile([8, TW + 2], F32) for _ in range(2)]
        sq = ar.tile([8, TW + 2], BF16)
        rt = ar.tile([TW + 2], F32)
        rstd = ar.tile([TW + 2], F32)
        h2s = [ar.tile([8, TW + 2], BF16) for _ in range(2)]
        cv = [ar.tile([TW], F32) for _ in range(2)]
        ge = [ar.tile([TW], F32) for _ in range(2)]
        uT = ar.tile([NFF, TW], BF16)
        sqfin, rtf, rstdf = sq, rt, rstd
        for c in range(8):
            dma("pool", wg[:, c, :], w_fg_in[l, c * 128:(c + 1) * 128, :], [], ["fg%d" % c], "fg%d" % c)
            dma("pool", wu[:, c, :], w_fu_in[l, c * 128:(c + 1) * 128, :], [], ["fu%d" % c], "fu%d" % c)
        for f in range(NFF):
            dma("pool", wd[:, f, :], w_fd_in[l, f * 128:(f + 1) * 128, :], [], ["fd%d" % (f % 4)], "fd%d" % (f % 4),
                partial=True)
        fdk = ["fd%d" % i for i in range(4)]
        NTI = S // TW
        tiles = [(s, T) for s in range(NS) for T in range(NTI)]
        xs = x_src.rearrange("(c p) n -> p c n", p=128)

        def load(idx):
            s, T = tiles[idx]
            b = idx % 2
            t0 = s * S + T * TW
            lo = 1 if T == 0 else 0
            hi = TW + 1 if T == NTI - 1 else TW + 2
            k = "xt%d" % b
            if lo == 1:
                P.op("pool", lambda e: e.memset(xt[b][:, :, 0:1], 0.0), writes=[k])
            if hi == TW + 1:
                P.op("pool", lambda e: e.memset(xt[b][:, :, TW + 1:TW + 2], 0.0), writes=[k])
            dma("sp", xt[b][:, :, lo:hi], xs[:, :, t0 - 1 + lo:t0 - 1 + hi], [], [k], k)

        def norm_d(idx):
            b_ = idx % 2
            return norm_stages(xt[b_], "xt%d" % b_, 8, TW + 2, lambda c: gffn[:, l * 8 + c:l * 8 + c + 1], sq, "sq",
                               rt, "rt", rstd, "rstd", h2s[b_], "h2%d" % b_, float(D))

        load(0)
        for st_ in norm_d(0):
            st_()
        k2 = 0
        pend_tail = [None]
        for idx, (s, T) in enumerate(tiles):
            b = idx % 2
            t0 = s * S + T * TW
            nst = None
            if idx + 1 < len(tiles):
                load(idx + 1)
                nst = norm_d(idx + 1)
            xk = "xt%d" % b
            h2 = h2s[b]
            h2k = "h2%d" % b
            for f in range(NFF):
                if nst is not None and f == 10:
                    nst[0]()
                if nst is not None and f == 15:
                    nst[1]()
                    nst[2]()
                psg, pkg = psum()
                for c in range(8):
                    P.op("pe", lambda e, c=c, f=f, psg=psg: e.matmul(psg[:, 0:TW + 2], wg[:, c, f * 128:(f + 1) * 128],
                                                                     h2[:, c, :], start=(c == 0), stop=(c == 7)),
                         reads=["fg%d" % c, h2k], writes=[pkg])
                psu, pku = psum()
                for c in range(8):
                    P.op("pe", lambda e, c=c, f=f, psu=psu: e.matmul(psu[:, 0:TW], wu[:, c, f * 128:(f + 1) * 128],
                                                                     h2[:, c, 1:TW + 1], start=(c == 0), stop=(c == 7)),
                         reads=["fu%d" % c, h2k], writes=[pku])
                j = k2 % 2
                k2 += 1
                cwb = (l * NFF + f) * 3
                P.op("act", lambda e, psg=psg, j=j, cwb=cwb, f=f: e.activation(
                    out=cv[j], in_=psg[:, 1:TW + 1], func=AF.Identity, scale=cw[:, cwb + 1:cwb + 2],
                    bias=cb[:, l * NFF + f:l * NFF + f + 1]), reads=[pkg], writes=["cv%d" % j])
                P.op("dve", lambda e, psg=psg, j=j, cwb=cwb: e.scalar_tensor_tensor(
                    cv[j], psg[:, 0:TW], cw[:, cwb:cwb + 1], cv[j], ALU.mult, ALU.add), reads=[pkg, "cv%d" % j],
                    writes=["cv%d" % j])
                P.op("dve", lambda e, psg=psg, j=j, cwb=cwb: e.scalar_tensor_tensor(
                    cv[j], psg[:, 2:TW + 2], cw[:, cwb + 2:cwb + 3], cv[j], ALU.mult, ALU.add), reads=[pkg, "cv%d" % j],
                    writes=["cv%d" % j])
                def tail(j=j, f=f, psu=psu, pku=pku):
                    P.op("act", lambda e: e.activation(out=ge[j], in_=cv[j], func=AF.Gelu_apprx_tanh),
                         reads=["cv%d" % j], writes=["ge%d" % j])
                    P.op("dve", lambda e: e.tensor_tensor(uT[:, f, :], ge[j], psu[:, 0:TW], ALU.mult),
                         reads=["ge%d" % j, pku], writes=["uT"], partial=True)

                if pend_tail[0] is not None:
                    pend_tail[0]()
                pend_tail[0] = tail
            pend_tail[0]()
            pend_tail[0] = None
            xo = xt[b][:, :, 1:TW + 1]
            for oc in range(8):
                ps, pk = psum()
                for f in range(NFF):
                    P.op("pe", lambda e, f=f, oc=oc, ps=ps: e.matmul(ps[:, 0:TW], wd[:, f, oc * 128:(oc + 1) * 128],
                                                                     uT[:, f, :], start=(f == 0), stop=(f == NFF - 1)),
                         reads=fdk + ["uT"], writes=[pk])
                P.op("dve", lambda e, oc=oc, ps=ps: e.tensor_tensor(xt[b][:, oc, 1:TW + 1], xt[b][:, oc, 1:TW + 1],
                                                                    ps[:, 0:TW], ALU.add),
                     reads=[pk, xk], writes=[xk], partial=True)
            if not final:
                dma("pool", x_dst.rearrange("(c p) n -> p c n", p=128)[:, :, t0:t0 + TW], xo, [xk], [], "st" + xk)
            else:
                sqf = sqfin[:, :, 0:TW]
                P.op("act", lambda e: e.activation(out=sqf, in_=xo, func=AF.Square), reads=[xk], writes=["sq"])
                ps, pk = psum()
                for c in range(8):
                    P.op("pe", lambda e, c=c, ps=ps: e.matmul(ps[:, 0:TW], ones_bf, sqfin[:, c, 0:TW], start=(c == 0),
                                                              stop=(c == 7)), reads=["sq", "ones"], writes=[pk])
                P.op("act", lambda e, ps=ps: e.activation(out=rtf[:, 0:TW], in_=ps[:, 0:TW], func=AF.Ln,
                                                          scale=1.0 / D, bias=epsc), reads=[pk, "epsc"], writes=["rt"])
                P.op("act", lambda e: e.activation(out=rstdf[:, 0:TW], in_=rtf[:, 0:TW], func=AF.Exp, scale=-0.5),
                     reads=["rt"], writes=["rstd"])
                for c in range(8):
                    P.op("dve", lambda e, c=c: e.scalar_tensor_tensor(xt[b][:, c, 1:TW + 1], xt[b][:, c, 1:TW + 1],
                                                                      gfin[:, c:c + 1], rstdf[:, 0:TW], ALU.mult,
                                                                      ALU.mult),
                         reads=[xk, "rstd"], writes=[xk], partial=True)
                dma("pool", outT.rearrange("(c p) n -> p c n", p=128)[:, :, t0:t0 + TW], xo, [xk], [], "st" + xk)
        P.barrier()

    setup()
    x_cur = xT_in
    for l in range(L):
        if "A" in phases:
            phase_a(l, x_cur)
        if "B" in phases:
            phase_b(l)
        if "C" in phases:
            x_c = xa_d if l == 0 else x_cur
            phase_c(l, x_cur, x_c)
        else:
            x_c = x_cur
        if "D" in phases:
            x_n = xb_d if x_c is xa_d else xa_d
            phase_d(l, x_c, x_n, final=(l == L - 1))
            x_cur = x_n
    P.barrier()
    P.emit(st)
    st.close()
    return nc


_NC_CACHE = {}


def _host_inputs(inp, core, NS, L):
    f = np.float32
    x = np.asarray(inp["x"], f)
    xs = x[core * NS:(core + 1) * NS]
    xT = np.ascontiguousarray(xs.reshape(NS * S, D).T)

    def cols(v, n):
        v = np.asarray(v, f).reshape(-1, n, 128)
        return np.ascontiguousarray(v.transpose(2, 0, 1).reshape(128, -1))

    m = dict(
        xT=xT,
        t5=np.asarray(inp["t5_table"], f),
        gmix=cols(inp["norm_mix_g"][:L], 8),
        gffn=cols(inp["norm_ffn_g"][:L], 8),
        gfin=cols(np.asarray(inp["final_g"])[None], 8),
        gq=cols(inp["q_norm_g"][:L], 2),
        gkv=cols(inp["kv_norm_g"][:L], 1),
        cb=cols(inp["conv_b"][:L], NFF),
        snk=np.ascontiguousarray(np.broadcast_to(np.asarray(inp["sink_logit"][:L], f).reshape(1, -1), (128, L * 4))),
        nab=np.ascontiguousarray(np.asarray(inp["na_bias"][:L], f).transpose(0, 3, 1, 2).reshape(L, 31, 60)),
        w_in=np.asarray(inp["w_in"][:L], f), w_uq=np.asarray(inp["w_uq"][:L], f),
        w_ukv=np.asarray(inp["w_ukv"][:L], f), w_gate=np.asarray(inp["w_gate"][:L], f),
        w_branch=np.asarray(inp["w_branch"][:L], f), w_out=np.asarray(inp["w_out"][:L], f),
        w_ffn_gate=np.asarray(inp["w_ffn_gate"][:L], f), w_ffn_up=np.asarray(inp["w_ffn_up"][:L], f),
        w_ffn_down=np.asarray(inp["w_ffn_down"][:L], f),
    )
    cwv = np.asarray(inp["conv_w"][:L], f)
    cwv = cwv.reshape(L, 3, NFF, 128).transpose(3, 0, 2, 1)
    m["cw"] = np.ascontiguousarray(cwv.reshape(128, -1))
    m.update(_consts())
    return m


def kernel(**inputs):
    NS, L = 2, DEPTH
    key = (NS, L)
    if key not in _NC_CACHE:
        _NC_CACHE[key] = build_nc(NS, L)
    nc = _NC_CACHE[key]
    n = 8
    in_maps = [_host_inputs(inputs, c, NS, L) for c in range(n)]
    res = run_bass_kernel_spmd(nc, in_maps, core_ids=list(range(n)))
    outs = []
    for c in range(n):
        oT = np.asarray(res.results[c]["outT"], np.float32)
        outs.append(oT.T.reshape(NS, S, D))
    return np.ascontiguousarray(np.concatenate(outs, 0))
```
